# Optimizing a Trainium2 kernel written in Bass

```python
import jax, jax.numpy as jnp
from jax import lax
import numpy as np

D_MODEL = 1024
BATCH = 8
SEQ = 2048
DEPTH = 1
DEC_BATCH = 16
DEC_SEQ = 64
PAST_LEN = 2048

CHUNK = 64
HEAD_DIM = 64
N_HEADS = 8
N_KV_HEADS = 2
GROUP = N_HEADS // N_KV_HEADS
ATTN_WIDTH = N_HEADS * HEAD_DIM
KV_WIDTH = N_KV_HEADS * HEAD_DIM
WINDOW = 128
WIN_CACHE = min(WINDOW, PAST_LEN)
ROPE_THETA = 10000.0
RET_HEADS = 4
RET_HEAD_DIM = 128
RET_WIDTH = RET_HEADS * RET_HEAD_DIM
RET_THETA = 10000.0
MIX_WIDTH = ATTN_WIDTH + RET_WIDTH
D_FF = 4 * D_MODEL
EPS = 1e-6
SPLITS = [ATTN_WIDTH,
          ATTN_WIDTH + KV_WIDTH,
          ATTN_WIDTH + 2 * KV_WIDTH,
          ATTN_WIDTH + 2 * KV_WIDTH + RET_WIDTH,
          ATTN_WIDTH + 2 * KV_WIDTH + 2 * RET_WIDTH,
          ATTN_WIDTH + 2 * KV_WIDTH + 3 * RET_WIDTH]
IN_WIDTH = ATTN_WIDTH + 2 * KV_WIDTH + 4 * RET_WIDTH

kernel_name = 'hymba_swa_sink_retention_stream_step'


def _rmsnorm(x, gain):
    xf = x.astype(jnp.float32)
    y = xf * lax.rsqrt(jnp.mean(xf * xf, axis=-1, keepdims=True) + EPS)
    return (y * gain.astype(jnp.float32)).astype(x.dtype)


def _attn_inv_freq():
    return 1.0 / (ROPE_THETA ** (jnp.arange(0, HEAD_DIM, 2, dtype=jnp.float32) / HEAD_DIM))


def _ret_inv_freq():
    return 1.0 / (RET_THETA ** jnp.linspace(0.0, 1.0, RET_HEAD_DIM // 2, dtype=jnp.float32))


def _ret_log_decay():
    return jnp.log1p(-jnp.exp2(-5.0 - jnp.arange(RET_HEADS, dtype=jnp.float32)))


def _rotate(x, pos, inv_freq):
    ang = pos.astype(jnp.float32)[:, None] * inv_freq[None, :]
    cos = jnp.cos(ang)[None, :, None, :]
    sin = jnp.sin(ang)[None, :, None, :]
    xf = x.astype(jnp.float32)
    x1, x2 = jnp.split(xf, 2, axis=-1)
    return jnp.concatenate([x1 * cos - x2 * sin, x2 * cos + x1 * sin], axis=-1).astype(x.dtype)


def _project(a, w_in, pos):
    B, L, _ = a.shape
    z = a @ w_in
    qa, ka, va, qr, kr, vr, gr = jnp.split(z, SPLITS, axis=-1)
    qa = _rotate(qa.reshape(B, L, N_HEADS, HEAD_DIM), pos, _attn_inv_freq())
    ka = _rotate(ka.reshape(B, L, N_KV_HEADS, HEAD_DIM), pos, _attn_inv_freq())
    va = va.reshape(B, L, N_KV_HEADS, HEAD_DIM)
    qr = _rotate(qr.reshape(B, L, RET_HEADS, RET_HEAD_DIM), pos, _ret_inv_freq())
    kr = _rotate(kr.reshape(B, L, RET_HEADS, RET_HEAD_DIM), pos, _ret_inv_freq()) * (RET_HEAD_DIM ** -0.5)
    vr = vr.reshape(B, L, RET_HEADS, RET_HEAD_DIM)
    return qa, ka, va, qr, kr, vr, gr


def _attend(q, k, v, sinks, mask):
    s = jnp.einsum('...qkgd,...skd->...kgqs', q, k).astype(jnp.float32) * (HEAD_DIM ** -0.5)
    if mask is not None:
        s = jnp.where(mask, s, -jnp.inf)
    sink = sinks.astype(jnp.float32).reshape(N_KV_HEADS, GROUP, 1)
    m = jnp.maximum(jnp.max(s, axis=-1), sink)
    p = jnp.exp(s - m[..., None])
    denom = jnp.sum(p, axis=-1) + jnp.exp(sink - m)
    p = (p / denom[..., None]).astype(v.dtype)
    o = jnp.einsum('...kgqs,...skd->...qkgd', p, v)
    return o.reshape(o.shape[:-3] + (ATTN_WIDTH,))


def _swa_prompt(q, k, v, sinks):
    B, S = q.shape[:2]
    n_c = S // CHUNK
    P = WINDOW // CHUNK
    qb = q.reshape(B, n_c, CHUNK, N_KV_HEADS, GROUP, HEAD_DIM)

    def band(t):
        tp = jnp.pad(t, ((0, 0), (WINDOW, 0), (0, 0), (0, 0)))
        tp = tp.reshape(B, n_c + P, CHUNK, N_KV_HEADS, HEAD_DIM)
        return jnp.concatenate([tp[:, j:j + n_c] for j in range(P + 1)], axis=2)

    kb, vb = band(k), band(v)
    key_pos = (jnp.arange(n_c)[:, None] - P) * CHUNK + jnp.arange((P + 1) * CHUNK)[None, :]
    mask = (key_pos >= 0)[:, None, None, None, :]
    o = _attend(qb, kb, vb, sinks, mask)
    return o.reshape(B, S, ATTN_WIDTH)


def _retention_chunk(state, q, k, v, log_g):
    L = q.shape[1]
    idx = jnp.arange(L, dtype=jnp.float32)
    diff = idx[:, None] - idx[None, :]
    decay = jnp.where(diff >= 0, jnp.exp(log_g[:, None, None] * jnp.maximum(diff, 0.0)), 0.0)
    sc = jnp.einsum('bihd,bjhd->bhij', q, k) * decay[None]
    intra = jnp.einsum('bhij,bjhe->bihe', sc, v)
    q_dec = jnp.exp(log_g[None, :] * (idx[:, None] + 1.0))
    cross = jnp.einsum('bihd,bhde->bihe', q, state) * q_dec[None, :, :, None]
    k_dec = jnp.exp(log_g[None, :] * (L - 1.0 - idx[:, None]))
    new_state = (jnp.exp(log_g * L)[None, :, None, None] * state
                 + jnp.einsum('bjhd,bjhe->bhde', k * k_dec[None, :, :, None], v))
    return intra + cross, new_state


def _retention_prompt(q, k, v):
    B, S, H, D = q.shape
    n_c = S // CHUNK
    log_g = _ret_log_decay()

    def to_chunks(t):
        return jnp.moveaxis(t.astype(jnp.float32).reshape(B, n_c, CHUNK, H, t.shape[-1]), 1, 0)

    def step(state, qkv):
        qc, kc, vc = qkv
        o, state = _retention_chunk(state, qc, kc, vc, log_g)
        return state, o

    s0 = jnp.zeros((B, H, D, RET_HEAD_DIM), jnp.float32)
    s_fin, o = lax.scan(step, s0, (to_chunks(q), to_chunks(k), to_chunks(v)))
    return jnp.moveaxis(o, 0, 1).reshape(B, S, H, RET_HEAD_DIM), s_fin


def _mix_out(attn_o, ret_o, gr, w_out):
    B, L = attn_o.shape[:2]
    r = ret_o * lax.rsqrt(jnp.mean(ret_o * ret_o, axis=-1, keepdims=True) + EPS)
    r = r.reshape(B, L, RET_WIDTH) * jax.nn.silu(gr.astype(jnp.float32))
    mixed = jnp.concatenate([attn_o, r.astype(attn_o.dtype)], axis=-1)
    return mixed @ w_out


def _sqrelu_mlp(a, w_up, w_down):
    u = jax.nn.relu(a @ w_up)
    return (u * u) @ w_down


def setup_inputs(seed: int = 0) -> dict:
    key = jax.random.key(seed)
    ks = jax.random.split(key, 16)
    f32 = jnp.float32
    x_prompt = jax.random.normal(ks[0], (BATCH, SEQ, D_MODEL), f32)
    x_sample = jax.random.normal(ks[1], (DEC_BATCH, DEC_SEQ, D_MODEL), f32)
    cache_k = jax.random.normal(ks[2], (DEPTH, DEC_BATCH, WIN_CACHE, N_KV_HEADS, HEAD_DIM), f32)
    cache_v = jax.random.normal(ks[3], (DEPTH, DEC_BATCH, WIN_CACHE, N_KV_HEADS, HEAD_DIM), f32)
    state_ret = jax.random.normal(ks[4], (DEPTH, DEC_BATCH, RET_HEADS, RET_HEAD_DIM, RET_HEAD_DIM), f32)
    norm1 = 1.0 + 0.05 * jax.random.normal(ks[5], (DEPTH, D_MODEL), f32)
    w_in = jax.random.normal(ks[6], (DEPTH, D_MODEL, IN_WIDTH), f32) * D_MODEL ** -0.5
    sinks = 0.5 * jax.random.normal(ks[7], (DEPTH, N_HEADS), f32)
    w_out = jax.random.normal(ks[8], (DEPTH, MIX_WIDTH, D_MODEL), f32) * MIX_WIDTH ** -0.5
    norm2 = 1.0 + 0.05 * jax.random.normal(ks[9], (DEPTH, D_MODEL), f32)
    w_up = jax.random.normal(ks[10], (DEPTH, D_MODEL, D_FF), f32) * D_MODEL ** -0.5
    w_down = jax.random.normal(ks[11], (DEPTH, D_FF, D_MODEL), f32) * D_FF ** -0.5
    norm_f = 1.0 + 0.05 * jax.random.normal(ks[12], (D_MODEL,), f32)
    return {'x_prompt': x_prompt, 'x_sample': x_sample, 'cache_k': cache_k, 'cache_v': cache_v,
            'state_ret': state_ret, 'norm1': norm1, 'w_in': w_in, 'sinks': sinks, 'w_out': w_out,
            'norm2': norm2, 'w_up': w_up, 'w_down': w_down, 'norm_f': norm_f}


def reference(x_prompt, x_sample, cache_k, cache_v, state_ret, norm1, w_in, sinks, w_out,
              norm2, w_up, w_down, norm_f):
    hp, hs = x_prompt, x_sample
    pos_p = jnp.arange(x_prompt.shape[1])
    pos_s = PAST_LEN + jnp.arange(x_sample.shape[1])
    nk_p, nv_p, ns_p, nk_s, nv_s, ns_s = [], [], [], [], [], []
    for l in range(DEPTH):
        a = _rmsnorm(hp, norm1[l])
        qa, ka, va, qr, kr, vr, gr = _project(a, w_in[l], pos_p)
        attn_o = _swa_prompt(qa, ka, va, sinks[l])
        ret_o, s_p = _retention_prompt(qr, kr, vr)
        hp = hp + _mix_out(attn_o, ret_o, gr, w_out[l])
        hp = hp + _sqrelu_mlp(_rmsnorm(hp, norm2[l]), w_up[l], w_down[l])
        nk_p.append(ka[:, -WIN_CACHE:])
        nv_p.append(va[:, -WIN_CACHE:])
        ns_p.append(s_p)
        a = _rmsnorm(hs, norm1[l])
        qa, ka, va, qr, kr, vr, gr = _project(a, w_in[l], pos_s)
        B, L = hs.shape[:2]
        k_all = jnp.concatenate([cache_k[l].astype(ka.dtype), ka], axis=1)
        v_all = jnp.concatenate([cache_v[l].astype(va.dtype), va], axis=1)
        attn_o = _attend(qa.reshape(B, L, N_KV_HEADS, GROUP, HEAD_DIM), k_all, v_all, sinks[l], None)
        ret_o, s_s = _retention_chunk(state_ret[l].astype(jnp.float32), qr.astype(jnp.float32),
                                      kr.astype(jnp.float32), vr.astype(jnp.float32), _ret_log_decay())
        hs = hs + _mix_out(attn_o, ret_o, gr, w_out[l])
        hs = hs + _sqrelu_mlp(_rmsnorm(hs, norm2[l]), w_up[l], w_down[l])
        nk_s.append(k_all[:, -WIN_CACHE:])
        nv_s.append(v_all[:, -WIN_CACHE:])
        ns_s.append(s_s)
    y_prompt = _rmsnorm(hp, norm_f)
    y_sample = _rmsnorm(hs, norm_f)
    return (y_prompt, y_sample,
            jnp.stack(nk_p, 0), jnp.stack(nv_p, 0), jnp.stack(ns_p, 0),
            jnp.stack(nk_s, 0), jnp.stack(nv_s, 0), jnp.stack(ns_s, 0))
```

```python
from contextlib import ExitStack

import numpy as np
import concourse.bass as bass
import concourse.mybir as mybir
from concourse.bass_utils import run_bass_kernel_spmd

F32 = mybir.dt.float32
BF16 = mybir.dt.bfloat16
AF = mybir.ActivationFunctionType
ALU = mybir.AluOpType

NT = 17
D = 1024
INW = 2816
DFF = 4096
EPS = 1e-6
GAM = [1.0 - 2.0 ** (-5 - h) for h in range(4)]
NCH = 8
CFG = dict(pG0=1, pG1=0, pG2=0, trb=6, trbQ=2, trbR=2, backpair=(2, 3), ret=(4, 7, 7, 6), backfirst=False)


class Buf:
    __slots__ = ("w", "r", "name", "excl")

    def __init__(self, name="", excl=False):
        self.w = None
        self.r = []
        self.name = name
        self.excl = excl


class Prog:
    ENGS = ("pe", "act", "dve", "pool", "sp")

    def __init__(self, nc, stack):
        self.nc = nc
        self.stack = stack
        self.sem = {e: stack.enter_context(nc.semaphore("s_" + e)) for e in self.ENGS}
        self.nodes = []
        self.bar = None
        self.nd = 0
        self.warm_ap = None
        self._pe_prev_end = 0.0

    def dsem(self, name=None):
        self.nd += 1
        return self.stack.enter_context(self.nc.semaphore(name or f"d{self.nd}"))

    def _deps(self, reads, writes, extra):
        d = set()
        for b in reads:
            if b.w is not None:
                d.add(b.w)
        for b in writes:
            if b.w is not None:
                d.add(b.w)
            d.update(b.r)
        d.update(x for x in extra if x is not None)
        if self.bar is not None:
            d.add(self.bar)
        return d

    def _reg(self, nid, reads, writes):
        for b in reads:
            b.r.append(nid)
        for b in writes:
            b.w = nid
            b.r = []

    def op(self, eng, fns, reads=(), writes=(), extra=(), dur=0.5):
        if not isinstance(fns, (list, tuple)):
            fns = [fns]
        writes = list(writes) + [b for b in reads if b.excl]
        reads = [b for b in reads if not b.excl]
        deps = self._deps(reads, writes, extra)
        nid = len(self.nodes)
        self.nodes.append(dict(eng=eng, fns=list(fns), deps=deps, dur=dur, lat=0.0, dsem=None))
        self._reg(nid, reads, writes)
        return nid

    def dma(self, eng, out, in_, dsem, reads=(), writes=(), extra=(), nbytes=4096):
        deps = self._deps(reads, writes, extra)
        nid = len(self.nodes)
        iss = 1.5 if eng == "pool" else 0.2
        self.nodes.append(dict(eng=eng, fns=[lambda e: e.dma_start(out=out, in_=in_)], deps=deps, dur=iss,
                               lat=2.0 + nbytes * 128 / 280e3, dsem=dsem))
        self._reg(nid, reads, writes)
        return nid

    def barrier(self, fn):
        nid = len(self.nodes)
        self.nodes.append(dict(eng="dve", fns=[fn], deps=set(range(nid)), dur=0.1, lat=0.0, dsem=None))
        self.bar = nid
        return nid

    def schedule(self):
        N = len(self.nodes)
        bar = self.bar
        succ = [[] for _ in range(N)]
        for i, n in enumerate(self.nodes):
            if i == bar:
                continue
            for d in n["deps"]:
                succ[d].append(i)
        prio = [0.0] * N
        for i in range(N - 1, -1, -1):
            n = self.nodes[i]
            best = 0.0
            for s in succ[i]:
                if prio[s] > best:
                    best = prio[s]
            if bar is not None and i < bar and prio[bar] > best:
                best = prio[bar]
            prio[i] = n["dur"] + n["lat"] + best
        ndep = [len(n["deps"]) for n in self.nodes]
        fin = [0.0] * N
        depfin = [0.0] * N
        tfree = {e: 0.0 for e in self.ENGS}
        ready = {e: [] for e in self.ENGS}
        for i in range(N):
            if ndep[i] == 0:
                ready[self.nodes[i]["eng"]].append(i)
        order = {e: [] for e in self.ENGS}
        done = 0
        while done < N:
            best = None
            for e in self.ENGS:
                r = ready[e]
                if not r:
                    continue
                tf = tfree[e]
                bi = None
                bk = None
                for i in r:
                    st = depfin[i] if depfin[i] > tf else tf
                    k = (st, -prio[i])
                    if bk is None or k < bk:
                        bk = k
                        bi = i
                if best is None or bk < best[0]:
                    best = (bk, e, bi)
            (st, _), e, i = best
            n = self.nodes[i]
            if not hasattr(self, "why"):
                self.why = {}
                self.stt = {}
                self.lastn = {}
            self.stt[i] = st
            if depfin[i] >= tfree[e] - 1e-9:
                dd = [d for d in n["deps"] if abs(fin[d] - depfin[i]) < 1e-9]
                self.why[i] = ("dep", dd[0] if dd else None)
            else:
                self.why[i] = ("eng", self.lastn.get(e))
            self.lastn[e] = i
            ready[e].remove(i)
            tfree[e] = st + n["dur"]
            fin[i] = st + n["dur"] + n["lat"]
            order[e].append(i)
            done += 1
            if bar is not None and i < bar:
                ndep[bar] -= 1
                if depfin[bar] < fin[i]:
                    depfin[bar] = fin[i]
                if ndep[bar] == 0:
                    ready["dve"].append(bar)
            for s in succ[i]:
                ndep[s] -= 1
                if depfin[s] < fin[i]:
                    depfin[s] = fin[i]
                if ndep[s] == 0:
                    ready[self.nodes[s]["eng"]].append(s)
        import os
        if os.environ.get("MK_SERIAL") == "1":
            order = {e: [i for i in range(N) if self.nodes[i]["eng"] == e] for e in self.ENGS}
        self.order = order
        self.est_total = max(fin) if fin else 0.0
        self.fin = fin

    def emit(self, final_dsems):
        nc = self.nc
        self.schedule()
        nodes = self.nodes
        idx = {}
        rank = {}
        dcount = {}
        for e in self.ENGS:
            k = 0
            for i in self.order[e]:
                ds_ = nodes[i]["dsem"]
                if ds_ is None:
                    k += 1
                    idx[i] = k
                else:
                    dcount[id(ds_)] = dcount.get(id(ds_), 0) + 1
                    rank[i] = dcount[id(ds_)]
        engmap = {"pe": "tensor", "act": "scalar", "dve": "vector", "pool": "gpsimd", "sp": "sync"}
        progs = {}
        for e in self.ENGS:
            seen = {}
            items = []
            for i in self.order[e]:
                n = nodes[i]
                best = {}
                for d in n["deps"]:
                    dn = nodes[d]
                    if dn["dsem"] is None:
                        if e == "pe" and dn["eng"] == "pe":
                            continue
                        s, v = self.sem[dn["eng"]], idx[d]
                    else:
                        s, v = dn["dsem"], 16 * rank[d]
                    k = id(s)
                    if k not in best or best[k][1] < v:
                        best[k] = (s, v)
                ws = []
                for k, (s, v) in best.items():
                    if seen.get(k, 0) >= v:
                        continue
                    seen[k] = v
                    ws.append((s, v))
                if n["dsem"] is None:
                    fns_ = n["fns"]
                    if e == "pe" and self.warm_ap is not None and len(items) > 4:
                        gap = self.stt[i] - self._pe_prev_end
                        k = min(24, int((gap - 0.4) / 0.11))
                        if k > 0:
                            wa = self.warm_ap
                            items.append(((), [(lambda e_: e_.ldweights(wa))] * k, None, 0))
                    if e == "pe":
                        self._pe_prev_end = self.stt[i] + n["dur"]
                    items.append((ws, fns_, self.sem[e], 1))
                else:
                    items.append((ws, n["fns"], n["dsem"], 16))
            progs[e] = items
        fw = [(ds_, 16 * dcount[id(ds_)]) for ds_ in final_dsems if id(ds_) in dcount]
        with nc.Block() as block:
            for ename in self.ENGS:
                items = progs[ename]
                f_ = fw if ename == "sp" else ()

                def body(e, items=items, f_=f_):
                    for ws, fns, s, inc in items:
                        for (ws_s, ws_v) in ws:
                            e.wait_ge(ws_s, ws_v)
                        nf = len(fns)
                        for j, fn in enumerate(fns):
                            ins = fn(e)
                            if j == nf - 1 and s is not None:
                                ins.then_inc(s, inc)
                    for (s, v) in f_:
                        e.wait_ge(s, v)
                getattr(block, engmap[ename])(body)


class Arena:
    def __init__(self, tensor, nbytes):
        self.t = tensor
        self.n = nbytes
        self.off = 0

    def alloc(self, shape, dt):
        esz = 2 if dt == BF16 else 4
        nel = int(np.prod(shape))
        nb = (nel * esz + 31) // 32 * 32
        at = self.off
        self.off += nb
        assert self.off <= self.n, ("arena overflow", self.off, self.n)
        a = self.t[:, at // 4:(at + nb) // 4]
        if dt == BF16:
            a = a.bitcast(BF16)
        a = a[:, 0:nel]
        if len(shape) == 2:
            a = a.rearrange("p (a b) -> p a b", a=shape[0])
        elif len(shape) == 3:
            a = a.rearrange("p (a b c) -> p a b c", a=shape[0], b=shape[1])
        return a


def t_act(F):
    return 0.22 + F / 1300.0


def t_dve(F, slow=1.0):
    return 0.2 + slow * F / 950.0


def t_mm(N, n=1):
    return n * (0.005 + (N / 2400.0 if N >= 512 else N / 1300.0))


def build_nc():
    nc = bass.Bass("TRN2", target_bir_lowering=False)
    di = lambda n, s: nc.dram_tensor(n, s, F32, kind="ExternalInput").ap()
    do = lambda n, s: nc.dram_tensor(n, s, F32, kind="ExternalOutput").ap()
    xs = di("xs", [NT, 128, D])
    ck = di("ck", [2, 128, 128])
    cv = di("cv", [2, 128, 128])
    sr = di("sr", [2, 4, 128, 128])
    norm1 = di("norm1", [D])
    w_in = di("w_in", [D, INW])
    sinks = di("sinks", [8])
    w_out = di("w_out", [D, D])
    norm2 = di("norm2", [D])
    w_up = di("w_up", [D, DFF])
    w_down = di("w_down", [DFF, D])
    norm_f = di("norm_f", [D])
    cosa_d = di("cosa", [128, NT, 32])
    sina_d = di("sina", [128, NT, 32])
    cosr_d = di("cosr", [128, NT, 64])
    sinr_d = di("sinr", [128, NT, 64])
    dt_d = di("dtab", [128, 2, 4, 128])
    qdec_d = di("qdec", [128, 2, 4, 128])
    kdec_d = di("kdec", [128, 2, 4])
    ident_d = di("ident", [128, 128])
    y_o = do("y", [NT, 128, D])
    nk_o = do("nk", [3, 128, 128])
    nv_o = do("nv", [3, 128, 128])
    ns_o = do("ns", [3, 4, 128, 128])

    with ExitStack() as st:
        P = Prog(nc, st)
        sbt = lambda name, shape, dt: st.enter_context(nc.sbuf_tensor(name, shape, dt))
        pairs = [st.enter_context(nc.psum_tensor(f"pp{i}", [128, 1024], F32)) for i in range(4)]
        _pb = [Buf(f"pair{i}", excl=True) for i in range(4)]
        bankbuf = [_pb[i // 2] for i in range(8)]

        def bank(b):
            return pairs[b // 2][:, (b % 2) * 512:(b % 2) * 512 + 512]

        def bank_bf(b):
            return bank(b).bitcast(BF16)

        PERS = 44 * 1024
        A1 = 151296
        pers_t = sbt("pers", [128, PERS // 4], F32)
        a1_t = sbt("a1", [128, A1 // 4], F32)
        wout_t = sbt("wout", [128, 8, D], BF16)
        pers = Arena(pers_t, PERS)
        mixflat = pers.alloc([NT * 1024], BF16)
        xbuf = [pers.alloc([D], F32) for _ in range(2)]
        ident = pers.alloc([128], BF16)
        ones = pers.alloc([64], BF16)
        stat = pers.alloc([64], F32)
        es = pers.alloc([8], F32)
        es_sel = pers.alloc([4], F32)
        sk = pers.alloc([8], F32)

        def mix_k(m):
            g, t = m // 4, m % 4
            ks = 512 if g < 4 else 128
            v = mixflat[:, g * 4096:g * 4096 + 8 * ks].rearrange("p (k x) -> p k x", k=8)
            return v[:, :, t * 128:(t + 1) * 128]

        def mix_grp(g, k):
            ks = 512 if g < 4 else 128
            return mixflat[:, g * 4096 + k * ks:g * 4096 + (k + 1) * ks]

        a1 = Arena(a1_t, A1)
        w_in_sb = a1.alloc([8, INW], BF16)
        g1b = a1.alloc([D], F32)
        cosa = a1.alloc([NT, 32], F32)
        sina = a1.alloc([NT, 32], F32)
        cosr = a1.alloc([NT, 64], F32)
        sinr = a1.alloc([NT, 64], F32)
        dtab = a1.alloc([2, 4, 128], F32)
        qdec = a1.alloc([2, 4, 128], F32)
        kdec = a1.alloc([2, 4], F32)
        dbl = lambda shape, dt: [a1.alloc(shape, dt) for _ in range(2)]
        a_bf = dbl([D], BF16)
        aT = dbl([8, 128], BF16)
        qrot = dbl([4, 2, 64], BF16)
        krot = dbl([128], BF16)
        QT = dbl([2, 4, 64], BF16)
        qkr = dbl([8, 128], BF16)
        QKrT = dbl([8, 128], BF16)
        QdT = dbl([4, 128], BF16)
        Kd = dbl([4, 128], BF16)
        Vr = dbl([4, 128], BF16)
        sg = dbl([4, 128], F32)
        scm = dbl([4, 128], BF16)
        rmix = dbl([4, 128], BF16)
        PTpc = dbl([1024], BF16)
        PTp = [a[:, 0:512] for a in PTpc]
        PTc = [a[:, 512:1024] for a in PTpc]
        rec = a1.alloc([2, 4, 64], F32)
        sgs = a1.alloc([512], F32)
        tcA = a1.alloc([640], F32)
        tsA = a1.alloc([640], F32)
        tcR = a1.alloc([D], F32)
        tsR = a1.alloc([D], F32)
        junk = a1.alloc([D], BF16)
        krot_f = [a1.alloc([128], F32) for _ in range(2)]
        vout_f = [a1.alloc([128], F32) for _ in range(2)]
        KT = [a1.alloc([128], BF16) for _ in range(3)]
        Vt = [a1.alloc([128], BF16) for _ in range(3)]
        KTc = [a1.alloc([128], BF16) for _ in range(2)]
        Vc = [a1.alloc([128], BF16) for _ in range(2)]
        cstage = [a1.alloc([256], F32) for _ in range(2)]
        cstage_bf = [a1.alloc([128], BF16) for _ in range(2)]
        S = [a1.alloc([4, 128], F32) for _ in range(3)]
        Sbf = [a1.alloc([4, 128], BF16) for _ in range(3)]
        a2 = Arena(a1_t, A1)
        yacc = a2.alloc([NT, D], F32)
        g2b = a2.alloc([D], F32)
        gfb = a2.alloc([D], F32)
        a2_bf = [a2.alloc([D], BF16) for _ in range(2)]
        uT = [a2.alloc([4, 512], BF16) for _ in range(2)]
        urel = [a2.alloc([512], F32) for _ in range(2)]
        ystage = [a2.alloc([D], F32) for _ in range(2)]
        junk2 = a2.alloc([D], BF16)
        ring_up = [a2.alloc([8, 512], BF16) for _ in range(2)]
        ring_dn = [a2.alloc([4, D], BF16) for _ in range(2)]

        B = {}

        def bf(name):
            if name not in B:
                B[name] = Buf(name)
            return B[name]

        def bf2(name):
            return [bf(name + "0"), bf(name + "1")]

        def cload(name, dst, src, eng="sp", nbytes=4096):
            b_ = bf(name)
            P.dma(eng, dst, src, P.dsem("d_" + name), writes=[b_], nbytes=nbytes)
            return b_

        fl2 = lambda a: a.rearrange("p a b -> p (a b)")
        fl3 = lambda a: a.rearrange("p a b c -> p (a b c)")
        g1bb = cload("g1b", g1b, norm1.partition_broadcast(128))
        idb = cload("ident", ident, ident_d, eng="pool", nbytes=512)
        skb = cload("sk", sk, sinks.partition_broadcast(128), nbytes=32)
        xsem = [P.dsem("dx0"), P.dsem("dx1")]
        xb = bf2("x")
        wbuf = {}
        wprev = None
        for gi_, (c0_, c1_) in enumerate(((0, 768), (768, 1792), (1792, INW))):
            b_ = bf(f"win{gi_}")
            wprev = P.dma("pool", w_in_sb[:, :, c0_:c1_], w_in[:, c0_:c1_].rearrange("(k p) n -> p k n", p=128), P.dsem(f"dwin{gi_}"), writes=[b_],
                          extra=[wprev], nbytes=(c1_ - c0_) * 32)
            wbuf[c0_] = b_
        cosab = cload("cosa", fl2(cosa), fl2(cosa_d), nbytes=2176)
        sinab = cload("sina", fl2(sina), fl2(sina_d), nbytes=2176)
        cosrb = cload("cosr", fl2(cosr), fl2(cosr_d), nbytes=4352)
        sinrb = cload("sinr", fl2(sinr), fl2(sinr_d), nbytes=4352)
        dtabb = cload("dtab", fl3(dtab), fl3(dt_d))
        qdecb = cload("qdec", fl3(qdec), fl3(qdec_d))
        kdecb = cload("kdec", fl2(kdec), fl2(kdec_d), nbytes=32)
        wob = bf("wout")
        P.dma("pool", wout_t[:], w_out.rearrange("(k p) n -> p k n", p=128), P.dsem("dwo"), writes=[wob], extra=[wprev], nbytes=16384)
        onb = bf("ones")
        P.op("dve", lambda e: e.memset(ones, 1.0), writes=[onb], dur=0.1)
        esb = bf("es")
        P.op("act", lambda e: e.activation(out=es, in_=sk, func=AF.Exp), reads=[skb], writes=[esb], dur=0.3)
        essb = bf("es_sel")
        P.op("dve", lambda e: e.tensor_copy(out=es_sel[0:64, :], in_=es[0:64, 0:4]), reads=[esb], writes=[essb], dur=0.1)
        P.op("dve", lambda e: e.tensor_copy(out=es_sel[64:128, :], in_=es[64:128, 4:8]), reads=[esb], writes=[essb], dur=0.1)

        stat_n = [0]

        def stat_col(n):
            c0 = stat_n[0]
            stat_n[0] += n
            assert stat_n[0] <= 60
            return stat[:, c0:c0 + n], bf(f"stat{c0}")

        st_n1 = [stat_col(1) for _ in range(2)]
        st_g = [stat_col(4) for _ in range(2)]
        st_n2 = [stat_col(1) for _ in range(2)]
        st_nf = [stat_col(1) for _ in range(2)]

        _jb = {}

        def jb_of(ap):
            k = id(ap)
            if k not in _jb:
                _jb[k] = Buf("junk")
            return _jb[k]

        def rms(src_ap, src_bufs, gain_ap, gain_bufs, out_ap, out_bufs, stc, junk_ap, F=1024):
            col, cb = stc
            fns = [lambda e: e.activation(out=junk_ap, in_=src_ap, func=AF.Square, accum_out=col),
                   lambda e: e.activation(out=col, in_=col, func=AF.Ln, scale=1.0 / D, bias=EPS),
                   lambda e: e.activation(out=col, in_=col, func=AF.Exp, scale=-0.5)]
            P.op("act", fns[0], reads=src_bufs, writes=[cb, jb_of(junk_ap)], dur=t_act(F))
            P.op("act", fns[1], writes=[cb], dur=0.25)
            P.op("act", fns[2], writes=[cb], dur=0.25)
            P.op("dve", lambda e: e.scalar_tensor_tensor(out=out_ap, in0=src_ap, scalar=col, in1=gain_ap, op0=ALU.mult, op1=ALU.mult),
                 reads=list(src_bufs) + [cb] + list(gain_bufs), writes=out_bufs, dur=t_dve(F))

        TRB = 2

        def transposes(srcs, src_bufs, dst_ap, dst_bufs, trb=TRB):
            n = len(srcs)
            tb = bank_bf(trb)
            fns = [(lambda e, i=i, s=s: e.transpose(out=tb[:, i * 128:(i + 1) * 128], in_=s, identity=ident)) for i, s in enumerate(srcs)]
            P.op("pe", fns, reads=list(src_bufs) + [idb], writes=[bankbuf[trb]], dur=t_mm(128, n))
            src = tb[:, 0:n * 128].rearrange("p (a b) -> p a b", a=n)
            P.op("act", lambda e: e.copy(out=dst_ap, in_=src), reads=[bankbuf[trb]], writes=dst_bufs, dur=t_act(n * 128))

        csb = bf2("cst")
        Sb = [bf("S0"), bf("S1"), bf("S2")]
        Sbfb = [bf("Sbf0"), bf("Sbf1"), bf("Sbf2")]
        d_misc = P.dsem("dmisc")
        KTcb, Vcb, csbf_b = bf2("KTc"), bf2("Vc"), bf2("csbf")
        csvb = bf2("cstv")
        for b in range(2):
            P.dma("sp", cstage[b][:, 0:128], ck[b], P.dsem(f"dck{b}"), writes=[csb[b]], nbytes=512)
            P.dma("sp", cstage[b][:, 128:256], cv[b], P.dsem(f"dcv{b}"), writes=[csvb[b]], nbytes=512)
            P.dma("sp", S[1 + b], sr[b].rearrange("h d e -> d h e"), P.dsem(f"dsr{b}"), writes=[Sb[1 + b]], nbytes=2048)
            P.op("dve", lambda e, b=b: e.tensor_copy(out=cstage_bf[b], in_=cstage[b][:, 0:128]), reads=[csb[b]], writes=[csbf_b[b]], dur=0.3)
            P.op("dve", lambda e, b=b: e.tensor_copy(out=Vc[b], in_=cstage[b][:, 128:256]), reads=[csvb[b]], writes=[Vcb[b]], dur=0.3)
            transposes([cstage_bf[b]], [csbf_b[b]], KTc[b].unsqueeze(1), [KTcb[b]])
            P.op("act", lambda e, b=b: e.copy(out=Sbf[1 + b], in_=S[1 + b]), reads=[Sb[1 + b]], writes=[Sbfb[1 + b]], dur=t_act(512))
            P.dma("sp", nk_o[1 + b, 0:64, :], ck[b, 64:128, :], d_misc, nbytes=512)
            P.dma("sp", nv_o[1 + b, 0:64, :], cv[b, 64:128, :], d_misc, nbytes=512)

        abf_b, aT_b = bf2("a_bf"), bf2("aT")
        tcAb, tsAb, tcRb, tsRb = bf("tcA"), bf("tsA"), bf("tcR"), bf("tsR")
        qrb, krb, QTb = bf2("qrot"), bf2("krot"), bf2("QT")
        KTb, Vtb = [bf(f"KT{i}") for i in range(3)], [bf(f"V{i}") for i in range(3)]
        PTcb, PTpb = bf2("PTc"), bf2("PTp")
        recb, sgsb = bf("rec"), bf("sgs")
        qkrb, QKrTb, QdTb, Kdb, Vrb, sgb, scmb, rmixb = (bf2(n) for n in ("qkr", "QKrT", "QdT", "Kd", "Vr", "sg", "scm", "rmix"))
        mixb = [bf(f"mix{m}") for m in range(NT)]
        krfb, vofb = bf2("krf"), bf2("vof")

        def front(m):
            sample = (m == NT - 1)
            ty = 1 if sample else 0
            xi = m % 2
            pi = m % 2
            P.dma("sp", xbuf[xi], xs[m], xsem[xi], writes=[xb[xi]], nbytes=4096)
            rms(xbuf[xi], [xb[xi]], g1b, [g1bb], a_bf[pi], [abf_b[pi]], st_n1[pi], junk)
            transposes([a_bf[pi][:, k * 128:(k + 1) * 128] for k in range(8)], [abf_b[pi]], aT[pi], [aT_b[pi]], trb=CFG['trb'])

            def proj(col0, ncols, pr, pi=pi):
                fns = []
                nb = (ncols + 511) // 512
                tot = 0.0
                for k in range(8):
                    for n in range(nb):
                        c0 = col0 + n * 512
                        w_ = min(512, col0 + ncols - c0)
                        tot += t_mm(w_)
                        fns.append(lambda e, k=k, n=n, c0=c0, w_=w_: e.matmul(
                            bank(2 * pr + n)[:, 0:w_], lhsT=aT[pi][:, k, :], rhs=w_in_sb[:, k, c0:c0 + w_],
                            start=(k == 0), stop=(k == 7)))
                P.op("pe", fns, reads=[aT_b[pi], wbuf[col0]], writes=[bankbuf[2 * pr + n] for n in range(nb)], dur=tot)

            PG0, PG1, PG2 = CFG['pG0'], CFG['pG1'], CFG['pG2']
            proj(0, 768, PG0)
            z0 = pairs[PG0]
            zall = z0[:, 0:640].rearrange("p (h two r) -> p h two r", h=10, two=2)
            tca = tcA.rearrange("p (h two r) -> p h two r", h=10, two=2)
            tsa = tsA.rearrange("p (h two r) -> p h two r", h=10, two=2)
            csb_ = cosa[:, m, :].unsqueeze(1).unsqueeze(1).broadcast_to([128, 10, 2, 32])
            snb_ = sina[:, m, :].unsqueeze(1).unsqueeze(1).broadcast_to([128, 10, 2, 32])
            P.op("dve", lambda e, csb_=csb_: e.tensor_tensor(out=tca, in0=zall, in1=csb_, op=ALU.mult),
                 reads=[bankbuf[2 * PG0], cosab], writes=[tcAb], dur=t_dve(640))
            P.op("dve", lambda e, snb_=snb_: e.tensor_tensor(out=tsa, in0=zall, in1=snb_, op=ALU.mult),
                 reads=[bankbuf[2 * PG0], sinab], writes=[tsAb], dur=t_dve(640))
            cur = m % 3
            prv = (m + 2) % 3
            P.op("act", lambda e, cur=cur: e.copy(out=Vt[cur], in_=z0[:, 640:768]), reads=[bankbuf[2 * PG0]], writes=[Vtb[cur]], dur=t_act(128))
            if m >= NT - 2:
                j = m - (NT - 2)
                P.op("act", lambda e, j=j: e.copy(out=vout_f[j], in_=z0[:, 640:768]), reads=[bankbuf[2 * PG0]], writes=[vofb[j]], dur=t_act(128))
            tcq = tcA[:, 0:512].rearrange("p (g t two r) -> p g t two r", g=2, t=4, two=2)
            tsq = tsA[:, 0:512].rearrange("p (g t two r) -> p g t two r", g=2, t=4, two=2)
            qro = qrot[pi].rearrange("p t g (two r) -> p g t two r", two=2)
            tck = tcA[:, 512:640].rearrange("p (h two r) -> p h two r", h=2, two=2)
            tsk = tsA[:, 512:640].rearrange("p (h two r) -> p h two r", h=2, two=2)
            kro = krot[pi].rearrange("p (h two r) -> p h two r", h=2, two=2)
            P.op("dve", lambda e, qro=qro: e.tensor_tensor(out=qro[:, :, :, 0, :], in0=tcq[:, :, :, 0, :], in1=tsq[:, :, :, 1, :], op=ALU.subtract),
                 reads=[tcAb, tsAb], writes=[qrb[pi]], dur=t_dve(256))
            P.op("dve", lambda e, qro=qro: e.tensor_tensor(out=qro[:, :, :, 1, :], in0=tcq[:, :, :, 1, :], in1=tsq[:, :, :, 0, :], op=ALU.add),
                 reads=[tcAb, tsAb], writes=[qrb[pi]], dur=t_dve(256))
            P.op("dve", lambda e, kro=kro: e.tensor_tensor(out=kro[:, :, 0, :], in0=tck[:, :, 0, :], in1=tsk[:, :, 1, :], op=ALU.subtract),
                 reads=[tcAb, tsAb], writes=[krb[pi]], dur=t_dve(64))
            P.op("dve", lambda e, kro=kro: e.tensor_tensor(out=kro[:, :, 1, :], in0=tck[:, :, 1, :], in1=tsk[:, :, 0, :], op=ALU.add),
                 reads=[tcAb, tsAb], writes=[krb[pi]], dur=t_dve(64))
            if m >= NT - 2:
                j = m - (NT - 2)
                krf = krot_f[j].rearrange("p (h two r) -> p h two r", h=2, two=2)
                P.op("dve", lambda e, krf=krf: e.tensor_tensor(out=krf[:, :, 0, :], in0=tck[:, :, 0, :], in1=tsk[:, :, 1, :], op=ALU.subtract),
                     reads=[tcAb, tsAb], writes=[krfb[j]], dur=t_dve(64))
                P.op("dve", lambda e, krf=krf: e.tensor_tensor(out=krf[:, :, 1, :], in0=tck[:, :, 1, :], in1=tsk[:, :, 0, :], op=ALU.add),
                     reads=[tcAb, tsAb], writes=[krfb[j]], dur=t_dve(64))
                if not sample:
                    P.dma("sp", nk_o[0], krot_f[j], d_misc, reads=[krfb[j]], nbytes=512)
                    P.dma("sp", nv_o[0], vout_f[j], d_misc, reads=[vofb[j]], nbytes=512)
                else:
                    for b in range(2):
                        P.dma("sp", nk_o[1 + b, 64:128, :], krot_f[j][64 * b:64 * b + 64, :], d_misc, reads=[krfb[j]], nbytes=512)
                        P.dma("sp", nv_o[1 + b, 64:128, :], vout_f[j][64 * b:64 * b + 64, :], d_misc, reads=[vofb[j]], nbytes=512)
            TRQ = CFG['trbQ']
            tb = bank_bf(TRQ)
            fns = [(lambda e, t=t, pi=pi: e.transpose(out=tb[:, t * 128:(t + 1) * 128], in_=qrot[pi][:, t, :, :].rearrange("p g d -> p (g d)"), identity=ident)) for t in range(4)]
            fns.append(lambda e, pi=pi: e.transpose(out=tb[:, 512:640], in_=krot[pi], identity=ident))
            P.op("pe", fns, reads=[qrb[pi], krb[pi], idb], writes=[bankbuf[TRQ]], dur=t_mm(128, 5))
            fns = [lambda e, pi=pi: e.copy(out=QT[pi], in_=tb[:, 0:512].rearrange("p (t c q) -> p c t q", t=4, c=2)),
                   lambda e, cur=cur: e.copy(out=KT[cur], in_=tb[:, 512:640])]
            P.op("act", fns, reads=[bankbuf[TRQ]], writes=[QTb[pi], KTb[cur]], dur=t_act(512) + t_act(128))

            proj(768, 1024, PG1)
            z1 = pairs[PG1]
            zr = z1[:, :].rearrange("p (h two r) -> p h two r", h=8, two=2)
            tcr = tcR.rearrange("p (h two r) -> p h two r", h=8, two=2)
            tsr = tsR.rearrange("p (h two r) -> p h two r", h=8, two=2)
            qko = qkr[pi].rearrange("p h (two r) -> p h two r", two=2)
            csr_ = cosr[:, m, :].unsqueeze(1).unsqueeze(1).broadcast_to([128, 8, 2, 64])
            snr_ = sinr[:, m, :].unsqueeze(1).unsqueeze(1).broadcast_to([128, 8, 2, 64])
            P.op("dve", lambda e, csr_=csr_: e.tensor_tensor(out=tcr, in0=zr, in1=csr_, op=ALU.mult), reads=[bankbuf[2 * PG1], cosrb], writes=[tcRb], dur=t_dve(1024))
            P.op("dve", lambda e, snr_=snr_: e.tensor_tensor(out=tsr, in0=zr, in1=snr_, op=ALU.mult), reads=[bankbuf[2 * PG1], sinrb], writes=[tsRb], dur=t_dve(1024))
            P.op("dve", lambda e, qko=qko: e.tensor_tensor(out=qko[:, :, 0, :], in0=tcr[:, :, 0, :], in1=tsr[:, :, 1, :], op=ALU.subtract), reads=[tcRb, tsRb], writes=[qkrb[pi]], dur=t_dve(512))
            P.op("dve", lambda e, qko=qko: e.tensor_tensor(out=qko[:, :, 1, :], in0=tcr[:, :, 1, :], in1=tsr[:, :, 0, :], op=ALU.add), reads=[tcRb, tsRb], writes=[qkrb[pi]], dur=t_dve(512))
            transposes([qkr[pi][:, h, :] for h in range(8)], [qkrb[pi]], QKrT[pi], [QKrTb[pi]], trb=CFG['trbR'])
            P.op("dve", lambda e, ty=ty, pi=pi: e.tensor_tensor(out=QdT[pi], in0=QKrT[pi][:, 0:4, :], in1=qdec[:, ty, :, :], op=ALU.mult),
                 reads=[QKrTb[pi], qdecb], writes=[QdTb[pi]], dur=t_dve(512))
            P.op("dve", lambda e, ty=ty, pi=pi: e.tensor_tensor(out=Kd[pi], in0=qkr[pi][:, 4:8, :], in1=kdec[:, ty, :].unsqueeze(2).broadcast_to([128, 4, 128]), op=ALU.mult),
                 reads=[qkrb[pi], kdecb], writes=[Kdb[pi]], dur=t_dve(512))
            proj(1792, 1024, PG2)
            P.op("act", lambda e, pi=pi: e.copy(out=Vr[pi].rearrange("p a b -> p (a b)"), in_=pairs[PG2][:, 0:512]), reads=[bankbuf[2 * PG2]], writes=[Vrb[pi]], dur=t_act(512))
            gps = pairs[PG2][:, 512:1024]
            P.op("act", lambda e: e.activation(out=sgs, in_=gps, func=AF.Exp, scale=-1.0), reads=[bankbuf[2 * PG2]], writes=[sgsb], dur=t_act(512))
            P.op("act", lambda e: e.activation(out=sgs, in_=sgs, func=AF.Ln, bias=1.0), writes=[sgsb], dur=t_act(512))
            P.op("act", lambda e: e.activation(out=sgs, in_=sgs, func=AF.Exp, scale=-1.0), writes=[sgsb], dur=t_act(512))
            P.op("dve", lambda e, pi=pi: e.tensor_tensor(out=sg[pi].rearrange("p a b -> p (a b)"), in0=gps, in1=sgs, op=ALU.mult),
                 reads=[bankbuf[2 * PG2], sgsb], writes=[sgb[pi]], dur=t_dve(512))

        def back(m):
            sample = (m == NT - 1)
            ty = 1 if sample else 0
            pi = m % 2
            cur = m % 3
            prv = (m + 2) % 3
            pS, pO = CFG['backpair']
            SBc, SBp, OB_, DB_ = 2 * pS + 1, 2 * pS, 2 * pO, 2 * pO + 1
            has_prev = sample or m > 0
            QTf = QT[pi].rearrange("p c t q -> p (c t q)")
            for g in range(2):
                r0 = 64 * g
                fns = [lambda e, r0=r0, cur=cur, QTf=QTf: e.matmul(bank(SBc), lhsT=KT[cur][r0:r0 + 64, :], rhs=QTf[r0:r0 + 64, :], start=True, stop=True)]
                rd = [KTb[cur], QTb[pi]]
                if has_prev:
                    if not sample:
                        fns.append(lambda e, r0=r0, prv=prv, QTf=QTf: e.matmul(bank(SBp), lhsT=KT[prv][r0:r0 + 64, :], rhs=QTf[r0:r0 + 64, :], start=True, stop=True))
                        rd.append(KTb[prv])
                    else:
                        fns += [(lambda e, b=b, r0=r0, QTf=QTf: e.matmul(bank(SBp)[:, 256 * b:256 * b + 256], lhsT=KTc[b][r0:r0 + 64, :], rhs=QTf[r0:r0 + 64, 256 * b:256 * b + 256],
                                                                       start=True, stop=True)) for b in range(2)]
                        rd += KTcb
                P.op("pe", fns, reads=rd, writes=[bankbuf[SBc], bankbuf[SBp]], dur=t_mm(512, len(fns)))
                if has_prev:
                    P.op("act", lambda e, g=g: e.activation(out=PTpc[g], in_=pairs[pS][:, :], func=AF.Exp, scale=0.125),
                         reads=[bankbuf[SBc], bankbuf[SBp]], writes=[PTcb[g], PTpb[g]], dur=t_act(1024))
                else:
                    P.op("act", lambda e, g=g: e.activation(out=PTc[g], in_=bank(SBc), func=AF.Exp, scale=0.125),
                         reads=[bankbuf[SBc]], writes=[PTcb[g]], dur=t_act(512))
                fns = []
                for c in range(2):
                    contrib = []
                    if has_prev:
                        if sample:
                            contrib.append((Vc[c], PTp[g], 0, 128))
                        else:
                            contrib.append((Vt[prv], PTp[g], 0, 128) if c == 0 else (Vt[prv], PTp[g], 64, 128))
                    if sample:
                        contrib.append((Vt[cur], PTc[g], 64 * c, 64 * c + 64))
                    else:
                        contrib.append((Vt[cur], PTc[g], 0, 64) if c == 0 else (Vt[cur], PTc[g], 0, 128))
                    for (dbk_, lsel) in ((OB_, 0), (DB_, 1)):
                        for i, (vv, pt, k0, k1) in enumerate(contrib):
                            lhs = vv[k0:k1, r0:r0 + 64] if lsel == 0 else ones[k0:k1, :]
                            fns.append(lambda e, dbk_=dbk_, lhs=lhs, pt=pt, k0=k0, k1=k1, c=c, i=i, nctr=len(contrib), r0=r0: e.matmul(
                                bank(dbk_)[r0:r0 + 64, 256 * c:256 * c + 256], lhsT=lhs, rhs=pt[k0:k1, 256 * c:256 * c + 256],
                                start=(i == 0), stop=(i == nctr - 1)))
                rd = [Vtb[cur], PTcb[g], onb] + ([PTpb[g]] + (Vcb if sample else [Vtb[prv]]) if has_prev else [])
                P.op("pe", fns, reads=rd, writes=[bankbuf[OB_], bankbuf[DB_]], dur=t_mm(256, len(fns)))
            dbv = bank(DB_).rearrange("p (c t q) -> p c t q", c=2, t=4)
            obv = bank(OB_).rearrange("p (c t q) -> p c t q", c=2, t=4)
            fns = [(lambda e, t=t: e.activation(out=rec[:, :, t, :], in_=dbv[:, :, t, :], func=AF.Ln, bias=es_sel[:, t:t + 1])) for t in range(4)]
            P.op("act", fns, reads=[bankbuf[DB_], essb], writes=[recb], dur=4 * t_act(128))
            P.op("act", lambda e: e.activation(out=rec.rearrange("p c t q -> p (c t q)"), in_=rec.rearrange("p c t q -> p (c t q)"), func=AF.Exp, scale=-1.0),
                 writes=[recb], dur=t_act(512))
            mo = mix_k(m)[:, 0:4, :].rearrange("p t (c q) -> p c t q", c=2)
            P.op("dve", lambda e, mo=mo: e.tensor_tensor(out=mo, in0=obv, in1=rec, op=ALU.mult),
                 reads=[bankbuf[OB_], recb], writes=[mixb[m]], dur=t_dve(512))

            SC_, OR_, UB_, RT_ = CFG.get('ret', (5, 6, 7, 4))
            scv = bank(SC_).rearrange("p (a b) -> p a b", a=4)
            fns = [(lambda e, h=h, pi=pi: e.matmul(scv[:, h, :], lhsT=QKrT[pi][:, 4 + h, :], rhs=QKrT[pi][:, h, :], start=True, stop=True)) for h in range(4)]
            P.op("pe", fns, reads=[QKrTb[pi]], writes=[bankbuf[SC_]], dur=t_mm(128, 4))
            P.op("dve", lambda e, ty=ty, pi=pi: e.tensor_tensor(out=scm[pi], in0=scv, in1=dtab[:, ty, :, :], op=ALU.mult),
                 reads=[bankbuf[SC_], dtabb], writes=[scmb[pi]], dur=t_dve(512))
            orv = bank(OR_).rearrange("p (a b) -> p a b", a=4)
            fns = []
            rd = [scmb[pi], Vrb[pi]]
            if not sample:
                cross = (m > 0)
                for h in range(4):
                    fns.append(lambda e, h=h, cross=cross, pi=pi: e.matmul(orv[:, h, :], lhsT=scm[pi][:, h, :], rhs=Vr[pi][:, h, :], start=True, stop=not cross))
                    if cross:
                        fns.append(lambda e, h=h, pi=pi: e.matmul(orv[:, h, :], lhsT=QdT[pi][:, h, :], rhs=Sbf[0][:, h, :], start=False, stop=True))
                if cross:
                    rd += [QdTb[pi], Sbfb[0]]
            else:
                for h in range(4):
                    fns.append(lambda e, h=h, pi=pi: e.matmul(orv[:, h, :], lhsT=scm[pi][:, h, :], rhs=Vr[pi][:, h, :], start=True, stop=True))
                    for b in range(2):
                        fns.append(lambda e, h=h, b=b, pi=pi: e.matmul(orv[64 * b:64 * b + 64, h, :], lhsT=QdT[pi][:, h, 64 * b:64 * b + 64], rhs=Sbf[1 + b][:, h, :],
                                                                      start=False, stop=False, skip_group_check=True))
                rd += [QdTb[pi], Sbfb[1], Sbfb[2]]
            P.op("pe", fns, reads=rd, writes=[bankbuf[OR_]], dur=t_mm(128, len(fns)))
            gcol, gcb = st_g[pi]
            fns = [(lambda e, h=h, gcol=gcol: e.activation(out=junk[:, h * 128:(h + 1) * 128], in_=orv[:, h, :], func=AF.Square, accum_out=gcol[:, h:h + 1])) for h in range(4)]
            P.op("act", fns, reads=[bankbuf[OR_]], writes=[gcb, jb_of(junk)], dur=4 * t_act(128))
            P.op("act", lambda e, gcol=gcol: e.activation(out=gcol, in_=gcol, func=AF.Ln, scale=1.0 / 128, bias=EPS), writes=[gcb], dur=0.25)
            P.op("act", lambda e, gcol=gcol: e.activation(out=gcol, in_=gcol, func=AF.Exp, scale=-0.5), writes=[gcb], dur=0.25)
            fns = [(lambda e, h=h, gcol=gcol, pi=pi: e.scalar_tensor_tensor(out=rmix[pi][:, h, :], in0=orv[:, h, :], scalar=gcol[:, h:h + 1], in1=sg[pi][:, h, :],
                                                                           op0=ALU.mult, op1=ALU.mult)) for h in range(4)]
            P.op("dve", fns, reads=[bankbuf[OR_], gcb, sgb[pi]], writes=[rmixb[pi]], dur=4 * t_dve(128))
            transposes([rmix[pi][:, h, :] for h in range(4)], [rmixb[pi]], mix_k(m)[:, 4:8, :], [mixb[m]], trb=RT_)
            if not sample:
                ubv = bank(UB_).rearrange("p (a b) -> p a b", a=4)
                fns = [(lambda e, h=h, pi=pi: e.matmul(ubv[:, h, :], lhsT=Kd[pi][:, h, :], rhs=Vr[pi][:, h, :], start=True, stop=True)) for h in range(4)]
                P.op("pe", fns, reads=[Kdb[pi], Vrb[pi]], writes=[bankbuf[UB_]], dur=t_mm(128, 4))
                if m == 0:
                    P.op("dve", lambda e: e.tensor_copy(out=S[0], in_=ubv), reads=[bankbuf[UB_]], writes=[Sb[0]], dur=t_dve(512))
                else:
                    fns = [(lambda e, h=h: e.scalar_tensor_tensor(out=S[0][:, h, :], in0=S[0][:, h, :], scalar=float(GAM[h] ** 128), in1=ubv[:, h, :],
                                                                 op0=ALU.mult, op1=ALU.add)) for h in range(4)]
                    P.op("dve", fns, reads=[bankbuf[UB_]], writes=[Sb[0]], dur=4 * t_dve(128))
                if m < NT - 2:
                    P.op("act", lambda e: e.copy(out=Sbf[0], in_=S[0]), reads=[Sb[0]], writes=[Sbfb[0]], dur=t_act(512))
                else:
                    P.dma("sp", ns_o[0].rearrange("h d e -> d h e"), S[0], d_misc, reads=[Sb[0]], nbytes=2048)
            else:
                for b in range(2):
                    ubk = UB_ if b == 0 else SC_
                    ubv = bank(ubk).rearrange("p (a b) -> p a b", a=4)
                    fns = [(lambda e, h=h, b=b, ubv=ubv, pi=pi: e.matmul(ubv[:, h, :], lhsT=Kd[pi][64 * b:64 * b + 64, h, :], rhs=Vr[pi][64 * b:64 * b + 64, h, :], start=True, stop=True)) for h in range(4)]
                    P.op("pe", fns, reads=[Kdb[pi], Vrb[pi]], writes=[bankbuf[ubk]], dur=t_mm(128, 4))
                    fns = [(lambda e, h=h, b=b, ubv=ubv: e.scalar_tensor_tensor(out=S[1 + b][:, h, :], in0=S[1 + b][:, h, :], scalar=float(GAM[h] ** 64), in1=ubv[:, h, :],
                                                                               op0=ALU.mult, op1=ALU.add)) for h in range(4)]
                    P.op("dve", fns, reads=[bankbuf[ubk], Sbfb[1 + b]], writes=[Sb[1 + b]], dur=4 * t_dve(128))
                    P.dma("sp", ns_o[1 + b].rearrange("h d e -> d h e"), S[1 + b], d_misc, reads=[Sb[1 + b]], nbytes=2048)

        front(0)
        for m in range(NT):
            if CFG['backfirst']:
                back(m)
                if m + 1 < NT:
                    front(m + 1)
            else:
                if m + 1 < NT:
                    front(m + 1)
                back(m)

        P.barrier(lambda e: e.memset(stat[:, 60:61], 0.0))
        yb = [bf(f"yacc{m}") for m in range(NT)]
        g2bb = cload("g2b", g2b, norm2.partition_broadcast(128))
        gfbb = cload("gfb", gfb, norm_f.partition_broadcast(128))
        ringub, ringdb = bf2("ringu"), bf2("ringd")
        rsu = [P.dsem("dru0"), P.dsem("dru1")]
        rsd = [P.dsem("drd0"), P.dsem("drd1")]

        def load_chunk(c):
            s = c % 2
            P.dma("pool", ring_up[s], w_up[:, c * 512:(c + 1) * 512].rearrange("(k p) f -> p k f", p=128), rsu[s], writes=[ringub[s]], nbytes=16384)
            P.dma("pool", ring_dn[s], w_down[c * 512:(c + 1) * 512, :].rearrange("(j p) n -> p j n", p=128), rsd[s], writes=[ringdb[s]], nbytes=16384)

        load_chunk(0)
        load_chunk(1)
        a2b = bf2("a2_bf")
        for m in range(NT):
            xi = m % 2
            pi = m % 2
            P.dma("sp", xbuf[xi], xs[m], xsem[xi], writes=[xb[xi]], nbytes=4096)
            pr = m % 2
            mk = mix_k(m)
            fns = []
            for k in range(8):
                for n in range(2):
                    fns.append(lambda e, k=k, n=n, mk=mk, pr=pr: e.matmul(bank(2 * pr + n), lhsT=mk[:, k, :], rhs=wout_t[:, k, n * 512:(n + 1) * 512],
                                                                         start=(k == 0), stop=(k == 7)))
            P.op("pe", fns, reads=[mixb[m], wob], writes=[bankbuf[2 * pr], bankbuf[2 * pr + 1]], dur=t_mm(512, 16))
            P.op("dve", lambda e, m=m, pr=pr, xi=xi: e.tensor_tensor(out=yacc[:, m, :], in0=pairs[pr][:, :], in1=xbuf[xi], op=ALU.add),
                 reads=[bankbuf[2 * pr], bankbuf[2 * pr + 1], xb[xi]], writes=[yb[m]], dur=t_dve(1024))
            rms(yacc[:, m, :], [yb[m]], g2b, [g2bb], a2_bf[pi], [a2b[pi]], st_n2[pi], junk2)
            transposes([a2_bf[pi][:, k * 128:(k + 1) * 128] for k in range(8)], [a2b[pi]], mk, [mixb[m]], trb=4)

        groups = [(0, 4), (4, 4), (8, 4), (12, 4), (16, 1)]
        uTb, urb = bf2("uT"), bf2("ur")
        ub_rot = up_rot = dn_rot = 0
        for c in range(NCH):
            s = c % 2
            for gi, (m0, nt_) in enumerate(groups):
                ntok = nt_ * 128
                ui = ub_rot % 2
                ub_rot += 1
                for j in range(4):
                    bk = 4 + (up_rot % 4)
                    up_rot += 1
                    fns = [(lambda e, k=k, j=j, bk=bk, s=s, ntok=ntok, gi=gi: e.matmul(bank(bk)[:, 0:ntok], lhsT=ring_up[s][:, k, j * 128:(j + 1) * 128],
                                                                                  rhs=mix_grp(gi, k), start=(k == 0), stop=(k == 7))) for k in range(8)]
                    P.op("pe", fns, reads=[mixb[m] for m in range(m0, m0 + nt_)] + [ringub[s]], writes=[bankbuf[bk]], dur=t_mm(ntok, 8))
                    ri = up_rot % 2
                    P.op("act", lambda e, bk=bk, ri=ri, ntok=ntok: e.activation(out=urel[ri][:, 0:ntok], in_=bank(bk)[:, 0:ntok], func=AF.Relu),
                         reads=[bankbuf[bk]], writes=[urb[ri]], dur=t_act(ntok))
                    P.op("act", lambda e, ri=ri, ui=ui, j=j, ntok=ntok: e.activation(out=uT[ui][:, j, 0:ntok], in_=urel[ri][:, 0:ntok], func=AF.Square),
                         reads=[urb[ri]], writes=[uTb[ui]], dur=t_act(ntok))
                for t in range(nt_):
                    m = m0 + t
                    pr = dn_rot % 2
                    dn_rot += 1
                    fns = []
                    for j in range(4):
                        for n in range(2):
                            fns.append(lambda e, j=j, n=n, t=t, pr=pr, ui=ui, s=s: e.matmul(bank(2 * pr + n), lhsT=uT[ui][:, j, t * 128:(t + 1) * 128],
                                                                                           rhs=ring_dn[s][:, j, n * 512:(n + 1) * 512], start=(j == 0), stop=(j == 3)))
                    P.op("pe", fns, reads=[uTb[ui], ringdb[s]], writes=[bankbuf[2 * pr], bankbuf[2 * pr + 1]], dur=t_mm(512, 8))
                    P.op("dve", lambda e, m=m, pr=pr: e.tensor_tensor(out=yacc[:, m, :], in0=pairs[pr][:, :], in1=yacc[:, m, :], op=ALU.add),
                         reads=[bankbuf[2 * pr], bankbuf[2 * pr + 1]], writes=[yb[m]], dur=t_dve(1024))
            if c + 2 < NCH:
                load_chunk(c + 2)

        ysem = [P.dsem("dy0"), P.dsem("dy1")]
        ysb = bf2("ys")
        for m in range(NT):
            yi = m % 2
            rms(yacc[:, m, :], [yb[m]], gfb, [gfbb], ystage[yi], [ysb[yi]], st_nf[yi], junk2)
            P.dma("sp", y_o[m], ystage[yi], ysem[yi], reads=[ysb[yi]], nbytes=4096)
        import os
        if os.environ.get('MK_WARM', '0') == '1':
            P.warm_ap = ident
        P.emit([ysem[0], ysem[1], d_misc])
    return nc


def _host_consts():
    pos = np.zeros((128, NT), np.float64)
    for m in range(16):
        pos[:, m] = 128 * m + np.arange(128)
    pos[:, 16] = 2048 + (np.arange(128) % 64)
    pos32 = pos.astype(np.float32)
    try:
        import jax
        import jax.numpy as jnp
        with jax.default_device(jax.devices("cpu")[0]):
            inv_a = np.asarray(1.0 / (10000.0 ** (jnp.arange(0, 64, 2, dtype=jnp.float32) / 64)), dtype=np.float32)
            inv_r = np.asarray(1.0 / (10000.0 ** jnp.linspace(0.0, 1.0, 64, dtype=jnp.float32)), dtype=np.float32)
    except Exception:
        inv_a = (1.0 / (np.float32(10000.0) ** (np.arange(0, 64, 2, dtype=np.float32) / np.float32(64)))).astype(np.float32)
        inv_r = (1.0 / (np.float32(10000.0) ** np.linspace(0.0, 1.0, 64, dtype=np.float32))).astype(np.float32)
    ang_a = (pos32[:, :, None] * inv_a[None, None, :]).astype(np.float32).astype(np.float64)
    ang_r = (pos32[:, :, None] * inv_r[None, None, :]).astype(np.float32).astype(np.float64)
    cosa, sina = np.cos(ang_a).astype(np.float32), np.sin(ang_a).astype(np.float32)
    cosr, sinr = np.cos(ang_r).astype(np.float32), np.sin(ang_r).astype(np.float32)
    logg = np.log1p(-np.exp2(-5.0 - np.arange(4, dtype=np.float64)))
    sc = 128.0 ** -0.5
    j = np.arange(128)[:, None]
    i = np.arange(128)[None, :]
    dtab = np.zeros((128, 2, 4, 128), np.float64)
    qdec = np.zeros((128, 2, 4, 128), np.float64)
    kdec = np.zeros((128, 2, 4), np.float64)
    for h in range(4):
        d0 = np.where(i >= j, np.exp(logg[h] * np.maximum(i - j, 0)), 0.0)
        dtab[:, 0, h, :] = sc * d0
        same = (i // 64) == (j // 64)
        dtab[:, 1, h, :] = sc * np.where(same, d0, 0.0)
        qdec[:, 0, h, :] = np.exp(logg[h] * (np.arange(128) + 1.0))[None, :]
        qdec[:, 1, h, :] = np.exp(logg[h] * ((np.arange(128) % 64) + 1.0))[None, :]
        kdec[:, 0, h] = sc * np.exp(logg[h] * (127.0 - np.arange(128)))
        kdec[:, 1, h] = sc * np.exp(logg[h] * (63.0 - (np.arange(128) % 64)))
    return dict(cosa=cosa, sina=sina, cosr=cosr, sinr=sinr, dtab=dtab.astype(np.float32), qdec=qdec.astype(np.float32),
                kdec=kdec.astype(np.float32), ident=np.eye(128, dtype=np.float32))


_NC_CACHE = {}


def kernel(x_prompt, x_sample, cache_k, cache_v, state_ret, norm1, w_in, sinks, w_out, norm2, w_up, w_down, norm_f):
    f = lambda a: np.ascontiguousarray(np.asarray(a, dtype=np.float32))
    x_prompt, x_sample, cache_k, cache_v, state_ret = map(f, (x_prompt, x_sample, cache_k, cache_v, state_ret))
    norm1, w_in, sinks, w_out, norm2, w_up, w_down, norm_f = map(f, (norm1, w_in, sinks, w_out, norm2, w_up, w_down, norm_f))
    consts = _host_consts()
    perm = []
    for t in range(4):
        perm += list(range(64 * t, 64 * t + 64)) + list(range(256 + 64 * t, 256 + 64 * t + 64))
    perm += list(range(512, 1024))
    w_out_p = np.ascontiguousarray(w_out[0][perm, :])
    if "nc" not in _NC_CACHE:
        _NC_CACHE["nc"] = build_nc()
    nc = _NC_CACHE["nc"]
    in_maps = []
    for c in range(8):
        xs = np.concatenate([x_prompt[c].reshape(16, 128, D), x_sample[2 * c:2 * c + 2].reshape(1, 128, D)], axis=0)
        m = dict(xs=np.ascontiguousarray(xs),
                 ck=np.ascontiguousarray(cache_k[0, 2 * c:2 * c + 2].reshape(2, 128, 128)),
                 cv=np.ascontiguousarray(cache_v[0, 2 * c:2 * c + 2].reshape(2, 128, 128)),
                 sr=np.ascontiguousarray(state_ret[0, 2 * c:2 * c + 2]),
                 norm1=norm1[0], w_in=w_in[0], sinks=sinks[0], w_out=w_out_p, norm2=norm2[0], w_up=w_up[0], w_down=w_down[0], norm_f=norm_f)
        m.update(consts)
        in_maps.append(m)
    res = run_bass_kernel_spmd(nc, in_maps, core_ids=list(range(8)))
    R = res.results
    y_prompt = np.stack([R[c]["y"][:16].reshape(2048, D) for c in range(8)], 0)
    y_sample = np.concatenate([R[c]["y"][16].reshape(2, 64, D) for c in range(8)], 0)
    nk_p = np.stack([R[c]["nk"][0].reshape(128, 2, 64) for c in range(8)], 0)[None]
    nv_p = np.stack([R[c]["nv"][0].reshape(128, 2, 64) for c in range(8)], 0)[None]
    ns_p = np.stack([R[c]["ns"][0] for c in range(8)], 0)[None]
    nk_s = np.concatenate([R[c]["nk"][1:3].reshape(2, 128, 2, 64) for c in range(8)], 0)[None]
    nv_s = np.concatenate([R[c]["nv"][1:3].reshape(2, 128, 2, 64) for c in range(8)], 0)[None]
    ns_s = np.concatenate([R[c]["ns"][1:3] for c in range(8)], 0)[None]
    return (y_prompt.astype(np.float32), y_sample.astype(np.float32), nk_p.astype(np.float32), nv_p.astype(np.float32),
            ns_p.astype(np.float32), nk_s.astype(np.float32), nv_s.astype(np.float32), ns_s.astype(np.float32))
```

```python
from contextlib import ExitStack

import numpy as np
import concourse.bass as bass
import concourse.mybir as mybir
from concourse.bass_utils import run_bass_kernel_spmd

F32 = mybir.dt.float32
BF16 = mybir.dt.bfloat16
AF = mybir.ActivationFunctionType
ALU = mybir.AluOpType

NT = 17
D = 1024
INW = 2816
DFF = 4096
EPS = 1e-6
GAM = [1.0 - 2.0 ** (-5 - h) for h in range(4)]
NCH = 8
CFG = dict(pG0=1, pG1=0, pG2=0, trb=6, trbQ=2, trbR=2, backpair=(2, 3), ret=(4, 7, 7, 6), backfirst=False)


class Buf:
    __slots__ = ("w", "r", "name", "excl")

    def __init__(self, name="", excl=False):
        self.w = None
        self.r = []
        self.name = name
        self.excl = excl


class Prog:
    ENGS = ("pe", "act", "dve", "pool", "sp")

    def __init__(self, nc, stack):
        self.nc = nc
        self.stack = stack
        self.sem = {e: stack.enter_context(nc.semaphore("s_" + e)) for e in self.ENGS}
        self.nodes = []
        self.bar = None
        self.nd = 0

    def dsem(self, name=None):
        self.nd += 1
        return self.stack.enter_context(self.nc.semaphore(name or f"d{self.nd}"))

    def _deps(self, reads, writes, extra):
        d = set()
        for b in reads:
            if b.w is not None:
                d.add(b.w)
        for b in writes:
            if b.w is not None:
                d.add(b.w)
            d.update(b.r)
        d.update(x for x in extra if x is not None)
        if self.bar is not None:
            d.add(self.bar)
        return d

    def _reg(self, nid, reads, writes):
        for b in reads:
            b.r.append(nid)
        for b in writes:
            b.w = nid
            b.r = []

    def op(self, eng, fns, reads=(), writes=(), extra=(), dur=0.5):
        if not isinstance(fns, (list, tuple)):
            fns = [fns]
        writes = list(writes) + [b for b in reads if b.excl]
        reads = [b for b in reads if not b.excl]
        deps = self._deps(reads, writes, extra)
        nid = len(self.nodes)
        self.nodes.append(dict(eng=eng, fns=list(fns), deps=deps, dur=dur, lat=0.0, dsem=None))
        self._reg(nid, reads, writes)
        return nid

    def dma(self, eng, out, in_, dsem, reads=(), writes=(), extra=(), nbytes=4096):
        deps = self._deps(reads, writes, extra)
        nid = len(self.nodes)
        iss = 1.5 if eng == "pool" else 0.2
        self.nodes.append(dict(eng=eng, fns=[lambda e: e.dma_start(out=out, in_=in_)], deps=deps, dur=iss,
                               lat=2.0 + nbytes * 128 / 280e3, dsem=dsem))
        self._reg(nid, reads, writes)
        return nid

    def barrier(self, fn):
        nid = len(self.nodes)
        self.nodes.append(dict(eng="dve", fns=[fn], deps=set(range(nid)), dur=0.1, lat=0.0, dsem=None))
        self.bar = nid
        return nid

    def schedule(self):
        N = len(self.nodes)
        bar = self.bar
        succ = [[] for _ in range(N)]
        for i, n in enumerate(self.nodes):
            if i == bar:
                continue
            for d in n["deps"]:
                succ[d].append(i)
        prio = [0.0] * N
        for i in range(N - 1, -1, -1):
            n = self.nodes[i]
            best = 0.0
            for s in succ[i]:
                if prio[s] > best:
                    best = prio[s]
            if bar is not None and i < bar and prio[bar] > best:
                best = prio[bar]
            prio[i] = n["dur"] + n["lat"] + best
        ndep = [len(n["deps"]) for n in self.nodes]
        fin = [0.0] * N
        depfin = [0.0] * N
        tfree = {e: 0.0 for e in self.ENGS}
        ready = {e: [] for e in self.ENGS}
        for i in range(N):
            if ndep[i] == 0:
                ready[self.nodes[i]["eng"]].append(i)
        order = {e: [] for e in self.ENGS}
        done = 0
        while done < N:
            best = None
            for e in self.ENGS:
                r = ready[e]
                if not r:
                    continue
                tf = tfree[e]
                bi = None
                bk = None
                for i in r:
                    st = depfin[i] if depfin[i] > tf else tf
                    k = (st, -prio[i])
                    if bk is None or k < bk:
                        bk = k
                        bi = i
                if best is None or bk < best[0]:
                    best = (bk, e, bi)
            (st, _), e, i = best
            n = self.nodes[i]
            if not hasattr(self, "why"):
                self.why = {}
                self.stt = {}
                self.lastn = {}
            self.stt[i] = st
            if depfin[i] >= tfree[e] - 1e-9:
                dd = [d for d in n["deps"] if abs(fin[d] - depfin[i]) < 1e-9]
                self.why[i] = ("dep", dd[0] if dd else None)
            else:
                self.why[i] = ("eng", self.lastn.get(e))
            self.lastn[e] = i
            ready[e].remove(i)
            tfree[e] = st + n["dur"]
            fin[i] = st + n["dur"] + n["lat"]
            order[e].append(i)
            done += 1
            if bar is not None and i < bar:
                ndep[bar] -= 1
                if depfin[bar] < fin[i]:
                    depfin[bar] = fin[i]
                if ndep[bar] == 0:
                    ready["dve"].append(bar)
            for s in succ[i]:
                ndep[s] -= 1
                if depfin[s] < fin[i]:
                    depfin[s] = fin[i]
                if ndep[s] == 0:
                    ready[self.nodes[s]["eng"]].append(s)
        import os
        if os.environ.get("MK_SERIAL") == "1":
            order = {e: [i for i in range(N) if self.nodes[i]["eng"] == e] for e in self.ENGS}
        self.order = order
        self.est_total = max(fin) if fin else 0.0
        self.fin = fin

    def emit(self, final_dsems):
        nc = self.nc
        self.schedule()
        nodes = self.nodes
        idx = {}
        rank = {}
        dcount = {}
        for e in self.ENGS:
            k = 0
            for i in self.order[e]:
                ds_ = nodes[i]["dsem"]
                if ds_ is None:
                    k += 1
                    idx[i] = k
                else:
                    dcount[id(ds_)] = dcount.get(id(ds_), 0) + 1
                    rank[i] = dcount[id(ds_)]
        engmap = {"pe": "tensor", "act": "scalar", "dve": "vector", "pool": "gpsimd", "sp": "sync"}
        progs = {}
        for e in self.ENGS:
            seen = {}
            items = []
            for i in self.order[e]:
                n = nodes[i]
                best = {}
                for d in n["deps"]:
                    dn = nodes[d]
                    if dn["dsem"] is None:
                        if e == "pe" and dn["eng"] == "pe":
                            continue
                        s, v = self.sem[dn["eng"]], idx[d]
                    else:
                        s, v = dn["dsem"], 16 * rank[d]
                    k = id(s)
                    if k not in best or best[k][1] < v:
                        best[k] = (s, v)
                ws = []
                for k, (s, v) in best.items():
                    if seen.get(k, 0) >= v:
                        continue
                    seen[k] = v
                    ws.append((s, v))
                if n["dsem"] is None:
                    items.append((ws, n["fns"], self.sem[e], 1))
                else:
                    items.append((ws, n["fns"], n["dsem"], 16))
            progs[e] = items
        fw = [(ds_, 16 * dcount[id(ds_)]) for ds_ in final_dsems if id(ds_) in dcount]
        with nc.Block() as block:
            for ename in self.ENGS:
                items = progs[ename]
                f_ = fw if ename == "sp" else ()

                def body(e, items=items, f_=f_):
                    for ws, fns, s, inc in items:
                        for (ws_s, ws_v) in ws:
                            e.wait_ge(ws_s, ws_v)
                        nf = len(fns)
                        for j, fn in enumerate(fns):
                            ins = fn(e)
                            if j == nf - 1:
                                ins.then_inc(s, inc)
                    for (s, v) in f_:
                        e.wait_ge(s, v)
                getattr(block, engmap[ename])(body)


class Arena:
    def __init__(self, tensor, nbytes):
        self.t = tensor
        self.n = nbytes
        self.off = 0

    def alloc(self, shape, dt):
        esz = 2 if dt == BF16 else 4
        nel = int(np.prod(shape))
        nb = (nel * esz + 31) // 32 * 32
        at = self.off
        self.off += nb
        assert self.off <= self.n, ("arena overflow", self.off, self.n)
        a = self.t[:, at // 4:(at + nb) // 4]
        if dt == BF16:
            a = a.bitcast(BF16)
        a = a[:, 0:nel]
        if len(shape) == 2:
            a = a.rearrange("p (a b) -> p a b", a=shape[0])
        elif len(shape) == 3:
            a = a.rearrange("p (a b c) -> p a b c", a=shape[0], b=shape[1])
        return a


def t_act(F):
    return 0.22 + F / 1300.0


def t_dve(F, slow=1.0):
    return 0.2 + slow * F / 950.0


def t_mm(N, n=1):
    return n * (0.035 + N / 2300.0)


def build_nc():
    nc = bass.Bass("TRN2", target_bir_lowering=False)
    di = lambda n, s: nc.dram_tensor(n, s, F32, kind="ExternalInput").ap()
    do = lambda n, s: nc.dram_tensor(n, s, F32, kind="ExternalOutput").ap()
    xs = di("xs", [NT, 128, D])
    ck = di("ck", [2, 128, 128])
    cv = di("cv", [2, 128, 128])
    sr = di("sr", [2, 4, 128, 128])
    norm1 = di("norm1", [D])
    w_in = di("w_in", [D, INW])
    sinks = di("sinks", [8])
    w_out = di("w_out", [D, D])
    norm2 = di("norm2", [D])
    w_up = di("w_up", [D, DFF])
    w_down = di("w_down", [DFF, D])
    norm_f = di("norm_f", [D])
    cosa_d = di("cosa", [128, NT, 32])
    sina_d = di("sina", [128, NT, 32])
    cosr_d = di("cosr", [128, NT, 64])
    sinr_d = di("sinr", [128, NT, 64])
    dt_d = di("dtab", [128, 2, 4, 128])
    qdec_d = di("qdec", [128, 2, 4, 128])
    kdec_d = di("kdec", [128, 2, 4])
    ident_d = di("ident", [128, 128])
    y_o = do("y", [NT, 128, D])
    nk_o = do("nk", [3, 128, 128])
    nv_o = do("nv", [3, 128, 128])
    ns_o = do("ns", [3, 4, 128, 128])

    with ExitStack() as st:
        P = Prog(nc, st)
        sbt = lambda name, shape, dt: st.enter_context(nc.sbuf_tensor(name, shape, dt))
        pairs = [st.enter_context(nc.psum_tensor(f"pp{i}", [128, 1024], F32)) for i in range(4)]
        _pb = [Buf(f"pair{i}", excl=True) for i in range(4)]
        bankbuf = [_pb[i // 2] for i in range(8)]

        def bank(b):
            return pairs[b // 2][:, (b % 2) * 512:(b % 2) * 512 + 512]

        def bank_bf(b):
            return bank(b).bitcast(BF16)

        PERS = 44 * 1024
        A1 = 151296
        pers_t = sbt("pers", [128, PERS // 4], F32)
        a1_t = sbt("a1", [128, A1 // 4], F32)
        wout_t = sbt("wout", [128, 8, D], BF16)
        pers = Arena(pers_t, PERS)
        mixflat = pers.alloc([NT * 1024], BF16)
        xbuf = [pers.alloc([D], F32) for _ in range(2)]
        ident = pers.alloc([128], BF16)
        ones = pers.alloc([64], BF16)
        stat = pers.alloc([64], F32)
        es = pers.alloc([8], F32)
        es_sel = pers.alloc([4], F32)
        sk = pers.alloc([8], F32)

        def mix_k(m):
            g, t = m // 4, m % 4
            ks = 512 if g < 4 else 128
            v = mixflat[:, g * 4096:g * 4096 + 8 * ks].rearrange("p (k x) -> p k x", k=8)
            return v[:, :, t * 128:(t + 1) * 128]

        def mix_grp(g, k):
            ks = 512 if g < 4 else 128
            return mixflat[:, g * 4096 + k * ks:g * 4096 + (k + 1) * ks]

        a1 = Arena(a1_t, A1)
        w_in_sb = a1.alloc([8, INW], BF16)
        g1b = a1.alloc([D], F32)
        cosa = a1.alloc([NT, 32], F32)
        sina = a1.alloc([NT, 32], F32)
        cosr = a1.alloc([NT, 64], F32)
        sinr = a1.alloc([NT, 64], F32)
        dtab = a1.alloc([2, 4, 128], F32)
        qdec = a1.alloc([2, 4, 128], F32)
        kdec = a1.alloc([2, 4], F32)
        dbl = lambda shape, dt: [a1.alloc(shape, dt) for _ in range(2)]
        a_bf = dbl([D], BF16)
        aT = dbl([8, 128], BF16)
        qrot = dbl([4, 2, 64], BF16)
        krot = dbl([128], BF16)
        QT = dbl([2, 4, 64], BF16)
        qkr = dbl([8, 128], BF16)
        QKrT = dbl([8, 128], BF16)
        QdT = dbl([4, 128], BF16)
        Kd = dbl([4, 128], BF16)
        Vr = dbl([4, 128], BF16)
        sg = dbl([4, 128], F32)
        scm = dbl([4, 128], BF16)
        rmix = dbl([4, 128], BF16)
        PTpc = dbl([1024], BF16)
        PTp = [a[:, 0:512] for a in PTpc]
        PTc = [a[:, 512:1024] for a in PTpc]
        rec = a1.alloc([2, 4, 64], F32)
        sgs = a1.alloc([512], F32)
        tcA = a1.alloc([640], F32)
        tsA = a1.alloc([640], F32)
        tcR = a1.alloc([D], F32)
        tsR = a1.alloc([D], F32)
        junk = a1.alloc([D], BF16)
        krot_f = [a1.alloc([128], F32) for _ in range(2)]
        vout_f = [a1.alloc([128], F32) for _ in range(2)]
        KT = [a1.alloc([128], BF16) for _ in range(3)]
        Vt = [a1.alloc([128], BF16) for _ in range(3)]
        KTc = [a1.alloc([128], BF16) for _ in range(2)]
        Vc = [a1.alloc([128], BF16) for _ in range(2)]
        cstage = [a1.alloc([256], F32) for _ in range(2)]
        cstage_bf = [a1.alloc([128], BF16) for _ in range(2)]
        S = [a1.alloc([4, 128], F32) for _ in range(3)]
        Sbf = [a1.alloc([4, 128], BF16) for _ in range(3)]
        a2 = Arena(a1_t, A1)
        yacc = a2.alloc([NT, D], F32)
        g2b = a2.alloc([D], F32)
        gfb = a2.alloc([D], F32)
        a2_bf = [a2.alloc([D], BF16) for _ in range(2)]
        uT = [a2.alloc([4, 512], BF16) for _ in range(2)]
        urel = [a2.alloc([512], F32) for _ in range(2)]
        ystage = [a2.alloc([D], F32) for _ in range(2)]
        junk2 = a2.alloc([D], BF16)
        ring_up = [a2.alloc([8, 512], BF16) for _ in range(2)]
        ring_dn = [a2.alloc([4, D], BF16) for _ in range(2)]

        B = {}

        def bf(name):
            if name not in B:
                B[name] = Buf(name)
            return B[name]

        def bf2(name):
            return [bf(name + "0"), bf(name + "1")]

        def cload(name, dst, src, eng="sp", nbytes=4096):
            b_ = bf(name)
            P.dma(eng, dst, src, P.dsem("d_" + name), writes=[b_], nbytes=nbytes)
            return b_

        fl2 = lambda a: a.rearrange("p a b -> p (a b)")
        fl3 = lambda a: a.rearrange("p a b c -> p (a b c)")
        g1bb = cload("g1b", g1b, norm1.partition_broadcast(128))
        idb = cload("ident", ident, ident_d, eng="pool", nbytes=512)
        skb = cload("sk", sk, sinks.partition_broadcast(128), nbytes=65536)
        xsem = [P.dsem("dx0"), P.dsem("dx1")]
        xb = bf2("x")
        x0n = P.dma("sp", xbuf[0], xs[0], xsem[0], writes=[xb[0]], nbytes=4096)
        wbuf = {}
        wprev = None
        wfirst = [x0n, g1bb.w, idb.w]
        for gi_, (c0_, c1_) in enumerate(((0, 768), (768, 1792), (1792, INW))):
            b_ = bf(f"win{gi_}")
            wprev = P.dma("pool", w_in_sb[:, :, c0_:c1_], w_in[:, c0_:c1_].rearrange("(k p) n -> p k n", p=128), P.dsem(f"dwin{gi_}"), writes=[b_],
                          extra=[wprev] + (wfirst if gi_ == 0 else []), nbytes=(c1_ - c0_) * 32)
            wbuf[c0_] = b_
        cosab = cload("cosa", fl2(cosa), fl2(cosa_d), nbytes=2176)
        sinab = cload("sina", fl2(sina), fl2(sina_d), nbytes=2176)
        cosrb = cload("cosr", fl2(cosr), fl2(cosr_d), nbytes=4352)
        sinrb = cload("sinr", fl2(sinr), fl2(sinr_d), nbytes=4352)
        dtabb = cload("dtab", fl3(dtab), fl3(dt_d))
        qdecb = cload("qdec", fl3(qdec), fl3(qdec_d))
        kdecb = cload("kdec", fl2(kdec), fl2(kdec_d), nbytes=32)
        wob = bf("wout")
        P.dma("pool", wout_t[:], w_out.rearrange("(k p) n -> p k n", p=128), P.dsem("dwo"), writes=[wob], extra=[wprev], nbytes=16384)
        onb = bf("ones")
        P.op("dve", lambda e: e.memset(ones, 1.0), writes=[onb], dur=0.1)
        esb = bf("es")
        P.op("act", lambda e: e.activation(out=es, in_=sk, func=AF.Exp), reads=[skb], writes=[esb], dur=0.3)
        essb = bf("es_sel")
        P.op("dve", lambda e: e.tensor_copy(out=es_sel[0:64, :], in_=es[0:64, 0:4]), reads=[esb], writes=[essb], dur=0.1)
        P.op("dve", lambda e: e.tensor_copy(out=es_sel[64:128, :], in_=es[64:128, 4:8]), reads=[esb], writes=[essb], dur=0.1)

        stat_n = [0]

        def stat_col(n):
            c0 = stat_n[0]
            stat_n[0] += n
            assert stat_n[0] <= 60
            return stat[:, c0:c0 + n], bf(f"stat{c0}")

        st_n1 = [stat_col(1) for _ in range(2)]
        st_g = [stat_col(4) for _ in range(2)]
        st_n2 = [stat_col(1) for _ in range(2)]
        st_nf = [stat_col(1) for _ in range(2)]

        _jb = {}

        def jb_of(ap):
            k = id(ap)
            if k not in _jb:
                _jb[k] = Buf("junk")
            return _jb[k]

        def rms(src_ap, src_bufs, gain_ap, gain_bufs, out_ap, out_bufs, stc, junk_ap, F=1024):
            col, cb = stc
            fns = [lambda e: e.activation(out=junk_ap, in_=src_ap, func=AF.Square, accum_out=col),
                   lambda e: e.activation(out=col, in_=col, func=AF.Ln, scale=1.0 / D, bias=EPS),
                   lambda e: e.activation(out=col, in_=col, func=AF.Exp, scale=-0.5)]
            P.op("act", fns[0], reads=src_bufs, writes=[cb, jb_of(junk_ap)], dur=t_act(F))
            P.op("act", fns[1], writes=[cb], dur=0.25)
            P.op("act", fns[2], writes=[cb], dur=0.25)
            P.op("dve", lambda e: e.scalar_tensor_tensor(out=out_ap, in0=src_ap, scalar=col, in1=gain_ap, op0=ALU.mult, op1=ALU.mult),
                 reads=list(src_bufs) + [cb] + list(gain_bufs), writes=out_bufs, dur=t_dve(F))

        TRB = 2

        def transposes(srcs, src_bufs, dst_ap, dst_bufs, trb=TRB):
            n = len(srcs)
            tb = bank_bf(trb)
            fns = [(lambda e, i=i, s=s: e.transpose(out=tb[:, i * 128:(i + 1) * 128], in_=s, identity=ident)) for i, s in enumerate(srcs)]
            P.op("pe", fns, reads=list(src_bufs) + [idb], writes=[bankbuf[trb]], dur=t_mm(128, n))
            src = tb[:, 0:n * 128].rearrange("p (a b) -> p a b", a=n)
            P.op("act", lambda e: e.copy(out=dst_ap, in_=src), reads=[bankbuf[trb]], writes=dst_bufs, dur=t_act(n * 128))

        csb = bf2("cst")
        Sb = [bf("S0"), bf("S1"), bf("S2")]
        Sbfb = [bf("Sbf0"), bf("Sbf1"), bf("Sbf2")]
        d_misc = P.dsem("dmisc")
        KTcb, Vcb, csbf_b = bf2("KTc"), bf2("Vc"), bf2("csbf")
        csvb = bf2("cstv")
        for b in range(2):
            P.dma("sp", cstage[b][:, 0:128], ck[b], P.dsem(f"dck{b}"), writes=[csb[b]], nbytes=512)
            P.dma("sp", cstage[b][:, 128:256], cv[b], P.dsem(f"dcv{b}"), writes=[csvb[b]], nbytes=512)
            P.dma("sp", S[1 + b], sr[b].rearrange("h d e -> d h e"), P.dsem(f"dsr{b}"), writes=[Sb[1 + b]], nbytes=2048)
            P.op("dve", lambda e, b=b: e.tensor_copy(out=cstage_bf[b], in_=cstage[b][:, 0:128]), reads=[csb[b]], writes=[csbf_b[b]], dur=0.3)
            P.op("dve", lambda e, b=b: e.tensor_copy(out=Vc[b], in_=cstage[b][:, 128:256]), reads=[csvb[b]], writes=[Vcb[b]], dur=0.3)
            transposes([cstage_bf[b]], [csbf_b[b]], KTc[b].unsqueeze(1), [KTcb[b]])
            P.op("act", lambda e, b=b: e.copy(out=Sbf[1 + b], in_=S[1 + b]), reads=[Sb[1 + b]], writes=[Sbfb[1 + b]], dur=t_act(512))
            P.dma("sp", nk_o[1 + b, 0:64, :], ck[b, 64:128, :], d_misc, nbytes=512)
            P.dma("sp", nv_o[1 + b, 0:64, :], cv[b, 64:128, :], d_misc, nbytes=512)

        abf_b, aT_b = bf2("a_bf"), bf2("aT")
        tcAb, tsAb, tcRb, tsRb = bf("tcA"), bf("tsA"), bf("tcR"), bf("tsR")
        qrb, krb, QTb = bf2("qrot"), bf2("krot"), bf2("QT")
        KTb, Vtb = [bf(f"KT{i}") for i in range(3)], [bf(f"V{i}") for i in range(3)]
        PTcb, PTpb = bf2("PTc"), bf2("PTp")
        recb, sgsb = bf("rec"), bf("sgs")
        qkrb, QKrTb, QdTb, Kdb, Vrb, sgb, scmb, rmixb = (bf2(n) for n in ("qkr", "QKrT", "QdT", "Kd", "Vr", "sg", "scm", "rmix"))
        mixb = [bf(f"mix{m}") for m in range(NT)]
        krfb, vofb = bf2("krf"), bf2("vof")

        def front(m):
            sample = (m == NT - 1)
            ty = 1 if sample else 0
            xi = m % 2
            pi = m % 2
            if m > 0:
                P.dma("sp", xbuf[xi], xs[m], xsem[xi], writes=[xb[xi]], nbytes=4096)
            rms(xbuf[xi], [xb[xi]], g1b, [g1bb], a_bf[pi], [abf_b[pi]], st_n1[pi], junk)
            transposes([a_bf[pi][:, k * 128:(k + 1) * 128] for k in range(8)], [abf_b[pi]], aT[pi], [aT_b[pi]], trb=CFG['trb'])

            def proj(col0, ncols, pr, pi=pi):
                fns = []
                nb = (ncols + 511) // 512
                tot = 0.0
                for k in range(8):
                    for n in range(nb):
                        c0 = col0 + n * 512
                        w_ = min(512, col0 + ncols - c0)
                        tot += t_mm(w_)
                        fns.append(lambda e, k=k, n=n, c0=c0, w_=w_: e.matmul(
                            bank(2 * pr + n)[:, 0:w_], lhsT=aT[pi][:, k, :], rhs=w_in_sb[:, k, c0:c0 + w_],
                            start=(k == 0), stop=(k == 7)))
                P.op("pe", fns, reads=[aT_b[pi], wbuf[col0]], writes=[bankbuf[2 * pr + n] for n in range(nb)], dur=tot)

            PG0, PG1, PG2 = CFG['pG0'], CFG['pG1'], CFG['pG2']
            proj(0, 768, PG0)
            z0 = pairs[PG0]
            zall = z0[:, 0:640].rearrange("p (h two r) -> p h two r", h=10, two=2)
            tca = tcA.rearrange("p (h two r) -> p h two r", h=10, two=2)
            tsa = tsA.rearrange("p (h two r) -> p h two r", h=10, two=2)
            csb_ = cosa[:, m, :].unsqueeze(1).unsqueeze(1).broadcast_to([128, 10, 2, 32])
            snb_ = sina[:, m, :].unsqueeze(1).unsqueeze(1).broadcast_to([128, 10, 2, 32])
            P.op("dve", lambda e, csb_=csb_: e.tensor_tensor(out=tca, in0=zall, in1=csb_, op=ALU.mult),
                 reads=[bankbuf[2 * PG0], cosab], writes=[tcAb], dur=t_dve(640))
            P.op("dve", lambda e, snb_=snb_: e.tensor_tensor(out=tsa, in0=zall, in1=snb_, op=ALU.mult),
                 reads=[bankbuf[2 * PG0], sinab], writes=[tsAb], dur=t_dve(640))
            cur = m % 3
            prv = (m + 2) % 3
            P.op("act", lambda e, cur=cur: e.copy(out=Vt[cur], in_=z0[:, 640:768]), reads=[bankbuf[2 * PG0]], writes=[Vtb[cur]], dur=t_act(128))
            if m >= NT - 2:
                j = m - (NT - 2)
                P.op("act", lambda e, j=j: e.copy(out=vout_f[j], in_=z0[:, 640:768]), reads=[bankbuf[2 * PG0]], writes=[vofb[j]], dur=t_act(128))
            tcq = tcA[:, 0:512].rearrange("p (g t two r) -> p g t two r", g=2, t=4, two=2)
            tsq = tsA[:, 0:512].rearrange("p (g t two r) -> p g t two r", g=2, t=4, two=2)
            qro = qrot[pi].rearrange("p t g (two r) -> p g t two r", two=2)
            tck = tcA[:, 512:640].rearrange("p (h two r) -> p h two r", h=2, two=2)
            tsk = tsA[:, 512:640].rearrange("p (h two r) -> p h two r", h=2, two=2)
            kro = krot[pi].rearrange("p (h two r) -> p h two r", h=2, two=2)
            P.op("dve", lambda e, qro=qro: e.tensor_tensor(out=qro[:, :, :, 0, :], in0=tcq[:, :, :, 0, :], in1=tsq[:, :, :, 1, :], op=ALU.subtract),
                 reads=[tcAb, tsAb], writes=[qrb[pi]], dur=t_dve(256))
            P.op("dve", lambda e, qro=qro: e.tensor_tensor(out=qro[:, :, :, 1, :], in0=tcq[:, :, :, 1, :], in1=tsq[:, :, :, 0, :], op=ALU.add),
                 reads=[tcAb, tsAb], writes=[qrb[pi]], dur=t_dve(256))
            P.op("dve", lambda e, kro=kro: e.tensor_tensor(out=kro[:, :, 0, :], in0=tck[:, :, 0, :], in1=tsk[:, :, 1, :], op=ALU.subtract),
                 reads=[tcAb, tsAb], writes=[krb[pi]], dur=t_dve(64))
            P.op("dve", lambda e, kro=kro: e.tensor_tensor(out=kro[:, :, 1, :], in0=tck[:, :, 1, :], in1=tsk[:, :, 0, :], op=ALU.add),
                 reads=[tcAb, tsAb], writes=[krb[pi]], dur=t_dve(64))
            if m >= NT - 2:
                j = m - (NT - 2)
                krf = krot_f[j].rearrange("p (h two r) -> p h two r", h=2, two=2)
                P.op("dve", lambda e, krf=krf: e.tensor_tensor(out=krf[:, :, 0, :], in0=tck[:, :, 0, :], in1=tsk[:, :, 1, :], op=ALU.subtract),
                     reads=[tcAb, tsAb], writes=[krfb[j]], dur=t_dve(64))
                P.op("dve", lambda e, krf=krf: e.tensor_tensor(out=krf[:, :, 1, :], in0=tck[:, :, 1, :], in1=tsk[:, :, 0, :], op=ALU.add),
                     reads=[tcAb, tsAb], writes=[krfb[j]], dur=t_dve(64))
                if not sample:
                    P.dma("sp", nk_o[0], krot_f[j], d_misc, reads=[krfb[j]], nbytes=512)
                    P.dma("sp", nv_o[0], vout_f[j], d_misc, reads=[vofb[j]], nbytes=512)
                else:
                    for b in range(2):
                        P.dma("sp", nk_o[1 + b, 64:128, :], krot_f[j][64 * b:64 * b + 64, :], d_misc, reads=[krfb[j]], nbytes=512)
                        P.dma("sp", nv_o[1 + b, 64:128, :], vout_f[j][64 * b:64 * b + 64, :], d_misc, reads=[vofb[j]], nbytes=512)
            TRQ = CFG['trbQ']
            tb = bank_bf(TRQ)
            fns = [(lambda e, t=t, pi=pi: e.transpose(out=tb[:, t * 128:(t + 1) * 128], in_=qrot[pi][:, t, :, :].rearrange("p g d -> p (g d)"), identity=ident)) for t in range(4)]
            fns.append(lambda e, pi=pi: e.transpose(out=tb[:, 512:640], in_=krot[pi], identity=ident))
            P.op("pe", fns, reads=[qrb[pi], krb[pi], idb], writes=[bankbuf[TRQ]], dur=t_mm(128, 5))
            fns = [lambda e, pi=pi: e.copy(out=QT[pi], in_=tb[:, 0:512].rearrange("p (t c q) -> p c t q", t=4, c=2)),
                   lambda e, cur=cur: e.copy(out=KT[cur], in_=tb[:, 512:640])]
            P.op("act", fns, reads=[bankbuf[TRQ]], writes=[QTb[pi], KTb[cur]], dur=t_act(512) + t_act(128))

            proj(768, 1024, PG1)
            z1 = pairs[PG1]
            zr = z1[:, :].rearrange("p (h two r) -> p h two r", h=8, two=2)
            tcr = tcR.rearrange("p (h two r) -> p h two r", h=8, two=2)
            tsr = tsR.rearrange("p (h two r) -> p h two r", h=8, two=2)
            qko = qkr[pi].rearrange("p h (two r) -> p h two r", two=2)
            csr_ = cosr[:, m, :].unsqueeze(1).unsqueeze(1).broadcast_to([128, 8, 2, 64])
            snr_ = sinr[:, m, :].unsqueeze(1).unsqueeze(1).broadcast_to([128, 8, 2, 64])
            P.op("dve", lambda e, csr_=csr_: e.tensor_tensor(out=tcr, in0=zr, in1=csr_, op=ALU.mult), reads=[bankbuf[2 * PG1], cosrb], writes=[tcRb], dur=t_dve(1024))
            P.op("dve", lambda e, snr_=snr_: e.tensor_tensor(out=tsr, in0=zr, in1=snr_, op=ALU.mult), reads=[bankbuf[2 * PG1], sinrb], writes=[tsRb], dur=t_dve(1024))
            P.op("dve", lambda e, qko=qko: e.tensor_tensor(out=qko[:, :, 0, :], in0=tcr[:, :, 0, :], in1=tsr[:, :, 1, :], op=ALU.subtract), reads=[tcRb, tsRb], writes=[qkrb[pi]], dur=t_dve(512))
            P.op("dve", lambda e, qko=qko: e.tensor_tensor(out=qko[:, :, 1, :], in0=tcr[:, :, 1, :], in1=tsr[:, :, 0, :], op=ALU.add), reads=[tcRb, tsRb], writes=[qkrb[pi]], dur=t_dve(512))
            transposes([qkr[pi][:, h, :] for h in range(8)], [qkrb[pi]], QKrT[pi], [QKrTb[pi]], trb=CFG['trbR'])
            P.op("dve", lambda e, ty=ty, pi=pi: e.tensor_tensor(out=QdT[pi], in0=QKrT[pi][:, 0:4, :], in1=qdec[:, ty, :, :], op=ALU.mult),
                 reads=[QKrTb[pi], qdecb], writes=[QdTb[pi]], dur=t_dve(512))
            P.op("dve", lambda e, ty=ty, pi=pi: e.tensor_tensor(out=Kd[pi], in0=qkr[pi][:, 4:8, :], in1=kdec[:, ty, :].unsqueeze(2).broadcast_to([128, 4, 128]), op=ALU.mult),
                 reads=[qkrb[pi], kdecb], writes=[Kdb[pi]], dur=t_dve(512))
            proj(1792, 1024, PG2)
            P.op("act", lambda e, pi=pi: e.copy(out=Vr[pi].rearrange("p a b -> p (a b)"), in_=pairs[PG2][:, 0:512]), reads=[bankbuf[2 * PG2]], writes=[Vrb[pi]], dur=t_act(512))
            gps = pairs[PG2][:, 512:1024]
            P.op("act", lambda e: e.activation(out=sgs, in_=gps, func=AF.Exp, scale=-1.0), reads=[bankbuf[2 * PG2]], writes=[sgsb], dur=t_act(512))
            P.op("act", lambda e: e.activation(out=sgs, in_=sgs, func=AF.Ln, bias=1.0), writes=[sgsb], dur=t_act(512))
            P.op("act", lambda e: e.activation(out=sgs, in_=sgs, func=AF.Exp, scale=-1.0), writes=[sgsb], dur=t_act(512))
            P.op("dve", lambda e, pi=pi: e.tensor_tensor(out=sg[pi].rearrange("p a b -> p (a b)"), in0=gps, in1=sgs, op=ALU.mult),
                 reads=[bankbuf[2 * PG2], sgsb], writes=[sgb[pi]], dur=t_dve(512))

        def back(m):
            sample = (m == NT - 1)
            ty = 1 if sample else 0
            pi = m % 2
            cur = m % 3
            prv = (m + 2) % 3
            pS, pO = CFG['backpair']
            SBc, SBp, OB_, DB_ = 2 * pS + 1, 2 * pS, 2 * pO, 2 * pO + 1
            has_prev = sample or m > 0
            QTf = QT[pi].rearrange("p c t q -> p (c t q)")
            for g in range(2):
                r0 = 64 * g
                fns = [lambda e, r0=r0, cur=cur, QTf=QTf: e.matmul(bank(SBc), lhsT=KT[cur][r0:r0 + 64, :], rhs=QTf[r0:r0 + 64, :], start=True, stop=True)]
                rd = [KTb[cur], QTb[pi]]
                if has_prev:
                    if not sample:
                        fns.append(lambda e, r0=r0, prv=prv, QTf=QTf: e.matmul(bank(SBp), lhsT=KT[prv][r0:r0 + 64, :], rhs=QTf[r0:r0 + 64, :], start=True, stop=True))
                        rd.append(KTb[prv])
                    else:
                        fns += [(lambda e, b=b, r0=r0, QTf=QTf: e.matmul(bank(SBp)[:, 256 * b:256 * b + 256], lhsT=KTc[b][r0:r0 + 64, :], rhs=QTf[r0:r0 + 64, 256 * b:256 * b + 256],
                                                                       start=True, stop=True)) for b in range(2)]
                        rd += KTcb
                P.op("pe", fns, reads=rd, writes=[bankbuf[SBc], bankbuf[SBp]], dur=t_mm(512, len(fns)))
                if has_prev:
                    P.op("act", lambda e, g=g: e.activation(out=PTpc[g], in_=pairs[pS][:, :], func=AF.Exp, scale=0.125),
                         reads=[bankbuf[SBc], bankbuf[SBp]], writes=[PTcb[g], PTpb[g]], dur=t_act(1024))
                else:
                    P.op("act", lambda e, g=g: e.activation(out=PTc[g], in_=bank(SBc), func=AF.Exp, scale=0.125),
                         reads=[bankbuf[SBc]], writes=[PTcb[g]], dur=t_act(512))
                fns = []
                for c in range(2):
                    contrib = []
                    if has_prev:
                        if sample:
                            contrib.append((Vc[c], PTp[g], 0, 128))
                        else:
                            contrib.append((Vt[prv], PTp[g], 0, 128) if c == 0 else (Vt[prv], PTp[g], 64, 128))
                    if sample:
                        contrib.append((Vt[cur], PTc[g], 64 * c, 64 * c + 64))
                    else:
                        contrib.append((Vt[cur], PTc[g], 0, 64) if c == 0 else (Vt[cur], PTc[g], 0, 128))
                    for (dbk_, lsel) in ((OB_, 0), (DB_, 1)):
                        for i, (vv, pt, k0, k1) in enumerate(contrib):
                            lhs = vv[k0:k1, r0:r0 + 64] if lsel == 0 else ones[k0:k1, :]
                            fns.append(lambda e, dbk_=dbk_, lhs=lhs, pt=pt, k0=k0, k1=k1, c=c, i=i, nctr=len(contrib), r0=r0: e.matmul(
                                bank(dbk_)[r0:r0 + 64, 256 * c:256 * c + 256], lhsT=lhs, rhs=pt[k0:k1, 256 * c:256 * c + 256],
                                start=(i == 0), stop=(i == nctr - 1)))
                rd = [Vtb[cur], PTcb[g], onb] + ([PTpb[g]] + (Vcb if sample else [Vtb[prv]]) if has_prev else [])
                P.op("pe", fns, reads=rd, writes=[bankbuf[OB_], bankbuf[DB_]], dur=t_mm(256, len(fns)))
            dbv = bank(DB_).rearrange("p (c t q) -> p c t q", c=2, t=4)
            obv = bank(OB_).rearrange("p (c t q) -> p c t q", c=2, t=4)
            fns = [(lambda e, t=t: e.activation(out=rec[:, :, t, :], in_=dbv[:, :, t, :], func=AF.Ln, bias=es_sel[:, t:t + 1])) for t in range(4)]
            P.op("act", fns, reads=[bankbuf[DB_], essb], writes=[recb], dur=4 * t_act(128))
            P.op("act", lambda e: e.activation(out=rec.rearrange("p c t q -> p (c t q)"), in_=rec.rearrange("p c t q -> p (c t q)"), func=AF.Exp, scale=-1.0),
                 writes=[recb], dur=t_act(512))
            mo = mix_k(m)[:, 0:4, :].rearrange("p t (c q) -> p c t q", c=2)
            P.op("dve", lambda e, mo=mo: e.tensor_tensor(out=mo, in0=obv, in1=rec, op=ALU.mult),
                 reads=[bankbuf[OB_], recb], writes=[mixb[m]], dur=t_dve(512))

            SC_, OR_, UB_, RT_ = CFG.get('ret', (5, 6, 7, 4))
            scv = bank(SC_).rearrange("p (a b) -> p a b", a=4)
            fns = [(lambda e, h=h, pi=pi: e.matmul(scv[:, h, :], lhsT=QKrT[pi][:, 4 + h, :], rhs=QKrT[pi][:, h, :], start=True, stop=True)) for h in range(4)]
            P.op("pe", fns, reads=[QKrTb[pi]], writes=[bankbuf[SC_]], dur=t_mm(128, 4))
            P.op("dve", lambda e, ty=ty, pi=pi: e.tensor_tensor(out=scm[pi], in0=scv, in1=dtab[:, ty, :, :], op=ALU.mult),
                 reads=[bankbuf[SC_], dtabb], writes=[scmb[pi]], dur=t_dve(512))
            orv = bank(OR_).rearrange("p (a b) -> p a b", a=4)
            fns = []
            rd = [scmb[pi], Vrb[pi]]
            if not sample:
                cross = (m > 0)
                for h in range(4):
                    fns.append(lambda e, h=h, cross=cross, pi=pi: e.matmul(orv[:, h, :], lhsT=scm[pi][:, h, :], rhs=Vr[pi][:, h, :], start=True, stop=not cross))
                    if cross:
                        fns.append(lambda e, h=h, pi=pi: e.matmul(orv[:, h, :], lhsT=QdT[pi][:, h, :], rhs=Sbf[0][:, h, :], start=False, stop=True))
                if cross:
                    rd += [QdTb[pi], Sbfb[0]]
            else:
                for h in range(4):
                    fns.append(lambda e, h=h, pi=pi: e.matmul(orv[:, h, :], lhsT=scm[pi][:, h, :], rhs=Vr[pi][:, h, :], start=True, stop=True))
                    for b in range(2):
                        fns.append(lambda e, h=h, b=b, pi=pi: e.matmul(orv[64 * b:64 * b + 64, h, :], lhsT=QdT[pi][:, h, 64 * b:64 * b + 64], rhs=Sbf[1 + b][:, h, :],
                                                                      start=False, stop=False, skip_group_check=True))
                rd += [QdTb[pi], Sbfb[1], Sbfb[2]]
            P.op("pe", fns, reads=rd, writes=[bankbuf[OR_]], dur=t_mm(128, len(fns)))
            gcol, gcb = st_g[pi]
            fns = [(lambda e, h=h, gcol=gcol: e.activation(out=junk[:, h * 128:(h + 1) * 128], in_=orv[:, h, :], func=AF.Square, accum_out=gcol[:, h:h + 1])) for h in range(4)]
            P.op("act", fns, reads=[bankbuf[OR_]], writes=[gcb, jb_of(junk)], dur=4 * t_act(128))
            P.op("act", lambda e, gcol=gcol: e.activation(out=gcol, in_=gcol, func=AF.Ln, scale=1.0 / 128, bias=EPS), writes=[gcb], dur=0.25)
            P.op("act", lambda e, gcol=gcol: e.activation(out=gcol, in_=gcol, func=AF.Exp, scale=-0.5), writes=[gcb], dur=0.25)
            fns = [(lambda e, h=h, gcol=gcol, pi=pi: e.scalar_tensor_tensor(out=rmix[pi][:, h, :], in0=orv[:, h, :], scalar=gcol[:, h:h + 1], in1=sg[pi][:, h, :],
                                                                           op0=ALU.mult, op1=ALU.mult)) for h in range(4)]
            P.op("dve", fns, reads=[bankbuf[OR_], gcb, sgb[pi]], writes=[rmixb[pi]], dur=4 * t_dve(128))
            transposes([rmix[pi][:, h, :] for h in range(4)], [rmixb[pi]], mix_k(m)[:, 4:8, :], [mixb[m]], trb=RT_)
            if not sample:
                ubv = bank(UB_).rearrange("p (a b) -> p a b", a=4)
                fns = [(lambda e, h=h, pi=pi: e.matmul(ubv[:, h, :], lhsT=Kd[pi][:, h, :], rhs=Vr[pi][:, h, :], start=True, stop=True)) for h in range(4)]
                P.op("pe", fns, reads=[Kdb[pi], Vrb[pi]], writes=[bankbuf[UB_]], dur=t_mm(128, 4))
                if m == 0:
                    P.op("dve", lambda e: e.tensor_copy(out=S[0], in_=ubv), reads=[bankbuf[UB_]], writes=[Sb[0]], dur=t_dve(512))
                else:
                    fns = [(lambda e, h=h: e.scalar_tensor_tensor(out=S[0][:, h, :], in0=S[0][:, h, :], scalar=float(GAM[h] ** 128), in1=ubv[:, h, :],
                                                                 op0=ALU.mult, op1=ALU.add)) for h in range(4)]
                    P.op("dve", fns, reads=[bankbuf[UB_]], writes=[Sb[0]], dur=4 * t_dve(128))
                if m < NT - 2:
                    P.op("act", lambda e: e.copy(out=Sbf[0], in_=S[0]), reads=[Sb[0]], writes=[Sbfb[0]], dur=t_act(512))
                else:
                    P.dma("sp", ns_o[0].rearrange("h d e -> d h e"), S[0], d_misc, reads=[Sb[0]], nbytes=2048)
            else:
                for b in range(2):
                    ubk = UB_ if b == 0 else SC_
                    ubv = bank(ubk).rearrange("p (a b) -> p a b", a=4)
                    fns = [(lambda e, h=h, b=b, ubv=ubv, pi=pi: e.matmul(ubv[:, h, :], lhsT=Kd[pi][64 * b:64 * b + 64, h, :], rhs=Vr[pi][64 * b:64 * b + 64, h, :], start=True, stop=True)) for h in range(4)]
                    P.op("pe", fns, reads=[Kdb[pi], Vrb[pi]], writes=[bankbuf[ubk]], dur=t_mm(128, 4))
                    fns = [(lambda e, h=h, b=b, ubv=ubv: e.scalar_tensor_tensor(out=S[1 + b][:, h, :], in0=S[1 + b][:, h, :], scalar=float(GAM[h] ** 64), in1=ubv[:, h, :],
                                                                               op0=ALU.mult, op1=ALU.add)) for h in range(4)]
                    P.op("dve", fns, reads=[bankbuf[ubk], Sbfb[1 + b]], writes=[Sb[1 + b]], dur=4 * t_dve(128))
                    P.dma("sp", ns_o[1 + b].rearrange("h d e -> d h e"), S[1 + b], d_misc, reads=[Sb[1 + b]], nbytes=2048)

        front(0)
        for m in range(NT):
            if CFG['backfirst']:
                back(m)
                if m + 1 < NT:
                    front(m + 1)
            else:
                if m + 1 < NT:
                    front(m + 1)
                back(m)

        P.barrier(lambda e: e.memset(stat[:, 60:61], 0.0))
        yb = [bf(f"yacc{m}") for m in range(NT)]
        g2bb = cload("g2b", g2b, norm2.partition_broadcast(128))
        gfbb = cload("gfb", gfb, norm_f.partition_broadcast(128))
        ringub, ringdb = bf2("ringu"), bf2("ringd")
        rsu = [P.dsem("dru0"), P.dsem("dru1")]
        rsd = [P.dsem("drd0"), P.dsem("drd1")]

        def load_chunk(c):
            s = c % 2
            P.dma("pool", ring_up[s], w_up[:, c * 512:(c + 1) * 512].rearrange("(k p) f -> p k f", p=128), rsu[s], writes=[ringub[s]], nbytes=16384)
            P.dma("pool", ring_dn[s], w_down[c * 512:(c + 1) * 512, :].rearrange("(j p) n -> p j n", p=128), rsd[s], writes=[ringdb[s]], nbytes=16384)

        load_chunk(0)
        load_chunk(1)
        a2b = bf2("a2_bf")
        for m in range(NT):
            xi = m % 2
            pi = m % 2
            P.dma("sp", xbuf[xi], xs[m], xsem[xi], writes=[xb[xi]], nbytes=4096)
            pr = m % 2
            mk = mix_k(m)
            fns = []
            for k in range(8):
                for n in range(2):
                    fns.append(lambda e, k=k, n=n, mk=mk, pr=pr: e.matmul(bank(2 * pr + n), lhsT=mk[:, k, :], rhs=wout_t[:, k, n * 512:(n + 1) * 512],
                                                                         start=(k == 0), stop=(k == 7)))
            P.op("pe", fns, reads=[mixb[m], wob], writes=[bankbuf[2 * pr], bankbuf[2 * pr + 1]], dur=t_mm(512, 16))
            P.op("dve", lambda e, m=m, pr=pr, xi=xi: e.tensor_tensor(out=yacc[:, m, :], in0=pairs[pr][:, :], in1=xbuf[xi], op=ALU.add),
                 reads=[bankbuf[2 * pr], bankbuf[2 * pr + 1], xb[xi]], writes=[yb[m]], dur=t_dve(1024))
            rms(yacc[:, m, :], [yb[m]], g2b, [g2bb], a2_bf[pi], [a2b[pi]], st_n2[pi], junk2)
            transposes([a2_bf[pi][:, k * 128:(k + 1) * 128] for k in range(8)], [a2b[pi]], mk, [mixb[m]], trb=4)

        groups = [(0, 4), (4, 4), (8, 4), (12, 4), (16, 1)]
        uTb, urb = bf2("uT"), bf2("ur")
        ub_rot = up_rot = dn_rot = 0
        for c in range(NCH):
            s = c % 2
            for gi, (m0, nt_) in enumerate(groups):
                ntok = nt_ * 128
                ui = ub_rot % 2
                ub_rot += 1
                for j in range(4):
                    bk = 4 + (up_rot % 4)
                    up_rot += 1
                    fns = [(lambda e, k=k, j=j, bk=bk, s=s, ntok=ntok, gi=gi: e.matmul(bank(bk)[:, 0:ntok], lhsT=ring_up[s][:, k, j * 128:(j + 1) * 128],
                                                                                  rhs=mix_grp(gi, k), start=(k == 0), stop=(k == 7))) for k in range(8)]
                    P.op("pe", fns, reads=[mixb[m] for m in range(m0, m0 + nt_)] + [ringub[s]], writes=[bankbuf[bk]], dur=t_mm(ntok, 8))
                    ri = up_rot % 2
                    P.op("act", lambda e, bk=bk, ri=ri, ntok=ntok: e.activation(out=urel[ri][:, 0:ntok], in_=bank(bk)[:, 0:ntok], func=AF.Relu),
                         reads=[bankbuf[bk]], writes=[urb[ri]], dur=t_act(ntok))
                    P.op("act", lambda e, ri=ri, ui=ui, j=j, ntok=ntok: e.activation(out=uT[ui][:, j, 0:ntok], in_=urel[ri][:, 0:ntok], func=AF.Square),
                         reads=[urb[ri]], writes=[uTb[ui]], dur=t_act(ntok))
                for t in range(nt_):
                    m = m0 + t
                    pr = dn_rot % 2
                    dn_rot += 1
                    fns = []
                    for j in range(4):
                        for n in range(2):
                            fns.append(lambda e, j=j, n=n, t=t, pr=pr, ui=ui, s=s: e.matmul(bank(2 * pr + n), lhsT=uT[ui][:, j, t * 128:(t + 1) * 128],
                                                                                           rhs=ring_dn[s][:, j, n * 512:(n + 1) * 512], start=(j == 0), stop=(j == 3)))
                    P.op("pe", fns, reads=[uTb[ui], ringdb[s]], writes=[bankbuf[2 * pr], bankbuf[2 * pr + 1]], dur=t_mm(512, 8))
                    P.op("dve", lambda e, m=m, pr=pr: e.tensor_tensor(out=yacc[:, m, :], in0=pairs[pr][:, :], in1=yacc[:, m, :], op=ALU.add),
                         reads=[bankbuf[2 * pr], bankbuf[2 * pr + 1]], writes=[yb[m]], dur=t_dve(1024))
            if c + 2 < NCH:
                load_chunk(c + 2)

        ysem = [P.dsem("dy0"), P.dsem("dy1")]
        ysb = bf2("ys")
        for m in range(NT):
            yi = m % 2
            rms(yacc[:, m, :], [yb[m]], gfb, [gfbb], ystage[yi], [ysb[yi]], st_nf[yi], junk2)
            P.dma("sp", y_o[m], ystage[yi], ysem[yi], reads=[ysb[yi]], nbytes=4096)
        P.emit([ysem[0], ysem[1], d_misc])
    return nc


def _host_consts():
    pos = np.zeros((128, NT), np.float64)
    for m in range(16):
        pos[:, m] = 128 * m + np.arange(128)
    pos[:, 16] = 2048 + (np.arange(128) % 64)
    pos32 = pos.astype(np.float32)
    try:
        import jax
        import jax.numpy as jnp
        with jax.default_device(jax.devices("cpu")[0]):
            inv_a = np.asarray(1.0 / (10000.0 ** (jnp.arange(0, 64, 2, dtype=jnp.float32) / 64)), dtype=np.float32)
            inv_r = np.asarray(1.0 / (10000.0 ** jnp.linspace(0.0, 1.0, 64, dtype=jnp.float32)), dtype=np.float32)
    except Exception:
        inv_a = (1.0 / (np.float32(10000.0) ** (np.arange(0, 64, 2, dtype=np.float32) / np.float32(64)))).astype(np.float32)
        inv_r = (1.0 / (np.float32(10000.0) ** np.linspace(0.0, 1.0, 64, dtype=np.float32))).astype(np.float32)
    ang_a = (pos32[:, :, None] * inv_a[None, None, :]).astype(np.float32).astype(np.float64)
    ang_r = (pos32[:, :, None] * inv_r[None, None, :]).astype(np.float32).astype(np.float64)
    cosa, sina = np.cos(ang_a).astype(np.float32), np.sin(ang_a).astype(np.float32)
    cosr, sinr = np.cos(ang_r).astype(np.float32), np.sin(ang_r).astype(np.float32)
    logg = np.log1p(-np.exp2(-5.0 - np.arange(4, dtype=np.float64)))
    sc = 128.0 ** -0.5
    j = np.arange(128)[:, None]
    i = np.arange(128)[None, :]
    dtab = np.zeros((128, 2, 4, 128), np.float64)
    qdec = np.zeros((128, 2, 4, 128), np.float64)
    kdec = np.zeros((128, 2, 4), np.float64)
    for h in range(4):
        d0 = np.where(i >= j, np.exp(logg[h] * np.maximum(i - j, 0)), 0.0)
        dtab[:, 0, h, :] = sc * d0
        same = (i // 64) == (j // 64)
        dtab[:, 1, h, :] = sc * np.where(same, d0, 0.0)
        qdec[:, 0, h, :] = np.exp(logg[h] * (np.arange(128) + 1.0))[None, :]
        qdec[:, 1, h, :] = np.exp(logg[h] * ((np.arange(128) % 64) + 1.0))[None, :]
        kdec[:, 0, h] = sc * np.exp(logg[h] * (127.0 - np.arange(128)))
        kdec[:, 1, h] = sc * np.exp(logg[h] * (63.0 - (np.arange(128) % 64)))
    return dict(cosa=cosa, sina=sina, cosr=cosr, sinr=sinr, dtab=dtab.astype(np.float32), qdec=qdec.astype(np.float32),
                kdec=kdec.astype(np.float32), ident=np.eye(128, dtype=np.float32))


_NC_CACHE = {}


def kernel(x_prompt, x_sample, cache_k, cache_v, state_ret, norm1, w_in, sinks, w_out, norm2, w_up, w_down, norm_f):
    f = lambda a: np.ascontiguousarray(np.asarray(a, dtype=np.float32))
    x_prompt, x_sample, cache_k, cache_v, state_ret = map(f, (x_prompt, x_sample, cache_k, cache_v, state_ret))
    norm1, w_in, sinks, w_out, norm2, w_up, w_down, norm_f = map(f, (norm1, w_in, sinks, w_out, norm2, w_up, w_down, norm_f))
    consts = _host_consts()
    perm = []
    for t in range(4):
        perm += list(range(64 * t, 64 * t + 64)) + list(range(256 + 64 * t, 256 + 64 * t + 64))
    perm += list(range(512, 1024))
    w_out_p = np.ascontiguousarray(w_out[0][perm, :])
    if "nc" not in _NC_CACHE:
        _NC_CACHE["nc"] = build_nc()
    nc = _NC_CACHE["nc"]
    in_maps = []
    for c in range(8):
        xs = np.concatenate([x_prompt[c].reshape(16, 128, D), x_sample[2 * c:2 * c + 2].reshape(1, 128, D)], axis=0)
        m = dict(xs=np.ascontiguousarray(xs),
                 ck=np.ascontiguousarray(cache_k[0, 2 * c:2 * c + 2].reshape(2, 128, 128)),
                 cv=np.ascontiguousarray(cache_v[0, 2 * c:2 * c + 2].reshape(2, 128, 128)),
                 sr=np.ascontiguousarray(state_ret[0, 2 * c:2 * c + 2]),
                 norm1=norm1[0], w_in=w_in[0], sinks=sinks[0], w_out=w_out_p, norm2=norm2[0], w_up=w_up[0], w_down=w_down[0], norm_f=norm_f)
        m.update(consts)
        in_maps.append(m)
    res = run_bass_kernel_spmd(nc, in_maps, core_ids=list(range(8)))
    R = res.results
    y_prompt = np.stack([R[c]["y"][:16].reshape(2048, D) for c in range(8)], 0)
    y_sample = np.concatenate([R[c]["y"][16].reshape(2, 64, D) for c in range(8)], 0)
    nk_p = np.stack([R[c]["nk"][0].reshape(128, 2, 64) for c in range(8)], 0)[None]
    nv_p = np.stack([R[c]["nv"][0].reshape(128, 2, 64) for c in range(8)], 0)[None]
    ns_p = np.stack([R[c]["ns"][0] for c in range(8)], 0)[None]
    nk_s = np.concatenate([R[c]["nk"][1:3].reshape(2, 128, 2, 64) for c in range(8)], 0)[None]
    nv_s = np.concatenate([R[c]["nv"][1:3].reshape(2, 128, 2, 64) for c in range(8)], 0)[None]
    ns_s = np.concatenate([R[c]["ns"][1:3] for c in range(8)], 0)[None]
    return (y_prompt.astype(np.float32), y_sample.astype(np.float32), nk_p.astype(np.float32), nv_p.astype(np.float32),
            ns_p.astype(np.float32), nk_s.astype(np.float32), nv_s.astype(np.float32), ns_s.astype(np.float32))
```

```python
from contextlib import ExitStack

import numpy as np
import concourse.bass as bass
import concourse.mybir as mybir
from concourse.bass_utils import run_bass_kernel_spmd

F32 = mybir.dt.float32
BF16 = mybir.dt.bfloat16
AF = mybir.ActivationFunctionType
ALU = mybir.AluOpType

NT = 17
D = 1024
INW = 2816
DFF = 4096
EPS = 1e-6
GAM = [1.0 - 2.0 ** (-5 - h) for h in range(4)]
NCH = 8
CFG = dict(pG0=1, pG1=0, pG2=0, trb=6, trbQ=2, trbR=2, backpair=(2, 3), ret=(4, 7, 7, 6), backfirst=False)


class Buf:
    __slots__ = ("w", "r", "name", "excl")

    def __init__(self, name="", excl=False):
        self.w = None
        self.r = []
        self.name = name
        self.excl = excl


class Prog:
    ENGS = ("pe", "act", "dve", "pool", "sp")

    def __init__(self, nc, stack):
        self.nc = nc
        self.stack = stack
        self.sem = {e: stack.enter_context(nc.semaphore("s_" + e)) for e in self.ENGS}
        self.nodes = []
        self.bar = None
        self.nd = 0

    def dsem(self, name=None):
        self.nd += 1
        return self.stack.enter_context(self.nc.semaphore(name or f"d{self.nd}"))

    def _deps(self, reads, writes, extra):
        d = set()
        for b in reads:
            if b.w is not None:
                d.add(b.w)
        for b in writes:
            if b.w is not None:
                d.add(b.w)
            d.update(b.r)
        d.update(x for x in extra if x is not None)
        if self.bar is not None:
            d.add(self.bar)
        return d

    def _reg(self, nid, reads, writes):
        for b in reads:
            b.r.append(nid)
        for b in writes:
            b.w = nid
            b.r = []

    def op(self, eng, fns, reads=(), writes=(), extra=(), dur=0.5):
        if not isinstance(fns, (list, tuple)):
            fns = [fns]
        writes = list(writes) + [b for b in reads if b.excl]
        reads = [b for b in reads if not b.excl]
        deps = self._deps(reads, writes, extra)
        nid = len(self.nodes)
        self.nodes.append(dict(eng=eng, fns=list(fns), deps=deps, dur=dur, lat=0.0, dsem=None))
        self._reg(nid, reads, writes)
        return nid

    def dma(self, eng, out, in_, dsem, reads=(), writes=(), extra=(), nbytes=4096):
        deps = self._deps(reads, writes, extra)
        nid = len(self.nodes)
        iss = 1.5 if eng == "pool" else 0.2
        self.nodes.append(dict(eng=eng, fns=[lambda e: e.dma_start(out=out, in_=in_)], deps=deps, dur=iss,
                               lat=2.0 + nbytes * 128 / 280e3, dsem=dsem))
        self._reg(nid, reads, writes)
        return nid

    def barrier(self, fn):
        nid = len(self.nodes)
        self.nodes.append(dict(eng="dve", fns=[fn], deps=set(range(nid)), dur=0.1, lat=0.0, dsem=None))
        self.bar = nid
        return nid

    def schedule(self):
        N = len(self.nodes)
        bar = self.bar
        succ = [[] for _ in range(N)]
        for i, n in enumerate(self.nodes):
            if i == bar:
                continue
            for d in n["deps"]:
                succ[d].append(i)
        prio = [0.0] * N
        for i in range(N - 1, -1, -1):
            n = self.nodes[i]
            best = 0.0
            for s in succ[i]:
                if prio[s] > best:
                    best = prio[s]
            if bar is not None and i < bar and prio[bar] > best:
                best = prio[bar]
            prio[i] = n["dur"] + n["lat"] + best
        ndep = [len(n["deps"]) for n in self.nodes]
        fin = [0.0] * N
        depfin = [0.0] * N
        tfree = {e: 0.0 for e in self.ENGS}
        ready = {e: [] for e in self.ENGS}
        for i in range(N):
            if ndep[i] == 0:
                ready[self.nodes[i]["eng"]].append(i)
        order = {e: [] for e in self.ENGS}
        done = 0
        while done < N:
            best = None
            for e in self.ENGS:
                r = ready[e]
                if not r:
                    continue
                tf = tfree[e]
                bi = None
                bk = None
                for i in r:
                    st = depfin[i] if depfin[i] > tf else tf
                    k = (st, -prio[i])
                    if bk is None or k < bk:
                        bk = k
                        bi = i
                if best is None or bk < best[0]:
                    best = (bk, e, bi)
            (st, _), e, i = best
            n = self.nodes[i]
            if not hasattr(self, "why"):
                self.why = {}
                self.stt = {}
                self.lastn = {}
            self.stt[i] = st
            if depfin[i] >= tfree[e] - 1e-9:
                dd = [d for d in n["deps"] if abs(fin[d] - depfin[i]) < 1e-9]
                self.why[i] = ("dep", dd[0] if dd else None)
            else:
                self.why[i] = ("eng", self.lastn.get(e))
            self.lastn[e] = i
            ready[e].remove(i)
            tfree[e] = st + n["dur"]
            fin[i] = st + n["dur"] + n["lat"]
            order[e].append(i)
            done += 1
            if bar is not None and i < bar:
                ndep[bar] -= 1
                if depfin[bar] < fin[i]:
                    depfin[bar] = fin[i]
                if ndep[bar] == 0:
                    ready["dve"].append(bar)
            for s in succ[i]:
                ndep[s] -= 1
                if depfin[s] < fin[i]:
                    depfin[s] = fin[i]
                if ndep[s] == 0:
                    ready[self.nodes[s]["eng"]].append(s)
        import os
        if os.environ.get("MK_SERIAL") == "1":
            order = {e: [i for i in range(N) if self.nodes[i]["eng"] == e] for e in self.ENGS}
        self.order = order
        self.est_total = max(fin) if fin else 0.0
        self.fin = fin

    def emit(self, final_dsems):
        nc = self.nc
        self.schedule()
        nodes = self.nodes
        idx = {}
        rank = {}
        dcount = {}
        for e in self.ENGS:
            k = 0
            for i in self.order[e]:
                ds_ = nodes[i]["dsem"]
                if ds_ is None:
                    k += 1
                    idx[i] = k
                else:
                    dcount[id(ds_)] = dcount.get(id(ds_), 0) + 1
                    rank[i] = dcount[id(ds_)]
        engmap = {"pe": "tensor", "act": "scalar", "dve": "vector", "pool": "gpsimd", "sp": "sync"}
        progs = {}
        for e in self.ENGS:
            seen = {}
            items = []
            for i in self.order[e]:
                n = nodes[i]
                best = {}
                for d in n["deps"]:
                    dn = nodes[d]
                    if dn["dsem"] is None:
                        if e == "pe" and dn["eng"] == "pe":
                            continue
                        s, v = self.sem[dn["eng"]], idx[d]
                    else:
                        s, v = dn["dsem"], 16 * rank[d]
                    k = id(s)
                    if k not in best or best[k][1] < v:
                        best[k] = (s, v)
                ws = []
                for k, (s, v) in best.items():
                    if seen.get(k, 0) >= v:
                        continue
                    seen[k] = v
                    ws.append((s, v))
                if n["dsem"] is None:
                    items.append((ws, n["fns"], self.sem[e], 1))
                else:
                    items.append((ws, n["fns"], n["dsem"], 16))
            progs[e] = items
        fw = [(ds_, 16 * dcount[id(ds_)]) for ds_ in final_dsems if id(ds_) in dcount]
        with nc.Block() as block:
            for ename in self.ENGS:
                items = progs[ename]
                f_ = fw if ename == "sp" else ()

                def body(e, items=items, f_=f_):
                    for ws, fns, s, inc in items:
                        for (ws_s, ws_v) in ws:
                            e.wait_ge(ws_s, ws_v)
                        nf = len(fns)
                        for j, fn in enumerate(fns):
                            ins = fn(e)
                            if j == nf - 1:
                                ins.then_inc(s, inc)
                    for (s, v) in f_:
                        e.wait_ge(s, v)
                getattr(block, engmap[ename])(body)


class Arena:
    def __init__(self, tensor, nbytes):
        self.t = tensor
        self.n = nbytes
        self.off = 0

    def alloc(self, shape, dt):
        esz = 2 if dt == BF16 else 4
        nel = int(np.prod(shape))
        nb = (nel * esz + 31) // 32 * 32
        at = self.off
        self.off += nb
        assert self.off <= self.n, ("arena overflow", self.off, self.n)
        a = self.t[:, at // 4:(at + nb) // 4]
        if dt == BF16:
            a = a.bitcast(BF16)
        a = a[:, 0:nel]
        if len(shape) == 2:
            a = a.rearrange("p (a b) -> p a b", a=shape[0])
        elif len(shape) == 3:
            a = a.rearrange("p (a b c) -> p a b c", a=shape[0], b=shape[1])
        return a


def t_act(F):
    return 0.22 + F / 1300.0


def t_dve(F, slow=1.0):
    return 0.2 + slow * F / 950.0


def t_mm(N, n=1):
    return n * (0.035 + N / 2300.0)


def build_nc():
    nc = bass.Bass("TRN2", target_bir_lowering=False)
    di = lambda n, s: nc.dram_tensor(n, s, F32, kind="ExternalInput").ap()
    do = lambda n, s: nc.dram_tensor(n, s, F32, kind="ExternalOutput").ap()
    xs = di("xs", [NT, 128, D])
    ck = di("ck", [2, 128, 128])
    cv = di("cv", [2, 128, 128])
    sr = di("sr", [2, 4, 128, 128])
    norm1 = di("norm1", [D])
    w_in = di("w_in", [D, INW])
    sinks = di("sinks", [8])
    w_out = di("w_out", [D, D])
    norm2 = di("norm2", [D])
    w_up = di("w_up", [D, DFF])
    w_down = di("w_down", [DFF, D])
    norm_f = di("norm_f", [D])
    cosa_d = di("cosa", [128, NT, 32])
    sina_d = di("sina", [128, NT, 32])
    cosr_d = di("cosr", [128, NT, 64])
    sinr_d = di("sinr", [128, NT, 64])
    dt_d = di("dtab", [128, 2, 4, 128])
    qdec_d = di("qdec", [128, 2, 4, 128])
    kdec_d = di("kdec", [128, 2, 4])
    ident_d = di("ident", [128, 128])
    y_o = do("y", [NT, 128, D])
    nk_o = do("nk", [3, 128, 128])
    nv_o = do("nv", [3, 128, 128])
    ns_o = do("ns", [3, 4, 128, 128])

    with ExitStack() as st:
        P = Prog(nc, st)
        sbt = lambda name, shape, dt: st.enter_context(nc.sbuf_tensor(name, shape, dt))
        pairs = [st.enter_context(nc.psum_tensor(f"pp{i}", [128, 1024], F32)) for i in range(4)]
        _pb = [Buf(f"pair{i}", excl=True) for i in range(4)]
        bankbuf = [_pb[i // 2] for i in range(8)]

        def bank(b):
            return pairs[b // 2][:, (b % 2) * 512:(b % 2) * 512 + 512]

        def bank_bf(b):
            return bank(b).bitcast(BF16)

        PERS = 44 * 1024
        A1 = 151296
        pers_t = sbt("pers", [128, PERS // 4], F32)
        a1_t = sbt("a1", [128, A1 // 4], F32)
        wout_t = sbt("wout", [128, 8, D], BF16)
        pers = Arena(pers_t, PERS)
        mixflat = pers.alloc([NT * 1024], BF16)
        xbuf = [pers.alloc([D], F32) for _ in range(2)]
        ident = pers.alloc([128], BF16)
        ones = pers.alloc([64], BF16)
        stat = pers.alloc([64], F32)
        es = pers.alloc([8], F32)
        es_sel = pers.alloc([4], F32)
        sk = pers.alloc([8], F32)

        def mix_k(m):
            g, t = m // 4, m % 4
            ks = 512 if g < 4 else 128
            v = mixflat[:, g * 4096:g * 4096 + 8 * ks].rearrange("p (k x) -> p k x", k=8)
            return v[:, :, t * 128:(t + 1) * 128]

        def mix_grp(g, k):
            ks = 512 if g < 4 else 128
            return mixflat[:, g * 4096 + k * ks:g * 4096 + (k + 1) * ks]

        a1 = Arena(a1_t, A1)
        w_in_sb = a1.alloc([8, INW], BF16)
        g1b = a1.alloc([D], F32)
        cosa = a1.alloc([NT, 32], F32)
        sina = a1.alloc([NT, 32], F32)
        cosr = a1.alloc([NT, 64], F32)
        sinr = a1.alloc([NT, 64], F32)
        dtab = a1.alloc([2, 4, 128], F32)
        qdec = a1.alloc([2, 4, 128], F32)
        kdec = a1.alloc([2, 4], F32)
        dbl = lambda shape, dt: [a1.alloc(shape, dt) for _ in range(2)]
        a_bf = dbl([D], BF16)
        aT = dbl([8, 128], BF16)
        qrot = dbl([4, 2, 64], BF16)
        krot = dbl([128], BF16)
        QT = dbl([2, 4, 64], BF16)
        qkr = dbl([8, 128], BF16)
        QKrT = dbl([8, 128], BF16)
        QdT = dbl([4, 128], BF16)
        Kd = dbl([4, 128], BF16)
        Vr = dbl([4, 128], BF16)
        sg = dbl([4, 128], F32)
        scm = dbl([4, 128], BF16)
        rmix = dbl([4, 128], BF16)
        PTpc = dbl([1024], BF16)
        PTp = [a[:, 0:512] for a in PTpc]
        PTc = [a[:, 512:1024] for a in PTpc]
        rec = a1.alloc([2, 4, 64], F32)
        sgs = a1.alloc([512], F32)
        tcA = a1.alloc([640], F32)
        tsA = a1.alloc([640], F32)
        tcR = a1.alloc([D], F32)
        tsR = a1.alloc([D], F32)
        junk = a1.alloc([D], BF16)
        krot_f = [a1.alloc([128], F32) for _ in range(2)]
        vout_f = [a1.alloc([128], F32) for _ in range(2)]
        KT = [a1.alloc([128], BF16) for _ in range(3)]
        Vt = [a1.alloc([128], BF16) for _ in range(3)]
        KTc = [a1.alloc([128], BF16) for _ in range(2)]
        Vc = [a1.alloc([128], BF16) for _ in range(2)]
        cstage = [a1.alloc([256], F32) for _ in range(2)]
        cstage_bf = [a1.alloc([128], BF16) for _ in range(2)]
        S = [a1.alloc([4, 128], F32) for _ in range(3)]
        Sbf = [a1.alloc([4, 128], BF16) for _ in range(3)]
        a2 = Arena(a1_t, A1)
        yacc = a2.alloc([NT, D], F32)
        g2b = a2.alloc([D], F32)
        gfb = a2.alloc([D], F32)
        a2_bf = [a2.alloc([D], BF16) for _ in range(2)]
        uT = [a2.alloc([4, 512], BF16) for _ in range(2)]
        urel = [a2.alloc([512], F32) for _ in range(2)]
        ystage = [a2.alloc([D], F32) for _ in range(2)]
        junk2 = a2.alloc([D], BF16)
        ring_up = [a2.alloc([8, 512], BF16) for _ in range(2)]
        ring_dn = [a2.alloc([4, D], BF16) for _ in range(2)]

        B = {}

        def bf(name):
            if name not in B:
                B[name] = Buf(name)
            return B[name]

        def bf2(name):
            return [bf(name + "0"), bf(name + "1")]

        def cload(name, dst, src, eng="sp", nbytes=4096):
            b_ = bf(name)
            P.dma(eng, dst, src, P.dsem("d_" + name), writes=[b_], nbytes=nbytes)
            return b_

        fl2 = lambda a: a.rearrange("p a b -> p (a b)")
        fl3 = lambda a: a.rearrange("p a b c -> p (a b c)")
        g1bb = cload("g1b", g1b, norm1.partition_broadcast(128))
        idb = cload("ident", ident, ident_d, eng="pool", nbytes=512)
        skb = cload("sk", sk, sinks.partition_broadcast(128), nbytes=65536)
        xsem = [P.dsem("dx0"), P.dsem("dx1")]
        xb = bf2("x")
        x0n = P.dma("sp", xbuf[0], xs[0], xsem[0], writes=[xb[0]], nbytes=4096)
        wbuf = {}
        wprev = None
        wfirst = [x0n, g1bb.w, idb.w]
        for gi_, (c0_, c1_) in enumerate(((0, 768), (768, 1792), (1792, INW))):
            b_ = bf(f"win{gi_}")
            wprev = P.dma("pool", w_in_sb[:, :, c0_:c1_], w_in[:, c0_:c1_].rearrange("(k p) n -> p k n", p=128), P.dsem(f"dwin{gi_}"), writes=[b_],
                          extra=[wprev] + (wfirst if gi_ == 0 else []), nbytes=(c1_ - c0_) * 32)
            wbuf[c0_] = b_
        cosab = cload("cosa", fl2(cosa), fl2(cosa_d), nbytes=2176)
        sinab = cload("sina", fl2(sina), fl2(sina_d), nbytes=2176)
        cosrb = cload("cosr", fl2(cosr), fl2(cosr_d), nbytes=4352)
        sinrb = cload("sinr", fl2(sinr), fl2(sinr_d), nbytes=4352)
        dtabb = cload("dtab", fl3(dtab), fl3(dt_d))
        qdecb = cload("qdec", fl3(qdec), fl3(qdec_d))
        kdecb = cload("kdec", fl2(kdec), fl2(kdec_d), nbytes=32)
        wob = bf("wout")
        P.dma("pool", wout_t[:], w_out.rearrange("(k p) n -> p k n", p=128), P.dsem("dwo"), writes=[wob], extra=[wprev], nbytes=16384)
        onb = bf("ones")
        P.op("dve", lambda e: e.memset(ones, 1.0), writes=[onb], dur=0.1)
        esb = bf("es")
        P.op("act", lambda e: e.activation(out=es, in_=sk, func=AF.Exp), reads=[skb], writes=[esb], dur=0.3)
        essb = bf("es_sel")
        P.op("dve", lambda e: e.tensor_copy(out=es_sel[0:64, :], in_=es[0:64, 0:4]), reads=[esb], writes=[essb], dur=0.1)
        P.op("dve", lambda e: e.tensor_copy(out=es_sel[64:128, :], in_=es[64:128, 4:8]), reads=[esb], writes=[essb], dur=0.1)

        stat_n = [0]

        def stat_col(n):
            c0 = stat_n[0]
            stat_n[0] += n
            assert stat_n[0] <= 60
            return stat[:, c0:c0 + n], bf(f"stat{c0}")

        st_n1 = [stat_col(1) for _ in range(2)]
        st_g = [stat_col(4) for _ in range(2)]
        st_n2 = [stat_col(1) for _ in range(2)]
        st_nf = [stat_col(1) for _ in range(2)]

        _jb = {}

        def jb_of(ap):
            k = id(ap)
            if k not in _jb:
                _jb[k] = Buf("junk")
            return _jb[k]

        def rms(src_ap, src_bufs, gain_ap, gain_bufs, out_ap, out_bufs, stc, junk_ap, F=1024):
            col, cb = stc
            fns = [lambda e: e.activation(out=junk_ap, in_=src_ap, func=AF.Square, accum_out=col),
                   lambda e: e.activation(out=col, in_=col, func=AF.Ln, scale=1.0 / D, bias=EPS),
                   lambda e: e.activation(out=col, in_=col, func=AF.Exp, scale=-0.5)]
            P.op("act", fns[0], reads=src_bufs, writes=[cb, jb_of(junk_ap)], dur=t_act(F))
            P.op("act", fns[1], writes=[cb], dur=0.25)
            P.op("act", fns[2], writes=[cb], dur=0.25)
            P.op("dve", lambda e: e.scalar_tensor_tensor(out=out_ap, in0=src_ap, scalar=col, in1=gain_ap, op0=ALU.mult, op1=ALU.mult),
                 reads=list(src_bufs) + [cb] + list(gain_bufs), writes=out_bufs, dur=t_dve(F))

        TRB = 2

        def transposes(srcs, src_bufs, dst_ap, dst_bufs, trb=TRB):
            n = len(srcs)
            tb = bank_bf(trb)
            fns = [(lambda e, i=i, s=s: e.transpose(out=tb[:, i * 128:(i + 1) * 128], in_=s, identity=ident)) for i, s in enumerate(srcs)]
            P.op("pe", fns, reads=list(src_bufs) + [idb], writes=[bankbuf[trb]], dur=t_mm(128, n))
            src = tb[:, 0:n * 128].rearrange("p (a b) -> p a b", a=n)
            P.op("act", lambda e: e.copy(out=dst_ap, in_=src), reads=[bankbuf[trb]], writes=dst_bufs, dur=t_act(n * 128))

        csb = bf2("cst")
        Sb = [bf("S0"), bf("S1"), bf("S2")]
        Sbfb = [bf("Sbf0"), bf("Sbf1"), bf("Sbf2")]
        d_misc = P.dsem("dmisc")
        KTcb, Vcb, csbf_b = bf2("KTc"), bf2("Vc"), bf2("csbf")
        csvb = bf2("cstv")
        def prep_sample():
            for b in range(2):
                P.dma("sp", cstage[b][:, 0:128], ck[b], P.dsem(f"dck{b}"), writes=[csb[b]], nbytes=512)
                P.dma("sp", cstage[b][:, 128:256], cv[b], P.dsem(f"dcv{b}"), writes=[csvb[b]], nbytes=512)
                P.dma("sp", S[1 + b], sr[b].rearrange("h d e -> d h e"), P.dsem(f"dsr{b}"), writes=[Sb[1 + b]], nbytes=2048)
                P.op("dve", lambda e, b=b: e.tensor_copy(out=cstage_bf[b], in_=cstage[b][:, 0:128]), reads=[csb[b]], writes=[csbf_b[b]], dur=0.3)
                P.op("dve", lambda e, b=b: e.tensor_copy(out=Vc[b], in_=cstage[b][:, 128:256]), reads=[csvb[b]], writes=[Vcb[b]], dur=0.3)
                transposes([cstage_bf[b]], [csbf_b[b]], KTc[b].unsqueeze(1), [KTcb[b]])
                P.op("act", lambda e, b=b: e.copy(out=Sbf[1 + b], in_=S[1 + b]), reads=[Sb[1 + b]], writes=[Sbfb[1 + b]], dur=t_act(512))
                P.dma("sp", nk_o[1 + b, 0:64, :], ck[b, 64:128, :], d_misc, nbytes=512)
                P.dma("sp", nv_o[1 + b, 0:64, :], cv[b, 64:128, :], d_misc, nbytes=512)

        abf_b, aT_b = bf2("a_bf"), bf2("aT")
        tcAb, tsAb, tcRb, tsRb = bf("tcA"), bf("tsA"), bf("tcR"), bf("tsR")
        qrb, krb, QTb = bf2("qrot"), bf2("krot"), bf2("QT")
        KTb, Vtb = [bf(f"KT{i}") for i in range(3)], [bf(f"V{i}") for i in range(3)]
        PTcb, PTpb = bf2("PTc"), bf2("PTp")
        recb, sgsb = bf("rec"), bf("sgs")
        qkrb, QKrTb, QdTb, Kdb, Vrb, sgb, scmb, rmixb = (bf2(n) for n in ("qkr", "QKrT", "QdT", "Kd", "Vr", "sg", "scm", "rmix"))
        mixb = [bf(f"mix{m}") for m in range(NT)]
        krfb, vofb = bf2("krf"), bf2("vof")

        def front(m):
            sample = (m == NT - 1)
            ty = 1 if sample else 0
            xi = m % 2
            pi = m % 2
            if m > 0:
                P.dma("sp", xbuf[xi], xs[m], xsem[xi], writes=[xb[xi]], nbytes=4096)
            rms(xbuf[xi], [xb[xi]], g1b, [g1bb], a_bf[pi], [abf_b[pi]], st_n1[pi], junk)
            transposes([a_bf[pi][:, k * 128:(k + 1) * 128] for k in range(8)], [abf_b[pi]], aT[pi], [aT_b[pi]], trb=CFG['trb'])

            def proj(col0, ncols, pr, pi=pi):
                fns = []
                nb = (ncols + 511) // 512
                tot = 0.0
                for k in range(8):
                    for n in range(nb):
                        c0 = col0 + n * 512
                        w_ = min(512, col0 + ncols - c0)
                        tot += t_mm(w_)
                        fns.append(lambda e, k=k, n=n, c0=c0, w_=w_: e.matmul(
                            bank(2 * pr + n)[:, 0:w_], lhsT=aT[pi][:, k, :], rhs=w_in_sb[:, k, c0:c0 + w_],
                            start=(k == 0), stop=(k == 7)))
                P.op("pe", fns, reads=[aT_b[pi], wbuf[col0]], writes=[bankbuf[2 * pr + n] for n in range(nb)], dur=tot)

            PG0, PG1, PG2 = CFG['pG0'], CFG['pG1'], CFG['pG2']
            proj(0, 768, PG0)
            z0 = pairs[PG0]
            zall = z0[:, 0:640].rearrange("p (h two r) -> p h two r", h=10, two=2)
            tca = tcA.rearrange("p (h two r) -> p h two r", h=10, two=2)
            tsa = tsA.rearrange("p (h two r) -> p h two r", h=10, two=2)
            csb_ = cosa[:, m, :].unsqueeze(1).unsqueeze(1).broadcast_to([128, 10, 2, 32])
            snb_ = sina[:, m, :].unsqueeze(1).unsqueeze(1).broadcast_to([128, 10, 2, 32])
            P.op("dve", lambda e, csb_=csb_: e.tensor_tensor(out=tca, in0=zall, in1=csb_, op=ALU.mult),
                 reads=[bankbuf[2 * PG0], cosab], writes=[tcAb], dur=t_dve(640))
            P.op("dve", lambda e, snb_=snb_: e.tensor_tensor(out=tsa, in0=zall, in1=snb_, op=ALU.mult),
                 reads=[bankbuf[2 * PG0], sinab], writes=[tsAb], dur=t_dve(640))
            cur = m % 3
            prv = (m + 2) % 3
            P.op("act", lambda e, cur=cur: e.copy(out=Vt[cur], in_=z0[:, 640:768]), reads=[bankbuf[2 * PG0]], writes=[Vtb[cur]], dur=t_act(128))
            if m >= NT - 2:
                j = m - (NT - 2)
                P.op("act", lambda e, j=j: e.copy(out=vout_f[j], in_=z0[:, 640:768]), reads=[bankbuf[2 * PG0]], writes=[vofb[j]], dur=t_act(128))
            tcq = tcA[:, 0:512].rearrange("p (g t two r) -> p g t two r", g=2, t=4, two=2)
            tsq = tsA[:, 0:512].rearrange("p (g t two r) -> p g t two r", g=2, t=4, two=2)
            qro = qrot[pi].rearrange("p t g (two r) -> p g t two r", two=2)
            tck = tcA[:, 512:640].rearrange("p (h two r) -> p h two r", h=2, two=2)
            tsk = tsA[:, 512:640].rearrange("p (h two r) -> p h two r", h=2, two=2)
            kro = krot[pi].rearrange("p (h two r) -> p h two r", h=2, two=2)
            P.op("dve", lambda e, qro=qro: e.tensor_tensor(out=qro[:, :, :, 0, :], in0=tcq[:, :, :, 0, :], in1=tsq[:, :, :, 1, :], op=ALU.subtract),
                 reads=[tcAb, tsAb], writes=[qrb[pi]], dur=t_dve(256))
            P.op("dve", lambda e, qro=qro: e.tensor_tensor(out=qro[:, :, :, 1, :], in0=tcq[:, :, :, 1, :], in1=tsq[:, :, :, 0, :], op=ALU.add),
                 reads=[tcAb, tsAb], writes=[qrb[pi]], dur=t_dve(256))
            P.op("dve", lambda e, kro=kro: e.tensor_tensor(out=kro[:, :, 0, :], in0=tck[:, :, 0, :], in1=tsk[:, :, 1, :], op=ALU.subtract),
                 reads=[tcAb, tsAb], writes=[krb[pi]], dur=t_dve(64))
            P.op("dve", lambda e, kro=kro: e.tensor_tensor(out=kro[:, :, 1, :], in0=tck[:, :, 1, :], in1=tsk[:, :, 0, :], op=ALU.add),
                 reads=[tcAb, tsAb], writes=[krb[pi]], dur=t_dve(64))
            if m >= NT - 2:
                j = m - (NT - 2)
                krf = krot_f[j].rearrange("p (h two r) -> p h two r", h=2, two=2)
                P.op("dve", lambda e, krf=krf: e.tensor_tensor(out=krf[:, :, 0, :], in0=tck[:, :, 0, :], in1=tsk[:, :, 1, :], op=ALU.subtract),
                     reads=[tcAb, tsAb], writes=[krfb[j]], dur=t_dve(64))
                P.op("dve", lambda e, krf=krf: e.tensor_tensor(out=krf[:, :, 1, :], in0=tck[:, :, 1, :], in1=tsk[:, :, 0, :], op=ALU.add),
                     reads=[tcAb, tsAb], writes=[krfb[j]], dur=t_dve(64))
                if not sample:
                    P.dma("sp", nk_o[0], krot_f[j], d_misc, reads=[krfb[j]], nbytes=512)
                    P.dma("sp", nv_o[0], vout_f[j], d_misc, reads=[vofb[j]], nbytes=512)
                else:
                    for b in range(2):
                        P.dma("sp", nk_o[1 + b, 64:128, :], krot_f[j][64 * b:64 * b + 64, :], d_misc, reads=[krfb[j]], nbytes=512)
                        P.dma("sp", nv_o[1 + b, 64:128, :], vout_f[j][64 * b:64 * b + 64, :], d_misc, reads=[vofb[j]], nbytes=512)
            TRQ = CFG['trbQ']
            tb = bank_bf(TRQ)
            fns = [(lambda e, t=t, pi=pi: e.transpose(out=tb[:, t * 128:(t + 1) * 128], in_=qrot[pi][:, t, :, :].rearrange("p g d -> p (g d)"), identity=ident)) for t in range(4)]
            fns.append(lambda e, pi=pi: e.transpose(out=tb[:, 512:640], in_=krot[pi], identity=ident))
            P.op("pe", fns, reads=[qrb[pi], krb[pi], idb], writes=[bankbuf[TRQ]], dur=t_mm(128, 5))
            fns = [lambda e, pi=pi: e.copy(out=QT[pi], in_=tb[:, 0:512].rearrange("p (t c q) -> p c t q", t=4, c=2)),
                   lambda e, cur=cur: e.copy(out=KT[cur], in_=tb[:, 512:640])]
            P.op("act", fns, reads=[bankbuf[TRQ]], writes=[QTb[pi], KTb[cur]], dur=t_act(512) + t_act(128))

            proj(768, 1024, PG1)
            z1 = pairs[PG1]
            zr = z1[:, :].rearrange("p (h two r) -> p h two r", h=8, two=2)
            tcr = tcR.rearrange("p (h two r) -> p h two r", h=8, two=2)
            tsr = tsR.rearrange("p (h two r) -> p h two r", h=8, two=2)
            qko = qkr[pi].rearrange("p h (two r) -> p h two r", two=2)
            csr_ = cosr[:, m, :].unsqueeze(1).unsqueeze(1).broadcast_to([128, 8, 2, 64])
            snr_ = sinr[:, m, :].unsqueeze(1).unsqueeze(1).broadcast_to([128, 8, 2, 64])
            P.op("dve", lambda e, csr_=csr_: e.tensor_tensor(out=tcr, in0=zr, in1=csr_, op=ALU.mult), reads=[bankbuf[2 * PG1], cosrb], writes=[tcRb], dur=t_dve(1024))
            P.op("dve", lambda e, snr_=snr_: e.tensor_tensor(out=tsr, in0=zr, in1=snr_, op=ALU.mult), reads=[bankbuf[2 * PG1], sinrb], writes=[tsRb], dur=t_dve(1024))
            P.op("dve", lambda e, qko=qko: e.tensor_tensor(out=qko[:, :, 0, :], in0=tcr[:, :, 0, :], in1=tsr[:, :, 1, :], op=ALU.subtract), reads=[tcRb, tsRb], writes=[qkrb[pi]], dur=t_dve(512))
            P.op("dve", lambda e, qko=qko: e.tensor_tensor(out=qko[:, :, 1, :], in0=tcr[:, :, 1, :], in1=tsr[:, :, 0, :], op=ALU.add), reads=[tcRb, tsRb], writes=[qkrb[pi]], dur=t_dve(512))
            transposes([qkr[pi][:, h, :] for h in range(8)], [qkrb[pi]], QKrT[pi], [QKrTb[pi]], trb=CFG['trbR'])
            P.op("dve", lambda e, ty=ty, pi=pi: e.tensor_tensor(out=QdT[pi], in0=QKrT[pi][:, 0:4, :], in1=qdec[:, ty, :, :], op=ALU.mult),
                 reads=[QKrTb[pi], qdecb], writes=[QdTb[pi]], dur=t_dve(512))
            P.op("dve", lambda e, ty=ty, pi=pi: e.tensor_tensor(out=Kd[pi], in0=qkr[pi][:, 4:8, :], in1=kdec[:, ty, :].unsqueeze(2).broadcast_to([128, 4, 128]), op=ALU.mult),
                 reads=[qkrb[pi], kdecb], writes=[Kdb[pi]], dur=t_dve(512))
            proj(1792, 1024, PG2)
            P.op("act", lambda e, pi=pi: e.copy(out=Vr[pi].rearrange("p a b -> p (a b)"), in_=pairs[PG2][:, 0:512]), reads=[bankbuf[2 * PG2]], writes=[Vrb[pi]], dur=t_act(512))
            gps = pairs[PG2][:, 512:1024]
            P.op("act", lambda e: e.activation(out=sgs, in_=gps, func=AF.Exp, scale=-1.0), reads=[bankbuf[2 * PG2]], writes=[sgsb], dur=t_act(512))
            P.op("act", lambda e: e.activation(out=sgs, in_=sgs, func=AF.Ln, bias=1.0), writes=[sgsb], dur=t_act(512))
            P.op("act", lambda e: e.activation(out=sgs, in_=sgs, func=AF.Exp, scale=-1.0), writes=[sgsb], dur=t_act(512))
            P.op("dve", lambda e, pi=pi: e.tensor_tensor(out=sg[pi].rearrange("p a b -> p (a b)"), in0=gps, in1=sgs, op=ALU.mult),
                 reads=[bankbuf[2 * PG2], sgsb], writes=[sgb[pi]], dur=t_dve(512))

        def back(m):
            sample = (m == NT - 1)
            ty = 1 if sample else 0
            pi = m % 2
            cur = m % 3
            prv = (m + 2) % 3
            pS, pO = CFG['backpair']
            SBc, SBp, OB_, DB_ = 2 * pS + 1, 2 * pS, 2 * pO, 2 * pO + 1
            has_prev = sample or m > 0
            QTf = QT[pi].rearrange("p c t q -> p (c t q)")
            for g in range(2):
                r0 = 64 * g
                fns = [lambda e, r0=r0, cur=cur, QTf=QTf: e.matmul(bank(SBc), lhsT=KT[cur][r0:r0 + 64, :], rhs=QTf[r0:r0 + 64, :], start=True, stop=True)]
                rd = [KTb[cur], QTb[pi]]
                if has_prev:
                    if not sample:
                        fns.append(lambda e, r0=r0, prv=prv, QTf=QTf: e.matmul(bank(SBp), lhsT=KT[prv][r0:r0 + 64, :], rhs=QTf[r0:r0 + 64, :], start=True, stop=True))
                        rd.append(KTb[prv])
                    else:
                        fns += [(lambda e, b=b, r0=r0, QTf=QTf: e.matmul(bank(SBp)[:, 256 * b:256 * b + 256], lhsT=KTc[b][r0:r0 + 64, :], rhs=QTf[r0:r0 + 64, 256 * b:256 * b + 256],
                                                                       start=True, stop=True)) for b in range(2)]
                        rd += KTcb
                P.op("pe", fns, reads=rd, writes=[bankbuf[SBc], bankbuf[SBp]], dur=t_mm(512, len(fns)))
                if has_prev:
                    P.op("act", lambda e, g=g: e.activation(out=PTpc[g], in_=pairs[pS][:, :], func=AF.Exp, scale=0.125),
                         reads=[bankbuf[SBc], bankbuf[SBp]], writes=[PTcb[g], PTpb[g]], dur=t_act(1024))
                else:
                    P.op("act", lambda e, g=g: e.activation(out=PTc[g], in_=bank(SBc), func=AF.Exp, scale=0.125),
                         reads=[bankbuf[SBc]], writes=[PTcb[g]], dur=t_act(512))
                fns = []
                for c in range(2):
                    contrib = []
                    if has_prev:
                        if sample:
                            contrib.append((Vc[c], PTp[g], 0, 128))
                        else:
                            contrib.append((Vt[prv], PTp[g], 0, 128) if c == 0 else (Vt[prv], PTp[g], 64, 128))
                    if sample:
                        contrib.append((Vt[cur], PTc[g], 64 * c, 64 * c + 64))
                    else:
                        contrib.append((Vt[cur], PTc[g], 0, 64) if c == 0 else (Vt[cur], PTc[g], 0, 128))
                    for (dbk_, lsel) in ((OB_, 0), (DB_, 1)):
                        for i, (vv, pt, k0, k1) in enumerate(contrib):
                            lhs = vv[k0:k1, r0:r0 + 64] if lsel == 0 else ones[k0:k1, :]
                            fns.append(lambda e, dbk_=dbk_, lhs=lhs, pt=pt, k0=k0, k1=k1, c=c, i=i, nctr=len(contrib), r0=r0: e.matmul(
                                bank(dbk_)[r0:r0 + 64, 256 * c:256 * c + 256], lhsT=lhs, rhs=pt[k0:k1, 256 * c:256 * c + 256],
                                start=(i == 0), stop=(i == nctr - 1)))
                rd = [Vtb[cur], PTcb[g], onb] + ([PTpb[g]] + (Vcb if sample else [Vtb[prv]]) if has_prev else [])
                P.op("pe", fns, reads=rd, writes=[bankbuf[OB_], bankbuf[DB_]], dur=t_mm(256, len(fns)))
            dbv = bank(DB_).rearrange("p (c t q) -> p c t q", c=2, t=4)
            obv = bank(OB_).rearrange("p (c t q) -> p c t q", c=2, t=4)
            fns = [(lambda e, t=t: e.activation(out=rec[:, :, t, :], in_=dbv[:, :, t, :], func=AF.Ln, bias=es_sel[:, t:t + 1])) for t in range(4)]
            P.op("act", fns, reads=[bankbuf[DB_], essb], writes=[recb], dur=4 * t_act(128))
            P.op("act", lambda e: e.activation(out=rec.rearrange("p c t q -> p (c t q)"), in_=rec.rearrange("p c t q -> p (c t q)"), func=AF.Exp, scale=-1.0),
                 writes=[recb], dur=t_act(512))
            mo = mix_k(m)[:, 0:4, :].rearrange("p t (c q) -> p c t q", c=2)
            P.op("dve", lambda e, mo=mo: e.tensor_tensor(out=mo, in0=obv, in1=rec, op=ALU.mult),
                 reads=[bankbuf[OB_], recb], writes=[mixb[m]], dur=t_dve(512))

            SC_, OR_, UB_, RT_ = CFG.get('ret', (5, 6, 7, 4))
            scv = bank(SC_).rearrange("p (a b) -> p a b", a=4)
            fns = [(lambda e, h=h, pi=pi: e.matmul(scv[:, h, :], lhsT=QKrT[pi][:, 4 + h, :], rhs=QKrT[pi][:, h, :], start=True, stop=True)) for h in range(4)]
            P.op("pe", fns, reads=[QKrTb[pi]], writes=[bankbuf[SC_]], dur=t_mm(128, 4))
            P.op("dve", lambda e, ty=ty, pi=pi: e.tensor_tensor(out=scm[pi], in0=scv, in1=dtab[:, ty, :, :], op=ALU.mult),
                 reads=[bankbuf[SC_], dtabb], writes=[scmb[pi]], dur=t_dve(512))
            orv = bank(OR_).rearrange("p (a b) -> p a b", a=4)
            fns = []
            rd = [scmb[pi], Vrb[pi]]
            if not sample:
                cross = (m > 0)
                for h in range(4):
                    fns.append(lambda e, h=h, cross=cross, pi=pi: e.matmul(orv[:, h, :], lhsT=scm[pi][:, h, :], rhs=Vr[pi][:, h, :], start=True, stop=not cross))
                    if cross:
                        fns.append(lambda e, h=h, pi=pi: e.matmul(orv[:, h, :], lhsT=QdT[pi][:, h, :], rhs=Sbf[0][:, h, :], start=False, stop=True))
                if cross:
                    rd += [QdTb[pi], Sbfb[0]]
            else:
                for h in range(4):
                    fns.append(lambda e, h=h, pi=pi: e.matmul(orv[:, h, :], lhsT=scm[pi][:, h, :], rhs=Vr[pi][:, h, :], start=True, stop=True))
                    for b in range(2):
                        fns.append(lambda e, h=h, b=b, pi=pi: e.matmul(orv[64 * b:64 * b + 64, h, :], lhsT=QdT[pi][:, h, 64 * b:64 * b + 64], rhs=Sbf[1 + b][:, h, :],
                                                                      start=False, stop=False, skip_group_check=True))
                rd += [QdTb[pi], Sbfb[1], Sbfb[2]]
            P.op("pe", fns, reads=rd, writes=[bankbuf[OR_]], dur=t_mm(128, len(fns)))
            gcol, gcb = st_g[pi]
            fns = [(lambda e, h=h, gcol=gcol: e.activation(out=junk[:, h * 128:(h + 1) * 128], in_=orv[:, h, :], func=AF.Square, accum_out=gcol[:, h:h + 1])) for h in range(4)]
            P.op("act", fns, reads=[bankbuf[OR_]], writes=[gcb, jb_of(junk)], dur=4 * t_act(128))
            P.op("act", lambda e, gcol=gcol: e.activation(out=gcol, in_=gcol, func=AF.Ln, scale=1.0 / 128, bias=EPS), writes=[gcb], dur=0.25)
            P.op("act", lambda e, gcol=gcol: e.activation(out=gcol, in_=gcol, func=AF.Exp, scale=-0.5), writes=[gcb], dur=0.25)
            fns = [(lambda e, h=h, gcol=gcol, pi=pi: e.scalar_tensor_tensor(out=rmix[pi][:, h, :], in0=orv[:, h, :], scalar=gcol[:, h:h + 1], in1=sg[pi][:, h, :],
                                                                           op0=ALU.mult, op1=ALU.mult)) for h in range(4)]
            P.op("dve", fns, reads=[bankbuf[OR_], gcb, sgb[pi]], writes=[rmixb[pi]], dur=4 * t_dve(128))
            transposes([rmix[pi][:, h, :] for h in range(4)], [rmixb[pi]], mix_k(m)[:, 4:8, :], [mixb[m]], trb=RT_)
            if not sample:
                ubv = bank(UB_).rearrange("p (a b) -> p a b", a=4)
                fns = [(lambda e, h=h, pi=pi: e.matmul(ubv[:, h, :], lhsT=Kd[pi][:, h, :], rhs=Vr[pi][:, h, :], start=True, stop=True)) for h in range(4)]
                P.op("pe", fns, reads=[Kdb[pi], Vrb[pi]], writes=[bankbuf[UB_]], dur=t_mm(128, 4))
                if m == 0:
                    P.op("dve", lambda e: e.tensor_copy(out=S[0], in_=ubv), reads=[bankbuf[UB_]], writes=[Sb[0]], dur=t_dve(512))
                else:
                    fns = [(lambda e, h=h: e.scalar_tensor_tensor(out=S[0][:, h, :], in0=S[0][:, h, :], scalar=float(GAM[h] ** 128), in1=ubv[:, h, :],
                                                                 op0=ALU.mult, op1=ALU.add)) for h in range(4)]
                    P.op("dve", fns, reads=[bankbuf[UB_]], writes=[Sb[0]], dur=4 * t_dve(128))
                if m < NT - 2:
                    P.op("act", lambda e: e.copy(out=Sbf[0], in_=S[0]), reads=[Sb[0]], writes=[Sbfb[0]], dur=t_act(512))
                else:
                    P.dma("sp", ns_o[0].rearrange("h d e -> d h e"), S[0], d_misc, reads=[Sb[0]], nbytes=2048)
            else:
                for b in range(2):
                    ubk = UB_ if b == 0 else SC_
                    ubv = bank(ubk).rearrange("p (a b) -> p a b", a=4)
                    fns = [(lambda e, h=h, b=b, ubv=ubv, pi=pi: e.matmul(ubv[:, h, :], lhsT=Kd[pi][64 * b:64 * b + 64, h, :], rhs=Vr[pi][64 * b:64 * b + 64, h, :], start=True, stop=True)) for h in range(4)]
                    P.op("pe", fns, reads=[Kdb[pi], Vrb[pi]], writes=[bankbuf[ubk]], dur=t_mm(128, 4))
                    fns = [(lambda e, h=h, b=b, ubv=ubv: e.scalar_tensor_tensor(out=S[1 + b][:, h, :], in0=S[1 + b][:, h, :], scalar=float(GAM[h] ** 64), in1=ubv[:, h, :],
                                                                               op0=ALU.mult, op1=ALU.add)) for h in range(4)]
                    P.op("dve", fns, reads=[bankbuf[ubk], Sbfb[1 + b]], writes=[Sb[1 + b]], dur=4 * t_dve(128))
                    P.dma("sp", ns_o[1 + b].rearrange("h d e -> d h e"), S[1 + b], d_misc, reads=[Sb[1 + b]], nbytes=2048)

        front(0)
        for m in range(NT):
            if m + 1 == NT - 2:
                prep_sample()
            if CFG['backfirst']:
                back(m)
                if m + 1 < NT:
                    front(m + 1)
            else:
                if m + 1 < NT:
                    front(m + 1)
                back(m)

        P.barrier(lambda e: e.memset(stat[:, 60:61], 0.0))
        yb = [bf(f"yacc{m}") for m in range(NT)]
        g2bb = cload("g2b", g2b, norm2.partition_broadcast(128))
        gfbb = cload("gfb", gfb, norm_f.partition_broadcast(128))
        ringub, ringdb = bf2("ringu"), bf2("ringd")
        rsu = [P.dsem("dru0"), P.dsem("dru1")]
        rsd = [P.dsem("drd0"), P.dsem("drd1")]

        def load_chunk(c):
            s = c % 2
            P.dma("pool", ring_up[s], w_up[:, c * 512:(c + 1) * 512].rearrange("(k p) f -> p k f", p=128), rsu[s], writes=[ringub[s]], nbytes=16384)
            P.dma("pool", ring_dn[s], w_down[c * 512:(c + 1) * 512, :].rearrange("(j p) n -> p j n", p=128), rsd[s], writes=[ringdb[s]], nbytes=16384)

        load_chunk(0)
        load_chunk(1)
        a2b = bf2("a2_bf")
        for m in range(NT):
            xi = m % 2
            pi = m % 2
            P.dma("sp", xbuf[xi], xs[m], xsem[xi], writes=[xb[xi]], nbytes=4096)
            pr = m % 2
            mk = mix_k(m)
            fns = []
            for k in range(8):
                for n in range(2):
                    fns.append(lambda e, k=k, n=n, mk=mk, pr=pr: e.matmul(bank(2 * pr + n), lhsT=mk[:, k, :], rhs=wout_t[:, k, n * 512:(n + 1) * 512],
                                                                         start=(k == 0), stop=(k == 7)))
            P.op("pe", fns, reads=[mixb[m], wob], writes=[bankbuf[2 * pr], bankbuf[2 * pr + 1]], dur=t_mm(512, 16))
            P.op("dve", lambda e, m=m, pr=pr, xi=xi: e.tensor_tensor(out=yacc[:, m, :], in0=pairs[pr][:, :], in1=xbuf[xi], op=ALU.add),
                 reads=[bankbuf[2 * pr], bankbuf[2 * pr + 1], xb[xi]], writes=[yb[m]], dur=t_dve(1024))
            rms(yacc[:, m, :], [yb[m]], g2b, [g2bb], a2_bf[pi], [a2b[pi]], st_n2[pi], junk2)
            transposes([a2_bf[pi][:, k * 128:(k + 1) * 128] for k in range(8)], [a2b[pi]], mk, [mixb[m]], trb=4)

        groups = [(0, 4), (4, 4), (8, 4), (12, 4), (16, 1)]
        uTb, urb = bf2("uT"), bf2("ur")
        ub_rot = up_rot = dn_rot = 0
        for c in range(NCH):
            s = c % 2
            for gi, (m0, nt_) in enumerate(groups):
                ntok = nt_ * 128
                ui = ub_rot % 2
                ub_rot += 1
                for j in range(4):
                    bk = 4 + (up_rot % 4)
                    up_rot += 1
                    fns = [(lambda e, k=k, j=j, bk=bk, s=s, ntok=ntok, gi=gi: e.matmul(bank(bk)[:, 0:ntok], lhsT=ring_up[s][:, k, j * 128:(j + 1) * 128],
                                                                                  rhs=mix_grp(gi, k), start=(k == 0), stop=(k == 7))) for k in range(8)]
                    P.op("pe", fns, reads=[mixb[m] for m in range(m0, m0 + nt_)] + [ringub[s]], writes=[bankbuf[bk]], dur=t_mm(ntok, 8))
                    ri = up_rot % 2
                    P.op("act", lambda e, bk=bk, ri=ri, ntok=ntok: e.activation(out=urel[ri][:, 0:ntok], in_=bank(bk)[:, 0:ntok], func=AF.Relu),
                         reads=[bankbuf[bk]], writes=[urb[ri]], dur=t_act(ntok))
                    P.op("act", lambda e, ri=ri, ui=ui, j=j, ntok=ntok: e.activation(out=uT[ui][:, j, 0:ntok], in_=urel[ri][:, 0:ntok], func=AF.Square),
                         reads=[urb[ri]], writes=[uTb[ui]], dur=t_act(ntok))
                for t in range(nt_):
                    m = m0 + t
                    pr = dn_rot % 2
                    dn_rot += 1
                    fns = []
                    for j in range(4):
                        for n in range(2):
                            fns.append(lambda e, j=j, n=n, t=t, pr=pr, ui=ui, s=s: e.matmul(bank(2 * pr + n), lhsT=uT[ui][:, j, t * 128:(t + 1) * 128],
                                                                                           rhs=ring_dn[s][:, j, n * 512:(n + 1) * 512], start=(j == 0), stop=(j == 3)))
                    P.op("pe", fns, reads=[uTb[ui], ringdb[s]], writes=[bankbuf[2 * pr], bankbuf[2 * pr + 1]], dur=t_mm(512, 8))
                    P.op("dve", lambda e, m=m, pr=pr: e.tensor_tensor(out=yacc[:, m, :], in0=pairs[pr][:, :], in1=yacc[:, m, :], op=ALU.add),
                         reads=[bankbuf[2 * pr], bankbuf[2 * pr + 1]], writes=[yb[m]], dur=t_dve(1024))
            if c + 2 < NCH:
                load_chunk(c + 2)

        ysem = [P.dsem("dy0"), P.dsem("dy1")]
        ysb = bf2("ys")
        for m in range(NT):
            yi = m % 2
            rms(yacc[:, m, :], [yb[m]], gfb, [gfbb], ystage[yi], [ysb[yi]], st_nf[yi], junk2)
            P.dma("sp", y_o[m], ystage[yi], ysem[yi], reads=[ysb[yi]], nbytes=4096)
        P.emit([ysem[0], ysem[1], d_misc])
    return nc


def _host_consts():
    pos = np.zeros((128, NT), np.float64)
    for m in range(16):
        pos[:, m] = 128 * m + np.arange(128)
    pos[:, 16] = 2048 + (np.arange(128) % 64)
    pos32 = pos.astype(np.float32)
    try:
        import jax
        import jax.numpy as jnp
        with jax.default_device(jax.devices("cpu")[0]):
            inv_a = np.asarray(1.0 / (10000.0 ** (jnp.arange(0, 64, 2, dtype=jnp.float32) / 64)), dtype=np.float32)
            inv_r = np.asarray(1.0 / (10000.0 ** jnp.linspace(0.0, 1.0, 64, dtype=jnp.float32)), dtype=np.float32)
    except Exception:
        inv_a = (1.0 / (np.float32(10000.0) ** (np.arange(0, 64, 2, dtype=np.float32) / np.float32(64)))).astype(np.float32)
        inv_r = (1.0 / (np.float32(10000.0) ** np.linspace(0.0, 1.0, 64, dtype=np.float32))).astype(np.float32)
    ang_a = (pos32[:, :, None] * inv_a[None, None, :]).astype(np.float32).astype(np.float64)
    ang_r = (pos32[:, :, None] * inv_r[None, None, :]).astype(np.float32).astype(np.float64)
    cosa, sina = np.cos(ang_a).astype(np.float32), np.sin(ang_a).astype(np.float32)
    cosr, sinr = np.cos(ang_r).astype(np.float32), np.sin(ang_r).astype(np.float32)
    logg = np.log1p(-np.exp2(-5.0 - np.arange(4, dtype=np.float64)))
    sc = 128.0 ** -0.5
    j = np.arange(128)[:, None]
    i = np.arange(128)[None, :]
    dtab = np.zeros((128, 2, 4, 128), np.float64)
    qdec = np.zeros((128, 2, 4, 128), np.float64)
    kdec = np.zeros((128, 2, 4), np.float64)
    for h in range(4):
        d0 = np.where(i >= j, np.exp(logg[h] * np.maximum(i - j, 0)), 0.0)
        dtab[:, 0, h, :] = sc * d0
        same = (i // 64) == (j // 64)
        dtab[:, 1, h, :] = sc * np.where(same, d0, 0.0)
        qdec[:, 0, h, :] = np.exp(logg[h] * (np.arange(128) + 1.0))[None, :]
        qdec[:, 1, h, :] = np.exp(logg[h] * ((np.arange(128) % 64) + 1.0))[None, :]
        kdec[:, 0, h] = sc * np.exp(logg[h] * (127.0 - np.arange(128)))
        kdec[:, 1, h] = sc * np.exp(logg[h] * (63.0 - (np.arange(128) % 64)))
    return dict(cosa=cosa, sina=sina, cosr=cosr, sinr=sinr, dtab=dtab.astype(np.float32), qdec=qdec.astype(np.float32),
                kdec=kdec.astype(np.float32), ident=np.eye(128, dtype=np.float32))


_NC_CACHE = {}


def kernel(x_prompt, x_sample, cache_k, cache_v, state_ret, norm1, w_in, sinks, w_out, norm2, w_up, w_down, norm_f):
    f = lambda a: np.ascontiguousarray(np.asarray(a, dtype=np.float32))
    x_prompt, x_sample, cache_k, cache_v, state_ret = map(f, (x_prompt, x_sample, cache_k, cache_v, state_ret))
    norm1, w_in, sinks, w_out, norm2, w_up, w_down, norm_f = map(f, (norm1, w_in, sinks, w_out, norm2, w_up, w_down, norm_f))
    consts = _host_consts()
    perm = []
    for t in range(4):
        perm += list(range(64 * t, 64 * t + 64)) + list(range(256 + 64 * t, 256 + 64 * t + 64))
    perm += list(range(512, 1024))
    w_out_p = np.ascontiguousarray(w_out[0][perm, :])
    if "nc" not in _NC_CACHE:
        _NC_CACHE["nc"] = build_nc()
    nc = _NC_CACHE["nc"]
    in_maps = []
    for c in range(8):
        xs = np.concatenate([x_prompt[c].reshape(16, 128, D), x_sample[2 * c:2 * c + 2].reshape(1, 128, D)], axis=0)
        m = dict(xs=np.ascontiguousarray(xs),
                 ck=np.ascontiguousarray(cache_k[0, 2 * c:2 * c + 2].reshape(2, 128, 128)),
                 cv=np.ascontiguousarray(cache_v[0, 2 * c:2 * c + 2].reshape(2, 128, 128)),
                 sr=np.ascontiguousarray(state_ret[0, 2 * c:2 * c + 2]),
                 norm1=norm1[0], w_in=w_in[0], sinks=sinks[0], w_out=w_out_p, norm2=norm2[0], w_up=w_up[0], w_down=w_down[0], norm_f=norm_f)
        m.update(consts)
        in_maps.append(m)
    res = run_bass_kernel_spmd(nc, in_maps, core_ids=list(range(8)))
    R = res.results
    y_prompt = np.stack([R[c]["y"][:16].reshape(2048, D) for c in range(8)], 0)
    y_sample = np.concatenate([R[c]["y"][16].reshape(2, 64, D) for c in range(8)], 0)
    nk_p = np.stack([R[c]["nk"][0].reshape(128, 2, 64) for c in range(8)], 0)[None]
    nv_p = np.stack([R[c]["nv"][0].reshape(128, 2, 64) for c in range(8)], 0)[None]
    ns_p = np.stack([R[c]["ns"][0] for c in range(8)], 0)[None]
    nk_s = np.concatenate([R[c]["nk"][1:3].reshape(2, 128, 2, 64) for c in range(8)], 0)[None]
    nv_s = np.concatenate([R[c]["nv"][1:3].reshape(2, 128, 2, 64) for c in range(8)], 0)[None]
    ns_s = np.concatenate([R[c]["ns"][1:3] for c in range(8)], 0)[None]
    return (y_prompt.astype(np.float32), y_sample.astype(np.float32), nk_p.astype(np.float32), nv_p.astype(np.float32),
            ns_p.astype(np.float32), nk_s.astype(np.float32), nv_s.astype(np.float32), ns_s.astype(np.float32))
```

```python
from contextlib import ExitStack

import numpy as np
import concourse.bass as bass
import concourse.mybir as mybir
from concourse.bass_utils import run_bass_kernel_spmd

F32 = mybir.dt.float32
BF16 = mybir.dt.bfloat16
AF = mybir.ActivationFunctionType
ALU = mybir.AluOpType

NT = 17
D = 1024
INW = 2816
DFF = 4096
EPS = 1e-6
GAM = [1.0 - 2.0 ** (-5 - h) for h in range(4)]
NCH = 8
CFG = dict(pG0=1, pG1=0, pG2=0, trb=6, trbQ=2, trbR=2, backpair=(2, 3), ret=(4, 7, 7, 6), backfirst=False)


class Buf:
    __slots__ = ("w", "r", "name", "excl")

    def __init__(self, name="", excl=False):
        self.w = None
        self.r = []
        self.name = name
        self.excl = excl


class Prog:
    ENGS = ("pe", "act", "dve", "pool", "sp")

    def __init__(self, nc, stack):
        self.nc = nc
        self.stack = stack
        self.sem = {e: stack.enter_context(nc.semaphore("s_" + e)) for e in self.ENGS}
        self.nodes = []
        self.bar = None
        self.nd = 0

    def dsem(self, name=None):
        self.nd += 1
        return self.stack.enter_context(self.nc.semaphore(name or f"d{self.nd}"))

    def _deps(self, reads, writes, extra, nobar=False):
        d = set()
        for b in reads:
            if b.w is not None:
                d.add(b.w)
        for b in writes:
            if b.w is not None:
                d.add(b.w)
            d.update(b.r)
        d.update(x for x in extra if x is not None)
        if self.bar is not None and not nobar:
            d.add(self.bar)
        return d

    def _reg(self, nid, reads, writes):
        for b in reads:
            b.r.append(nid)
        for b in writes:
            b.w = nid
            b.r = []

    def op(self, eng, fns, reads=(), writes=(), extra=(), dur=0.5, nobar=False):
        if not isinstance(fns, (list, tuple)):
            fns = [fns]
        writes = list(writes) + [b for b in reads if b.excl]
        reads = [b for b in reads if not b.excl]
        deps = self._deps(reads, writes, extra, nobar)
        nid = len(self.nodes)
        self.nodes.append(dict(eng=eng, fns=list(fns), deps=deps, dur=dur, lat=0.0, dsem=None))
        self._reg(nid, reads, writes)
        return nid

    def dma(self, eng, out, in_, dsem, reads=(), writes=(), extra=(), nbytes=4096, nobar=False):
        deps = self._deps(reads, writes, extra, nobar)
        nid = len(self.nodes)
        iss = 1.5 if eng == "pool" else 0.2
        self.nodes.append(dict(eng=eng, fns=[lambda e: e.dma_start(out=out, in_=in_)], deps=deps, dur=iss,
                               lat=2.0 + nbytes * 128 / 280e3, dsem=dsem))
        self._reg(nid, reads, writes)
        return nid

    def barrier(self, fn):
        nid = len(self.nodes)
        self.nodes.append(dict(eng="dve", fns=[fn], deps=set(range(nid)), dur=0.1, lat=0.0, dsem=None))
        self.bar = nid
        return nid

    def schedule(self):
        N = len(self.nodes)
        bar = self.bar
        succ = [[] for _ in range(N)]
        for i, n in enumerate(self.nodes):
            if i == bar:
                continue
            for d in n["deps"]:
                succ[d].append(i)
        prio = [0.0] * N
        for i in range(N - 1, -1, -1):
            n = self.nodes[i]
            best = 0.0
            for s in succ[i]:
                if prio[s] > best:
                    best = prio[s]
            if bar is not None and i < bar and prio[bar] > best:
                best = prio[bar]
            prio[i] = n["dur"] + n["lat"] + best
        ndep = [len(n["deps"]) for n in self.nodes]
        fin = [0.0] * N
        depfin = [0.0] * N
        tfree = {e: 0.0 for e in self.ENGS}
        ready = {e: [] for e in self.ENGS}
        for i in range(N):
            if ndep[i] == 0:
                ready[self.nodes[i]["eng"]].append(i)
        order = {e: [] for e in self.ENGS}
        done = 0
        while done < N:
            best = None
            for e in self.ENGS:
                r = ready[e]
                if not r:
                    continue
                tf = tfree[e]
                bi = None
                bk = None
                for i in r:
                    st = depfin[i] if depfin[i] > tf else tf
                    k = (st, -prio[i])
                    if bk is None or k < bk:
                        bk = k
                        bi = i
                if best is None or bk < best[0]:
                    best = (bk, e, bi)
            (st, _), e, i = best
            n = self.nodes[i]
            if not hasattr(self, "why"):
                self.why = {}
                self.stt = {}
                self.lastn = {}
            self.stt[i] = st
            if depfin[i] >= tfree[e] - 1e-9:
                dd = [d for d in n["deps"] if abs(fin[d] - depfin[i]) < 1e-9]
                self.why[i] = ("dep", dd[0] if dd else None)
            else:
                self.why[i] = ("eng", self.lastn.get(e))
            self.lastn[e] = i
            ready[e].remove(i)
            tfree[e] = st + n["dur"]
            fin[i] = st + n["dur"] + n["lat"]
            order[e].append(i)
            done += 1
            if bar is not None and i < bar:
                ndep[bar] -= 1
                if depfin[bar] < fin[i]:
                    depfin[bar] = fin[i]
                if ndep[bar] == 0:
                    ready["dve"].append(bar)
            for s in succ[i]:
                ndep[s] -= 1
                if depfin[s] < fin[i]:
                    depfin[s] = fin[i]
                if ndep[s] == 0:
                    ready[self.nodes[s]["eng"]].append(s)
        import os
        if os.environ.get("MK_SERIAL") == "1":
            order = {e: [i for i in range(N) if self.nodes[i]["eng"] == e] for e in self.ENGS}
        self.order = order
        self.est_total = max(fin) if fin else 0.0
        self.fin = fin

    def emit(self, final_dsems):
        nc = self.nc
        self.schedule()
        nodes = self.nodes
        idx = {}
        rank = {}
        dcount = {}
        for e in self.ENGS:
            k = 0
            for i in self.order[e]:
                ds_ = nodes[i]["dsem"]
                if ds_ is None:
                    k += 1
                    idx[i] = k
                else:
                    dcount[id(ds_)] = dcount.get(id(ds_), 0) + 1
                    rank[i] = dcount[id(ds_)]
        engmap = {"pe": "tensor", "act": "scalar", "dve": "vector", "pool": "gpsimd", "sp": "sync"}
        progs = {}
        for e in self.ENGS:
            seen = {}
            items = []
            for i in self.order[e]:
                n = nodes[i]
                best = {}
                for d in n["deps"]:
                    dn = nodes[d]
                    if dn["dsem"] is None:
                        if e == "pe" and dn["eng"] == "pe":
                            continue
                        s, v = self.sem[dn["eng"]], idx[d]
                    else:
                        s, v = dn["dsem"], 16 * rank[d]
                    k = id(s)
                    if k not in best or best[k][1] < v:
                        best[k] = (s, v)
                ws = []
                for k, (s, v) in best.items():
                    if seen.get(k, 0) >= v:
                        continue
                    seen[k] = v
                    ws.append((s, v))
                if n["dsem"] is None:
                    items.append((ws, n["fns"], self.sem[e], 1))
                else:
                    items.append((ws, n["fns"], n["dsem"], 16))
            progs[e] = items
        fw = [(ds_, 16 * dcount[id(ds_)]) for ds_ in final_dsems if id(ds_) in dcount]
        with nc.Block() as block:
            for ename in self.ENGS:
                items = progs[ename]
                f_ = fw if ename == "sp" else ()

                def body(e, items=items, f_=f_):
                    for ws, fns, s, inc in items:
                        for (ws_s, ws_v) in ws:
                            e.wait_ge(ws_s, ws_v)
                        nf = len(fns)
                        for j, fn in enumerate(fns):
                            ins = fn(e)
                            if j == nf - 1:
                                ins.then_inc(s, inc)
                    for (s, v) in f_:
                        e.wait_ge(s, v)
                getattr(block, engmap[ename])(body)


class Arena:
    def __init__(self, tensor, nbytes):
        self.t = tensor
        self.n = nbytes
        self.off = 0

    def alloc(self, shape, dt):
        esz = 2 if dt == BF16 else 4
        nel = int(np.prod(shape))
        nb = (nel * esz + 31) // 32 * 32
        at = self.off
        self.off += nb
        assert self.off <= self.n, ("arena overflow", self.off, self.n)
        a = self.t[:, at // 4:(at + nb) // 4]
        if dt == BF16:
            a = a.bitcast(BF16)
        a = a[:, 0:nel]
        if len(shape) == 2:
            a = a.rearrange("p (a b) -> p a b", a=shape[0])
        elif len(shape) == 3:
            a = a.rearrange("p (a b c) -> p a b c", a=shape[0], b=shape[1])
        return a


def t_act(F):
    return 0.22 + F / 1300.0


def t_dve(F, slow=1.0):
    return 0.2 + slow * F / 950.0


def t_mm(N, n=1):
    return n * (0.035 + N / 2300.0)


def build_nc():
    nc = bass.Bass("TRN2", target_bir_lowering=False)
    di = lambda n, s: nc.dram_tensor(n, s, F32, kind="ExternalInput").ap()
    do = lambda n, s: nc.dram_tensor(n, s, F32, kind="ExternalOutput").ap()
    xs = di("xs", [NT, 128, D])
    ck = di("ck", [2, 128, 128])
    cv = di("cv", [2, 128, 128])
    sr = di("sr", [2, 4, 128, 128])
    norm1 = di("norm1", [D])
    w_in = di("w_in", [D, INW])
    sinks = di("sinks", [8])
    w_out = di("w_out", [D, D])
    norm2 = di("norm2", [D])
    w_up = di("w_up", [D, DFF])
    w_down = di("w_down", [DFF, D])
    norm_f = di("norm_f", [D])
    cosa_d = di("cosa", [128, NT, 32])
    sina_d = di("sina", [128, NT, 32])
    cosr_d = di("cosr", [128, NT, 64])
    sinr_d = di("sinr", [128, NT, 64])
    dt_d = di("dtab", [128, 2, 4, 128])
    qdec_d = di("qdec", [128, 2, 4, 128])
    kdec_d = di("kdec", [128, 2, 4])
    ident_d = di("ident", [128, 128])
    y_o = do("y", [NT, 128, D])
    nk_o = do("nk", [3, 128, 128])
    nv_o = do("nv", [3, 128, 128])
    ns_o = do("ns", [3, 4, 128, 128])

    with ExitStack() as st:
        P = Prog(nc, st)
        sbt = lambda name, shape, dt: st.enter_context(nc.sbuf_tensor(name, shape, dt))
        pairs = [st.enter_context(nc.psum_tensor(f"pp{i}", [128, 1024], F32)) for i in range(4)]
        _pb = [Buf(f"pair{i}", excl=True) for i in range(4)]
        bankbuf = [_pb[i // 2] for i in range(8)]

        def bank(b):
            return pairs[b // 2][:, (b % 2) * 512:(b % 2) * 512 + 512]

        def bank_bf(b):
            return bank(b).bitcast(BF16)

        PERS = 44 * 1024
        A1 = 151296
        pers_t = sbt("pers", [128, PERS // 4], F32)
        a1_t = sbt("a1", [128, A1 // 4], F32)
        wout_t = sbt("wout", [128, 8, D], BF16)
        pers = Arena(pers_t, PERS)
        mixflat = pers.alloc([NT * 1024], BF16)
        xbuf = [pers.alloc([D], F32) for _ in range(2)]
        ident = pers.alloc([128], BF16)
        ones = pers.alloc([64], BF16)
        stat = pers.alloc([64], F32)
        es = pers.alloc([8], F32)
        es_sel = pers.alloc([4], F32)
        sk = pers.alloc([8], F32)

        def mix_k(m):
            g, t = m // 4, m % 4
            ks = 512 if g < 4 else 128
            v = mixflat[:, g * 4096:g * 4096 + 8 * ks].rearrange("p (k x) -> p k x", k=8)
            return v[:, :, t * 128:(t + 1) * 128]

        def mix_grp(g, k):
            ks = 512 if g < 4 else 128
            return mixflat[:, g * 4096 + k * ks:g * 4096 + (k + 1) * ks]

        a1 = Arena(a1_t, A1)
        w_in_sb = a1.alloc([8, INW], BF16)
        g1b = a1.alloc([D], F32)
        cosa = a1.alloc([NT, 32], F32)
        sina = a1.alloc([NT, 32], F32)
        cosr = a1.alloc([NT, 64], F32)
        sinr = a1.alloc([NT, 64], F32)
        dtab = a1.alloc([2, 4, 128], F32)
        qdec = a1.alloc([2, 4, 128], F32)
        kdec = a1.alloc([2, 4], F32)
        dbl = lambda shape, dt: [a1.alloc(shape, dt) for _ in range(2)]
        a_bf = dbl([D], BF16)
        aT = dbl([8, 128], BF16)
        qrot = dbl([4, 2, 64], BF16)
        krot = dbl([128], BF16)
        QT = dbl([2, 4, 64], BF16)
        qkr = dbl([8, 128], BF16)
        QKrT = dbl([8, 128], BF16)
        QdT = dbl([4, 128], BF16)
        Kd = dbl([4, 128], BF16)
        Vr = dbl([4, 128], BF16)
        sg = dbl([4, 128], F32)
        scm = dbl([4, 128], BF16)
        rmix = dbl([4, 128], BF16)
        PTpc = dbl([1024], BF16)
        PTp = [a[:, 0:512] for a in PTpc]
        PTc = [a[:, 512:1024] for a in PTpc]
        rec = a1.alloc([2, 4, 64], F32)
        sgs = a1.alloc([512], F32)
        tcA = a1.alloc([640], F32)
        tsA = a1.alloc([640], F32)
        tcR = a1.alloc([D], F32)
        tsR = a1.alloc([D], F32)
        junk = a1.alloc([D], BF16)
        krot_f = [a1.alloc([128], F32) for _ in range(2)]
        vout_f = [a1.alloc([128], F32) for _ in range(2)]
        KT = [a1.alloc([128], BF16) for _ in range(3)]
        Vt = [a1.alloc([128], BF16) for _ in range(3)]
        KTc = [a1.alloc([128], BF16) for _ in range(2)]
        Vc = [a1.alloc([128], BF16) for _ in range(2)]
        cstage = [a1.alloc([256], F32) for _ in range(2)]
        cstage_bf = [a1.alloc([128], BF16) for _ in range(2)]
        S = [a1.alloc([4, 128], F32) for _ in range(3)]
        Sbf = [a1.alloc([4, 128], BF16) for _ in range(3)]
        a2 = Arena(a1_t, A1)
        yacc = a2.alloc([NT, D], F32)
        g2b = a2.alloc([D], F32)
        gfb = a2.alloc([D], F32)
        a2_bf = [a2.alloc([D], BF16) for _ in range(2)]
        uT = [a2.alloc([4, 512], BF16) for _ in range(2)]
        urel = [a2.alloc([512], F32) for _ in range(2)]
        ystage = [a2.alloc([D], F32) for _ in range(2)]
        junk2 = a2.alloc([D], BF16)
        ring_up = [a2.alloc([8, 512], BF16) for _ in range(2)]
        ring_dn = [a2.alloc([4, D], BF16) for _ in range(2)]

        B = {}

        def bf(name):
            if name not in B:
                B[name] = Buf(name)
            return B[name]

        def bf2(name):
            return [bf(name + "0"), bf(name + "1")]

        def cload(name, dst, src, eng="sp", nbytes=4096):
            b_ = bf(name)
            P.dma(eng, dst, src, P.dsem("d_" + name), writes=[b_], nbytes=nbytes)
            return b_

        fl2 = lambda a: a.rearrange("p a b -> p (a b)")
        fl3 = lambda a: a.rearrange("p a b c -> p (a b c)")
        g1bb = cload("g1b", g1b, norm1.partition_broadcast(128))
        idb = cload("ident", ident, ident_d, eng="pool", nbytes=512)
        skb = cload("sk", sk, sinks.partition_broadcast(128), nbytes=65536)
        xsem = [P.dsem("dx0"), P.dsem("dx1")]
        xb = bf2("x")
        x0n = P.dma("sp", xbuf[0], xs[0], xsem[0], writes=[xb[0]], nbytes=4096)
        wbuf = {}
        wprev = None
        wfirst = [x0n, g1bb.w, idb.w]
        for gi_, (c0_, c1_) in enumerate(((0, 768), (768, 1792), (1792, INW))):
            b_ = bf(f"win{gi_}")
            wprev = P.dma("pool", w_in_sb[:, :, c0_:c1_], w_in[:, c0_:c1_].rearrange("(k p) n -> p k n", p=128), P.dsem(f"dwin{gi_}"), writes=[b_],
                          extra=[wprev] + (wfirst if gi_ == 0 else []), nbytes=(c1_ - c0_) * 32)
            wbuf[c0_] = b_
        cosab = cload("cosa", fl2(cosa), fl2(cosa_d), nbytes=2176)
        sinab = cload("sina", fl2(sina), fl2(sina_d), nbytes=2176)
        cosrb = cload("cosr", fl2(cosr), fl2(cosr_d), nbytes=4352)
        sinrb = cload("sinr", fl2(sinr), fl2(sinr_d), nbytes=4352)
        dtabb = cload("dtab", fl3(dtab), fl3(dt_d))
        qdecb = cload("qdec", fl3(qdec), fl3(qdec_d))
        kdecb = cload("kdec", fl2(kdec), fl2(kdec_d), nbytes=32)
        wob = bf("wout")
        P.dma("pool", wout_t[:], w_out.rearrange("(k p) n -> p k n", p=128), P.dsem("dwo"), writes=[wob], extra=[wprev], nbytes=16384)
        onb = bf("ones")
        P.op("dve", lambda e: e.memset(ones, 1.0), writes=[onb], dur=0.1)
        esb = bf("es")
        P.op("act", lambda e: e.activation(out=es, in_=sk, func=AF.Exp), reads=[skb], writes=[esb], dur=0.3)
        essb = bf("es_sel")
        P.op("dve", lambda e: e.tensor_copy(out=es_sel[0:64, :], in_=es[0:64, 0:4]), reads=[esb], writes=[essb], dur=0.1)
        P.op("dve", lambda e: e.tensor_copy(out=es_sel[64:128, :], in_=es[64:128, 4:8]), reads=[esb], writes=[essb], dur=0.1)

        stat_n = [0]

        def stat_col(n):
            c0 = stat_n[0]
            stat_n[0] += n
            assert stat_n[0] <= 60
            return stat[:, c0:c0 + n], bf(f"stat{c0}")

        st_n1 = [stat_col(1) for _ in range(2)]
        st_g = [stat_col(4) for _ in range(2)]
        st_n2 = [stat_col(1) for _ in range(2)]
        st_nf = [stat_col(1) for _ in range(2)]

        _jb = {}

        def jb_of(ap):
            k = id(ap)
            if k not in _jb:
                _jb[k] = Buf("junk")
            return _jb[k]

        def rms(src_ap, src_bufs, gain_ap, gain_bufs, out_ap, out_bufs, stc, junk_ap, F=1024):
            col, cb = stc
            fns = [lambda e: e.activation(out=junk_ap, in_=src_ap, func=AF.Square, accum_out=col),
                   lambda e: e.activation(out=col, in_=col, func=AF.Ln, scale=1.0 / D, bias=EPS),
                   lambda e: e.activation(out=col, in_=col, func=AF.Exp, scale=-0.5)]
            P.op("act", fns[0], reads=src_bufs, writes=[cb, jb_of(junk_ap)], dur=t_act(F))
            P.op("act", fns[1], writes=[cb], dur=0.25)
            P.op("act", fns[2], writes=[cb], dur=0.25)
            P.op("dve", lambda e: e.scalar_tensor_tensor(out=out_ap, in0=src_ap, scalar=col, in1=gain_ap, op0=ALU.mult, op1=ALU.mult),
                 reads=list(src_bufs) + [cb] + list(gain_bufs), writes=out_bufs, dur=t_dve(F))

        TRB = 2

        def transposes(srcs, src_bufs, dst_ap, dst_bufs, trb=TRB):
            n = len(srcs)
            tb = bank_bf(trb)
            fns = [(lambda e, i=i, s=s: e.transpose(out=tb[:, i * 128:(i + 1) * 128], in_=s, identity=ident)) for i, s in enumerate(srcs)]
            P.op("pe", fns, reads=list(src_bufs) + [idb], writes=[bankbuf[trb]], dur=t_mm(128, n))
            src = tb[:, 0:n * 128].rearrange("p (a b) -> p a b", a=n)
            P.op("act", lambda e: e.copy(out=dst_ap, in_=src), reads=[bankbuf[trb]], writes=dst_bufs, dur=t_act(n * 128))

        csb = bf2("cst")
        Sb = [bf("S0"), bf("S1"), bf("S2")]
        Sbfb = [bf("Sbf0"), bf("Sbf1"), bf("Sbf2")]
        d_misc = P.dsem("dmisc")
        KTcb, Vcb, csbf_b = bf2("KTc"), bf2("Vc"), bf2("csbf")
        csvb = bf2("cstv")
        def prep_sample():
            for b in range(2):
                P.dma("sp", cstage[b][:, 0:128], ck[b], P.dsem(f"dck{b}"), writes=[csb[b]], nbytes=512)
                P.dma("sp", cstage[b][:, 128:256], cv[b], P.dsem(f"dcv{b}"), writes=[csvb[b]], nbytes=512)
                P.dma("sp", S[1 + b], sr[b].rearrange("h d e -> d h e"), P.dsem(f"dsr{b}"), writes=[Sb[1 + b]], nbytes=2048)
                P.op("dve", lambda e, b=b: e.tensor_copy(out=cstage_bf[b], in_=cstage[b][:, 0:128]), reads=[csb[b]], writes=[csbf_b[b]], dur=0.3)
                P.op("dve", lambda e, b=b: e.tensor_copy(out=Vc[b], in_=cstage[b][:, 128:256]), reads=[csvb[b]], writes=[Vcb[b]], dur=0.3)
                transposes([cstage_bf[b]], [csbf_b[b]], KTc[b].unsqueeze(1), [KTcb[b]])
                P.op("act", lambda e, b=b: e.copy(out=Sbf[1 + b], in_=S[1 + b]), reads=[Sb[1 + b]], writes=[Sbfb[1 + b]], dur=t_act(512))
                P.dma("sp", nk_o[1 + b, 0:64, :], ck[b, 64:128, :], d_misc, nbytes=512)
                P.dma("sp", nv_o[1 + b, 0:64, :], cv[b, 64:128, :], d_misc, nbytes=512)

        abf_b, aT_b = bf2("a_bf"), bf2("aT")
        tcAb, tsAb, tcRb, tsRb = bf("tcA"), bf("tsA"), bf("tcR"), bf("tsR")
        qrb, krb, QTb = bf2("qrot"), bf2("krot"), bf2("QT")
        KTb, Vtb = [bf(f"KT{i}") for i in range(3)], [bf(f"V{i}") for i in range(3)]
        PTcb, PTpb = bf2("PTc"), bf2("PTp")
        recb, sgsb = bf("rec"), bf("sgs")
        qkrb, QKrTb, QdTb, Kdb, Vrb, sgb, scmb, rmixb = (bf2(n) for n in ("qkr", "QKrT", "QdT", "Kd", "Vr", "sg", "scm", "rmix"))
        mixb = [bf(f"mix{m}") for m in range(NT)]
        krfb, vofb = bf2("krf"), bf2("vof")

        def front(m):
            sample = (m == NT - 1)
            ty = 1 if sample else 0
            xi = m % 2
            pi = m % 2
            if m > 0:
                P.dma("sp", xbuf[xi], xs[m], xsem[xi], writes=[xb[xi]], nbytes=4096)
            rms(xbuf[xi], [xb[xi]], g1b, [g1bb], a_bf[pi], [abf_b[pi]], st_n1[pi], junk)
            transposes([a_bf[pi][:, k * 128:(k + 1) * 128] for k in range(8)], [abf_b[pi]], aT[pi], [aT_b[pi]], trb=CFG['trb'])

            def proj(col0, ncols, pr, pi=pi):
                fns = []
                nb = (ncols + 511) // 512
                tot = 0.0
                for k in range(8):
                    for n in range(nb):
                        c0 = col0 + n * 512
                        w_ = min(512, col0 + ncols - c0)
                        tot += t_mm(w_)
                        fns.append(lambda e, k=k, n=n, c0=c0, w_=w_: e.matmul(
                            bank(2 * pr + n)[:, 0:w_], lhsT=aT[pi][:, k, :], rhs=w_in_sb[:, k, c0:c0 + w_],
                            start=(k == 0), stop=(k == 7)))
                P.op("pe", fns, reads=[aT_b[pi], wbuf[col0]], writes=[bankbuf[2 * pr + n] for n in range(nb)], dur=tot)

            PG0, PG1, PG2 = CFG['pG0'], CFG['pG1'], CFG['pG2']
            proj(0, 768, PG0)
            z0 = pairs[PG0]
            zall = z0[:, 0:640].rearrange("p (h two r) -> p h two r", h=10, two=2)
            tca = tcA.rearrange("p (h two r) -> p h two r", h=10, two=2)
            tsa = tsA.rearrange("p (h two r) -> p h two r", h=10, two=2)
            csb_ = cosa[:, m, :].unsqueeze(1).unsqueeze(1).broadcast_to([128, 10, 2, 32])
            snb_ = sina[:, m, :].unsqueeze(1).unsqueeze(1).broadcast_to([128, 10, 2, 32])
            P.op("dve", lambda e, csb_=csb_: e.tensor_tensor(out=tca, in0=zall, in1=csb_, op=ALU.mult),
                 reads=[bankbuf[2 * PG0], cosab], writes=[tcAb], dur=t_dve(640))
            P.op("dve", lambda e, snb_=snb_: e.tensor_tensor(out=tsa, in0=zall, in1=snb_, op=ALU.mult),
                 reads=[bankbuf[2 * PG0], sinab], writes=[tsAb], dur=t_dve(640))
            cur = m % 3
            prv = (m + 2) % 3
            P.op("act", lambda e, cur=cur: e.copy(out=Vt[cur], in_=z0[:, 640:768]), reads=[bankbuf[2 * PG0]], writes=[Vtb[cur]], dur=t_act(128))
            if m >= NT - 2:
                j = m - (NT - 2)
                P.op("act", lambda e, j=j: e.copy(out=vout_f[j], in_=z0[:, 640:768]), reads=[bankbuf[2 * PG0]], writes=[vofb[j]], dur=t_act(128))
            tcq = tcA[:, 0:512].rearrange("p (g t two r) -> p g t two r", g=2, t=4, two=2)
            tsq = tsA[:, 0:512].rearrange("p (g t two r) -> p g t two r", g=2, t=4, two=2)
            qro = qrot[pi].rearrange("p t g (two r) -> p g t two r", two=2)
            tck = tcA[:, 512:640].rearrange("p (h two r) -> p h two r", h=2, two=2)
            tsk = tsA[:, 512:640].rearrange("p (h two r) -> p h two r", h=2, two=2)
            kro = krot[pi].rearrange("p (h two r) -> p h two r", h=2, two=2)
            P.op("dve", lambda e, qro=qro: e.tensor_tensor(out=qro[:, :, :, 0, :], in0=tcq[:, :, :, 0, :], in1=tsq[:, :, :, 1, :], op=ALU.subtract),
                 reads=[tcAb, tsAb], writes=[qrb[pi]], dur=t_dve(256))
            P.op("dve", lambda e, qro=qro: e.tensor_tensor(out=qro[:, :, :, 1, :], in0=tcq[:, :, :, 1, :], in1=tsq[:, :, :, 0, :], op=ALU.add),
                 reads=[tcAb, tsAb], writes=[qrb[pi]], dur=t_dve(256))
            P.op("dve", lambda e, kro=kro: e.tensor_tensor(out=kro[:, :, 0, :], in0=tck[:, :, 0, :], in1=tsk[:, :, 1, :], op=ALU.subtract),
                 reads=[tcAb, tsAb], writes=[krb[pi]], dur=t_dve(64))
            P.op("dve", lambda e, kro=kro: e.tensor_tensor(out=kro[:, :, 1, :], in0=tck[:, :, 1, :], in1=tsk[:, :, 0, :], op=ALU.add),
                 reads=[tcAb, tsAb], writes=[krb[pi]], dur=t_dve(64))
            if m >= NT - 2:
                j = m - (NT - 2)
                krf = krot_f[j].rearrange("p (h two r) -> p h two r", h=2, two=2)
                P.op("dve", lambda e, krf=krf: e.tensor_tensor(out=krf[:, :, 0, :], in0=tck[:, :, 0, :], in1=tsk[:, :, 1, :], op=ALU.subtract),
                     reads=[tcAb, tsAb], writes=[krfb[j]], dur=t_dve(64))
                P.op("dve", lambda e, krf=krf: e.tensor_tensor(out=krf[:, :, 1, :], in0=tck[:, :, 1, :], in1=tsk[:, :, 0, :], op=ALU.add),
                     reads=[tcAb, tsAb], writes=[krfb[j]], dur=t_dve(64))
                if not sample:
                    P.dma("sp", nk_o[0], krot_f[j], d_misc, reads=[krfb[j]], nbytes=512)
                    P.dma("sp", nv_o[0], vout_f[j], d_misc, reads=[vofb[j]], nbytes=512)
                else:
                    for b in range(2):
                        P.dma("sp", nk_o[1 + b, 64:128, :], krot_f[j][64 * b:64 * b + 64, :], d_misc, reads=[krfb[j]], nbytes=512)
                        P.dma("sp", nv_o[1 + b, 64:128, :], vout_f[j][64 * b:64 * b + 64, :], d_misc, reads=[vofb[j]], nbytes=512)
            TRQ = CFG['trbQ']
            tb = bank_bf(TRQ)
            fns = [(lambda e, t=t, pi=pi: e.transpose(out=tb[:, t * 128:(t + 1) * 128], in_=qrot[pi][:, t, :, :].rearrange("p g d -> p (g d)"), identity=ident)) for t in range(4)]
            fns.append(lambda e, pi=pi: e.transpose(out=tb[:, 512:640], in_=krot[pi], identity=ident))
            P.op("pe", fns, reads=[qrb[pi], krb[pi], idb], writes=[bankbuf[TRQ]], dur=t_mm(128, 5))
            fns = [lambda e, pi=pi: e.copy(out=QT[pi], in_=tb[:, 0:512].rearrange("p (t c q) -> p c t q", t=4, c=2)),
                   lambda e, cur=cur: e.copy(out=KT[cur], in_=tb[:, 512:640])]
            P.op("act", fns, reads=[bankbuf[TRQ]], writes=[QTb[pi], KTb[cur]], dur=t_act(512) + t_act(128))

            proj(768, 1024, PG1)
            z1 = pairs[PG1]
            zr = z1[:, :].rearrange("p (h two r) -> p h two r", h=8, two=2)
            tcr = tcR.rearrange("p (h two r) -> p h two r", h=8, two=2)
            tsr = tsR.rearrange("p (h two r) -> p h two r", h=8, two=2)
            qko = qkr[pi].rearrange("p h (two r) -> p h two r", two=2)
            csr_ = cosr[:, m, :].unsqueeze(1).unsqueeze(1).broadcast_to([128, 8, 2, 64])
            snr_ = sinr[:, m, :].unsqueeze(1).unsqueeze(1).broadcast_to([128, 8, 2, 64])
            P.op("dve", lambda e, csr_=csr_: e.tensor_tensor(out=tcr, in0=zr, in1=csr_, op=ALU.mult), reads=[bankbuf[2 * PG1], cosrb], writes=[tcRb], dur=t_dve(1024))
            P.op("dve", lambda e, snr_=snr_: e.tensor_tensor(out=tsr, in0=zr, in1=snr_, op=ALU.mult), reads=[bankbuf[2 * PG1], sinrb], writes=[tsRb], dur=t_dve(1024))
            P.op("dve", lambda e, qko=qko: e.tensor_tensor(out=qko[:, :, 0, :], in0=tcr[:, :, 0, :], in1=tsr[:, :, 1, :], op=ALU.subtract), reads=[tcRb, tsRb], writes=[qkrb[pi]], dur=t_dve(512))
            P.op("dve", lambda e, qko=qko: e.tensor_tensor(out=qko[:, :, 1, :], in0=tcr[:, :, 1, :], in1=tsr[:, :, 0, :], op=ALU.add), reads=[tcRb, tsRb], writes=[qkrb[pi]], dur=t_dve(512))
            transposes([qkr[pi][:, h, :] for h in range(8)], [qkrb[pi]], QKrT[pi], [QKrTb[pi]], trb=CFG['trbR'])
            P.op("dve", lambda e, ty=ty, pi=pi: e.tensor_tensor(out=QdT[pi], in0=QKrT[pi][:, 0:4, :], in1=qdec[:, ty, :, :], op=ALU.mult),
                 reads=[QKrTb[pi], qdecb], writes=[QdTb[pi]], dur=t_dve(512))
            P.op("dve", lambda e, ty=ty, pi=pi: e.tensor_tensor(out=Kd[pi], in0=qkr[pi][:, 4:8, :], in1=kdec[:, ty, :].unsqueeze(2).broadcast_to([128, 4, 128]), op=ALU.mult),
                 reads=[qkrb[pi], kdecb], writes=[Kdb[pi]], dur=t_dve(512))
            proj(1792, 1024, PG2)
            P.op("act", lambda e, pi=pi: e.copy(out=Vr[pi].rearrange("p a b -> p (a b)"), in_=pairs[PG2][:, 0:512]), reads=[bankbuf[2 * PG2]], writes=[Vrb[pi]], dur=t_act(512))
            gps = pairs[PG2][:, 512:1024]
            P.op("act", lambda e: e.activation(out=sgs, in_=gps, func=AF.Exp, scale=-1.0), reads=[bankbuf[2 * PG2]], writes=[sgsb], dur=t_act(512))
            P.op("act", lambda e: e.activation(out=sgs, in_=sgs, func=AF.Ln, bias=1.0), writes=[sgsb], dur=t_act(512))
            P.op("act", lambda e: e.activation(out=sgs, in_=sgs, func=AF.Exp, scale=-1.0), writes=[sgsb], dur=t_act(512))
            P.op("dve", lambda e, pi=pi: e.tensor_tensor(out=sg[pi].rearrange("p a b -> p (a b)"), in0=gps, in1=sgs, op=ALU.mult),
                 reads=[bankbuf[2 * PG2], sgsb], writes=[sgb[pi]], dur=t_dve(512))

        def back(m):
            sample = (m == NT - 1)
            ty = 1 if sample else 0
            pi = m % 2
            cur = m % 3
            prv = (m + 2) % 3
            pS, pO = CFG['backpair']
            SBc, SBp, OB_, DB_ = 2 * pS + 1, 2 * pS, 2 * pO, 2 * pO + 1
            has_prev = sample or m > 0
            QTf = QT[pi].rearrange("p c t q -> p (c t q)")
            for g in range(2):
                r0 = 64 * g
                fns = [lambda e, r0=r0, cur=cur, QTf=QTf: e.matmul(bank(SBc), lhsT=KT[cur][r0:r0 + 64, :], rhs=QTf[r0:r0 + 64, :], start=True, stop=True)]
                rd = [KTb[cur], QTb[pi]]
                if has_prev:
                    if not sample:
                        fns.append(lambda e, r0=r0, prv=prv, QTf=QTf: e.matmul(bank(SBp), lhsT=KT[prv][r0:r0 + 64, :], rhs=QTf[r0:r0 + 64, :], start=True, stop=True))
                        rd.append(KTb[prv])
                    else:
                        fns += [(lambda e, b=b, r0=r0, QTf=QTf: e.matmul(bank(SBp)[:, 256 * b:256 * b + 256], lhsT=KTc[b][r0:r0 + 64, :], rhs=QTf[r0:r0 + 64, 256 * b:256 * b + 256],
                                                                       start=True, stop=True)) for b in range(2)]
                        rd += KTcb
                P.op("pe", fns, reads=rd, writes=[bankbuf[SBc], bankbuf[SBp]], dur=t_mm(512, len(fns)))
                if has_prev:
                    P.op("act", lambda e, g=g: e.activation(out=PTpc[g], in_=pairs[pS][:, :], func=AF.Exp, scale=0.125),
                         reads=[bankbuf[SBc], bankbuf[SBp]], writes=[PTcb[g], PTpb[g]], dur=t_act(1024))
                else:
                    P.op("act", lambda e, g=g: e.activation(out=PTc[g], in_=bank(SBc), func=AF.Exp, scale=0.125),
                         reads=[bankbuf[SBc]], writes=[PTcb[g]], dur=t_act(512))
                fns = []
                for c in range(2):
                    contrib = []
                    if has_prev:
                        if sample:
                            contrib.append((Vc[c], PTp[g], 0, 128))
                        else:
                            contrib.append((Vt[prv], PTp[g], 0, 128) if c == 0 else (Vt[prv], PTp[g], 64, 128))
                    if sample:
                        contrib.append((Vt[cur], PTc[g], 64 * c, 64 * c + 64))
                    else:
                        contrib.append((Vt[cur], PTc[g], 0, 64) if c == 0 else (Vt[cur], PTc[g], 0, 128))
                    for (dbk_, lsel) in ((OB_, 0), (DB_, 1)):
                        for i, (vv, pt, k0, k1) in enumerate(contrib):
                            lhs = vv[k0:k1, r0:r0 + 64] if lsel == 0 else ones[k0:k1, :]
                            fns.append(lambda e, dbk_=dbk_, lhs=lhs, pt=pt, k0=k0, k1=k1, c=c, i=i, nctr=len(contrib), r0=r0: e.matmul(
                                bank(dbk_)[r0:r0 + 64, 256 * c:256 * c + 256], lhsT=lhs, rhs=pt[k0:k1, 256 * c:256 * c + 256],
                                start=(i == 0), stop=(i == nctr - 1)))
                rd = [Vtb[cur], PTcb[g], onb] + ([PTpb[g]] + (Vcb if sample else [Vtb[prv]]) if has_prev else [])
                P.op("pe", fns, reads=rd, writes=[bankbuf[OB_], bankbuf[DB_]], dur=t_mm(256, len(fns)))
            dbv = bank(DB_).rearrange("p (c t q) -> p c t q", c=2, t=4)
            obv = bank(OB_).rearrange("p (c t q) -> p c t q", c=2, t=4)
            fns = [(lambda e, t=t: e.activation(out=rec[:, :, t, :], in_=dbv[:, :, t, :], func=AF.Ln, bias=es_sel[:, t:t + 1])) for t in range(4)]
            P.op("act", fns, reads=[bankbuf[DB_], essb], writes=[recb], dur=4 * t_act(128))
            P.op("act", lambda e: e.activation(out=rec.rearrange("p c t q -> p (c t q)"), in_=rec.rearrange("p c t q -> p (c t q)"), func=AF.Exp, scale=-1.0),
                 writes=[recb], dur=t_act(512))
            mo = mix_k(m)[:, 0:4, :].rearrange("p t (c q) -> p c t q", c=2)
            P.op("dve", lambda e, mo=mo: e.tensor_tensor(out=mo, in0=obv, in1=rec, op=ALU.mult),
                 reads=[bankbuf[OB_], recb], writes=[mixb[m]], dur=t_dve(512))

            SC_, OR_, UB_, RT_ = CFG.get('ret', (5, 6, 7, 4))
            scv = bank(SC_).rearrange("p (a b) -> p a b", a=4)
            fns = [(lambda e, h=h, pi=pi: e.matmul(scv[:, h, :], lhsT=QKrT[pi][:, 4 + h, :], rhs=QKrT[pi][:, h, :], start=True, stop=True)) for h in range(4)]
            P.op("pe", fns, reads=[QKrTb[pi]], writes=[bankbuf[SC_]], dur=t_mm(128, 4))
            P.op("dve", lambda e, ty=ty, pi=pi: e.tensor_tensor(out=scm[pi], in0=scv, in1=dtab[:, ty, :, :], op=ALU.mult),
                 reads=[bankbuf[SC_], dtabb], writes=[scmb[pi]], dur=t_dve(512))
            orv = bank(OR_).rearrange("p (a b) -> p a b", a=4)
            fns = []
            rd = [scmb[pi], Vrb[pi]]
            if not sample:
                cross = (m > 0)
                for h in range(4):
                    fns.append(lambda e, h=h, cross=cross, pi=pi: e.matmul(orv[:, h, :], lhsT=scm[pi][:, h, :], rhs=Vr[pi][:, h, :], start=True, stop=not cross))
                    if cross:
                        fns.append(lambda e, h=h, pi=pi: e.matmul(orv[:, h, :], lhsT=QdT[pi][:, h, :], rhs=Sbf[0][:, h, :], start=False, stop=True))
                if cross:
                    rd += [QdTb[pi], Sbfb[0]]
            else:
                for h in range(4):
                    fns.append(lambda e, h=h, pi=pi: e.matmul(orv[:, h, :], lhsT=scm[pi][:, h, :], rhs=Vr[pi][:, h, :], start=True, stop=True))
                    for b in range(2):
                        fns.append(lambda e, h=h, b=b, pi=pi: e.matmul(orv[64 * b:64 * b + 64, h, :], lhsT=QdT[pi][:, h, 64 * b:64 * b + 64], rhs=Sbf[1 + b][:, h, :],
                                                                      start=False, stop=False, skip_group_check=True))
                rd += [QdTb[pi], Sbfb[1], Sbfb[2]]
            P.op("pe", fns, reads=rd, writes=[bankbuf[OR_]], dur=t_mm(128, len(fns)))
            gcol, gcb = st_g[pi]
            fns = [(lambda e, h=h, gcol=gcol: e.activation(out=junk[:, h * 128:(h + 1) * 128], in_=orv[:, h, :], func=AF.Square, accum_out=gcol[:, h:h + 1])) for h in range(4)]
            P.op("act", fns, reads=[bankbuf[OR_]], writes=[gcb, jb_of(junk)], dur=4 * t_act(128))
            P.op("act", lambda e, gcol=gcol: e.activation(out=gcol, in_=gcol, func=AF.Ln, scale=1.0 / 128, bias=EPS), writes=[gcb], dur=0.25)
            P.op("act", lambda e, gcol=gcol: e.activation(out=gcol, in_=gcol, func=AF.Exp, scale=-0.5), writes=[gcb], dur=0.25)
            fns = [(lambda e, h=h, gcol=gcol, pi=pi: e.scalar_tensor_tensor(out=rmix[pi][:, h, :], in0=orv[:, h, :], scalar=gcol[:, h:h + 1], in1=sg[pi][:, h, :],
                                                                           op0=ALU.mult, op1=ALU.mult)) for h in range(4)]
            P.op("dve", fns, reads=[bankbuf[OR_], gcb, sgb[pi]], writes=[rmixb[pi]], dur=4 * t_dve(128))
            transposes([rmix[pi][:, h, :] for h in range(4)], [rmixb[pi]], mix_k(m)[:, 4:8, :], [mixb[m]], trb=RT_)
            if not sample:
                ubv = bank(UB_).rearrange("p (a b) -> p a b", a=4)
                fns = [(lambda e, h=h, pi=pi: e.matmul(ubv[:, h, :], lhsT=Kd[pi][:, h, :], rhs=Vr[pi][:, h, :], start=True, stop=True)) for h in range(4)]
                P.op("pe", fns, reads=[Kdb[pi], Vrb[pi]], writes=[bankbuf[UB_]], dur=t_mm(128, 4))
                if m == 0:
                    P.op("dve", lambda e: e.tensor_copy(out=S[0], in_=ubv), reads=[bankbuf[UB_]], writes=[Sb[0]], dur=t_dve(512))
                else:
                    fns = [(lambda e, h=h: e.scalar_tensor_tensor(out=S[0][:, h, :], in0=S[0][:, h, :], scalar=float(GAM[h] ** 128), in1=ubv[:, h, :],
                                                                 op0=ALU.mult, op1=ALU.add)) for h in range(4)]
                    P.op("dve", fns, reads=[bankbuf[UB_]], writes=[Sb[0]], dur=4 * t_dve(128))
                if m < NT - 2:
                    P.op("act", lambda e: e.copy(out=Sbf[0], in_=S[0]), reads=[Sb[0]], writes=[Sbfb[0]], dur=t_act(512))
                else:
                    P.dma("sp", ns_o[0].rearrange("h d e -> d h e"), S[0], d_misc, reads=[Sb[0]], nbytes=2048)
            else:
                for b in range(2):
                    ubk = UB_ if b == 0 else SC_
                    ubv = bank(ubk).rearrange("p (a b) -> p a b", a=4)
                    fns = [(lambda e, h=h, b=b, ubv=ubv, pi=pi: e.matmul(ubv[:, h, :], lhsT=Kd[pi][64 * b:64 * b + 64, h, :], rhs=Vr[pi][64 * b:64 * b + 64, h, :], start=True, stop=True)) for h in range(4)]
                    P.op("pe", fns, reads=[Kdb[pi], Vrb[pi]], writes=[bankbuf[ubk]], dur=t_mm(128, 4))
                    fns = [(lambda e, h=h, b=b, ubv=ubv: e.scalar_tensor_tensor(out=S[1 + b][:, h, :], in0=S[1 + b][:, h, :], scalar=float(GAM[h] ** 64), in1=ubv[:, h, :],
                                                                               op0=ALU.mult, op1=ALU.add)) for h in range(4)]
                    P.op("dve", fns, reads=[bankbuf[ubk], Sbfb[1 + b]], writes=[Sb[1 + b]], dur=4 * t_dve(128))
                    P.dma("sp", ns_o[1 + b].rearrange("h d e -> d h e"), S[1 + b], d_misc, reads=[Sb[1 + b]], nbytes=2048)

        front(0)
        for m in range(NT):
            if m + 1 == NT - 2:
                prep_sample()
            if CFG['backfirst']:
                back(m)
                if m + 1 < NT:
                    front(m + 1)
            else:
                if m + 1 < NT:
                    front(m + 1)
                back(m)

        P.barrier(lambda e: e.memset(stat[:, 60:61], 0.0))
        yb = [bf(f"yacc{m}") for m in range(NT)]
        g2bb = cload("g2b", g2b, norm2.partition_broadcast(128))
        gfbb = cload("gfb", gfb, norm_f.partition_broadcast(128))
        ringub, ringdb = bf2("ringu"), bf2("ringd")
        rsu = [P.dsem("dru0"), P.dsem("dru1")]
        rsd = [P.dsem("drd0"), P.dsem("drd1")]

        def load_chunk(c):
            s = c % 2
            P.dma("pool", ring_up[s], w_up[:, c * 512:(c + 1) * 512].rearrange("(k p) f -> p k f", p=128), rsu[s], writes=[ringub[s]], nbytes=16384)
            P.dma("pool", ring_dn[s], w_down[c * 512:(c + 1) * 512, :].rearrange("(j p) n -> p j n", p=128), rsd[s], writes=[ringdb[s]], nbytes=16384)

        load_chunk(0)
        load_chunk(1)
        a2b = bf2("a2_bf")
        for m in range(NT):
            xi = m % 2
            pi = m % 2
            P.dma("sp", xbuf[xi], xs[m], xsem[xi], writes=[xb[xi]], nbytes=4096, nobar=True)
            pr = m % 2
            mk = mix_k(m)
            fns = []
            for k in range(8):
                for n in range(2):
                    fns.append(lambda e, k=k, n=n, mk=mk, pr=pr: e.matmul(bank(2 * pr + n), lhsT=mk[:, k, :], rhs=wout_t[:, k, n * 512:(n + 1) * 512],
                                                                         start=(k == 0), stop=(k == 7)))
            P.op("pe", fns, reads=[mixb[m], wob], writes=[bankbuf[2 * pr], bankbuf[2 * pr + 1]], dur=t_mm(512, 16), nobar=True)
            P.op("dve", lambda e, m=m, pr=pr, xi=xi: e.tensor_tensor(out=yacc[:, m, :], in0=pairs[pr][:, :], in1=xbuf[xi], op=ALU.add),
                 reads=[bankbuf[2 * pr], bankbuf[2 * pr + 1], xb[xi]], writes=[yb[m]], dur=t_dve(1024))
            rms(yacc[:, m, :], [yb[m]], g2b, [g2bb], a2_bf[pi], [a2b[pi]], st_n2[pi], junk2)
            transposes([a2_bf[pi][:, k * 128:(k + 1) * 128] for k in range(8)], [a2b[pi]], mk, [mixb[m]], trb=4 + 2 * (m % 2))

        groups = [(0, 4), (4, 4), (8, 4), (12, 4), (16, 1)]
        uTb, urb = bf2("uT"), bf2("ur")
        ub_rot = up_rot = dn_rot = 0
        for c in range(NCH):
            s = c % 2
            for gi, (m0, nt_) in enumerate(groups):
                ntok = nt_ * 128
                ui = ub_rot % 2
                ub_rot += 1
                for j in range(4):
                    bk = 4 + (up_rot % 4)
                    up_rot += 1
                    fns = [(lambda e, k=k, j=j, bk=bk, s=s, ntok=ntok, gi=gi: e.matmul(bank(bk)[:, 0:ntok], lhsT=ring_up[s][:, k, j * 128:(j + 1) * 128],
                                                                                  rhs=mix_grp(gi, k), start=(k == 0), stop=(k == 7))) for k in range(8)]
                    P.op("pe", fns, reads=[mixb[m] for m in range(m0, m0 + nt_)] + [ringub[s]], writes=[bankbuf[bk]], dur=t_mm(ntok, 8))
                    ri = up_rot % 2
                    P.op("act", lambda e, bk=bk, ri=ri, ntok=ntok: e.activation(out=urel[ri][:, 0:ntok], in_=bank(bk)[:, 0:ntok], func=AF.Relu),
                         reads=[bankbuf[bk]], writes=[urb[ri]], dur=t_act(ntok))
                    P.op("act", lambda e, ri=ri, ui=ui, j=j, ntok=ntok: e.activation(out=uT[ui][:, j, 0:ntok], in_=urel[ri][:, 0:ntok], func=AF.Square),
                         reads=[urb[ri]], writes=[uTb[ui]], dur=t_act(ntok))
                for t in range(nt_):
                    m = m0 + t
                    pr = dn_rot % 2
                    dn_rot += 1
                    fns = []
                    for j in range(4):
                        for n in range(2):
                            fns.append(lambda e, j=j, n=n, t=t, pr=pr, ui=ui, s=s: e.matmul(bank(2 * pr + n), lhsT=uT[ui][:, j, t * 128:(t + 1) * 128],
                                                                                           rhs=ring_dn[s][:, j, n * 512:(n + 1) * 512], start=(j == 0), stop=(j == 3)))
                    P.op("pe", fns, reads=[uTb[ui], ringdb[s]], writes=[bankbuf[2 * pr], bankbuf[2 * pr + 1]], dur=t_mm(512, 8))
                    P.op("dve", lambda e, m=m, pr=pr: e.tensor_tensor(out=yacc[:, m, :], in0=pairs[pr][:, :], in1=yacc[:, m, :], op=ALU.add),
                         reads=[bankbuf[2 * pr], bankbuf[2 * pr + 1]], writes=[yb[m]], dur=t_dve(1024))
            if c + 2 < NCH:
                load_chunk(c + 2)

        ysem = [P.dsem("dy0"), P.dsem("dy1")]
        ysb = bf2("ys")
        for m in range(NT):
            yi = m % 2
            rms(yacc[:, m, :], [yb[m]], gfb, [gfbb], ystage[yi], [ysb[yi]], st_nf[yi], junk2)
            P.dma("sp", y_o[m], ystage[yi], ysem[yi], reads=[ysb[yi]], nbytes=4096)
        P.emit([ysem[0], ysem[1], d_misc])
    return nc


def _host_consts():
    pos = np.zeros((128, NT), np.float64)
    for m in range(16):
        pos[:, m] = 128 * m + np.arange(128)
    pos[:, 16] = 2048 + (np.arange(128) % 64)
    pos32 = pos.astype(np.float32)
    try:
        import jax
        import jax.numpy as jnp
        with jax.default_device(jax.devices("cpu")[0]):
            inv_a = np.asarray(1.0 / (10000.0 ** (jnp.arange(0, 64, 2, dtype=jnp.float32) / 64)), dtype=np.float32)
            inv_r = np.asarray(1.0 / (10000.0 ** jnp.linspace(0.0, 1.0, 64, dtype=jnp.float32)), dtype=np.float32)
    except Exception:
        inv_a = (1.0 / (np.float32(10000.0) ** (np.arange(0, 64, 2, dtype=np.float32) / np.float32(64)))).astype(np.float32)
        inv_r = (1.0 / (np.float32(10000.0) ** np.linspace(0.0, 1.0, 64, dtype=np.float32))).astype(np.float32)
    ang_a = (pos32[:, :, None] * inv_a[None, None, :]).astype(np.float32).astype(np.float64)
    ang_r = (pos32[:, :, None] * inv_r[None, None, :]).astype(np.float32).astype(np.float64)
    cosa, sina = np.cos(ang_a).astype(np.float32), np.sin(ang_a).astype(np.float32)
    cosr, sinr = np.cos(ang_r).astype(np.float32), np.sin(ang_r).astype(np.float32)
    logg = np.log1p(-np.exp2(-5.0 - np.arange(4, dtype=np.float64)))
    sc = 128.0 ** -0.5
    j = np.arange(128)[:, None]
    i = np.arange(128)[None, :]
    dtab = np.zeros((128, 2, 4, 128), np.float64)
    qdec = np.zeros((128, 2, 4, 128), np.float64)
    kdec = np.zeros((128, 2, 4), np.float64)
    for h in range(4):
        d0 = np.where(i >= j, np.exp(logg[h] * np.maximum(i - j, 0)), 0.0)
        dtab[:, 0, h, :] = sc * d0
        same = (i // 64) == (j // 64)
        dtab[:, 1, h, :] = sc * np.where(same, d0, 0.0)
        qdec[:, 0, h, :] = np.exp(logg[h] * (np.arange(128) + 1.0))[None, :]
        qdec[:, 1, h, :] = np.exp(logg[h] * ((np.arange(128) % 64) + 1.0))[None, :]
        kdec[:, 0, h] = sc * np.exp(logg[h] * (127.0 - np.arange(128)))
        kdec[:, 1, h] = sc * np.exp(logg[h] * (63.0 - (np.arange(128) % 64)))
    return dict(cosa=cosa, sina=sina, cosr=cosr, sinr=sinr, dtab=dtab.astype(np.float32), qdec=qdec.astype(np.float32),
                kdec=kdec.astype(np.float32), ident=np.eye(128, dtype=np.float32))


_NC_CACHE = {}


def kernel(x_prompt, x_sample, cache_k, cache_v, state_ret, norm1, w_in, sinks, w_out, norm2, w_up, w_down, norm_f):
    f = lambda a: np.ascontiguousarray(np.asarray(a, dtype=np.float32))
    x_prompt, x_sample, cache_k, cache_v, state_ret = map(f, (x_prompt, x_sample, cache_k, cache_v, state_ret))
    norm1, w_in, sinks, w_out, norm2, w_up, w_down, norm_f = map(f, (norm1, w_in, sinks, w_out, norm2, w_up, w_down, norm_f))
    consts = _host_consts()
    perm = []
    for t in range(4):
        perm += list(range(64 * t, 64 * t + 64)) + list(range(256 + 64 * t, 256 + 64 * t + 64))
    perm += list(range(512, 1024))
    w_out_p = np.ascontiguousarray(w_out[0][perm, :])
    if "nc" not in _NC_CACHE:
        _NC_CACHE["nc"] = build_nc()
    nc = _NC_CACHE["nc"]
    in_maps = []
    for c in range(8):
        xs = np.concatenate([x_prompt[c].reshape(16, 128, D), x_sample[2 * c:2 * c + 2].reshape(1, 128, D)], axis=0)
        m = dict(xs=np.ascontiguousarray(xs),
                 ck=np.ascontiguousarray(cache_k[0, 2 * c:2 * c + 2].reshape(2, 128, 128)),
                 cv=np.ascontiguousarray(cache_v[0, 2 * c:2 * c + 2].reshape(2, 128, 128)),
                 sr=np.ascontiguousarray(state_ret[0, 2 * c:2 * c + 2]),
                 norm1=norm1[0], w_in=w_in[0], sinks=sinks[0], w_out=w_out_p, norm2=norm2[0], w_up=w_up[0], w_down=w_down[0], norm_f=norm_f)
        m.update(consts)
        in_maps.append(m)
    res = run_bass_kernel_spmd(nc, in_maps, core_ids=list(range(8)))
    R = res.results
    y_prompt = np.stack([R[c]["y"][:16].reshape(2048, D) for c in range(8)], 0)
    y_sample = np.concatenate([R[c]["y"][16].reshape(2, 64, D) for c in range(8)], 0)
    nk_p = np.stack([R[c]["nk"][0].reshape(128, 2, 64) for c in range(8)], 0)[None]
    nv_p = np.stack([R[c]["nv"][0].reshape(128, 2, 64) for c in range(8)], 0)[None]
    ns_p = np.stack([R[c]["ns"][0] for c in range(8)], 0)[None]
    nk_s = np.concatenate([R[c]["nk"][1:3].reshape(2, 128, 2, 64) for c in range(8)], 0)[None]
    nv_s = np.concatenate([R[c]["nv"][1:3].reshape(2, 128, 2, 64) for c in range(8)], 0)[None]
    ns_s = np.concatenate([R[c]["ns"][1:3] for c in range(8)], 0)[None]
    return (y_prompt.astype(np.float32), y_sample.astype(np.float32), nk_p.astype(np.float32), nv_p.astype(np.float32),
            ns_p.astype(np.float32), nk_s.astype(np.float32), nv_s.astype(np.float32), ns_s.astype(np.float32))
```

```python
from contextlib import ExitStack

import numpy as np
import concourse.bass as bass
import concourse.mybir as mybir
from concourse.bass_utils import run_bass_kernel_spmd

F32 = mybir.dt.float32
BF16 = mybir.dt.bfloat16
AF = mybir.ActivationFunctionType
ALU = mybir.AluOpType

NT = 17
D = 1024
INW = 2816
DFF = 4096
EPS = 1e-6
GAM = [1.0 - 2.0 ** (-5 - h) for h in range(4)]
NCH = 8
CFG = dict(pG0=1, pG1=0, pG2=0, trb=6, trbQ=2, trbR=2, backpair=(2, 3), ret=(4, 7, 7, 6), backfirst=False)


class Buf:
    __slots__ = ("w", "r", "name", "excl")

    def __init__(self, name="", excl=False):
        self.w = None
        self.r = []
        self.name = name
        self.excl = excl


class Prog:
    ENGS = ("pe", "act", "dve", "pool", "sp")

    def __init__(self, nc, stack):
        self.nc = nc
        self.stack = stack
        self.sem = {e: stack.enter_context(nc.semaphore("s_" + e)) for e in self.ENGS}
        self.nodes = []
        self.bar = None
        self.gate = None
        self.nd = 0

    def dsem(self, name=None):
        self.nd += 1
        return self.stack.enter_context(self.nc.semaphore(name or f"d{self.nd}"))

    def _deps(self, reads, writes, extra, nobar=False):
        d = set()
        for b in reads:
            if b.w is not None:
                d.add(b.w)
        for b in writes:
            if b.w is not None:
                d.add(b.w)
            d.update(b.r)
        d.update(x for x in extra if x is not None)
        if self.bar is not None and not nobar:
            d.add(self.bar)
        if self.gate is not None:
            d.add(self.gate)
        return d

    def _reg(self, nid, reads, writes):
        for b in reads:
            b.r.append(nid)
        for b in writes:
            b.w = nid
            b.r = []

    def op(self, eng, fns, reads=(), writes=(), extra=(), dur=0.5, nobar=False):
        if not isinstance(fns, (list, tuple)):
            fns = [fns]
        writes = list(writes) + [b for b in reads if b.excl]
        reads = [b for b in reads if not b.excl]
        deps = self._deps(reads, writes, extra, nobar)
        nid = len(self.nodes)
        self.nodes.append(dict(eng=eng, fns=list(fns), deps=deps, dur=dur, lat=0.0, dsem=None))
        self._reg(nid, reads, writes)
        return nid

    def dma(self, eng, out, in_, dsem, reads=(), writes=(), extra=(), nbytes=4096, nobar=False):
        deps = self._deps(reads, writes, extra, nobar)
        nid = len(self.nodes)
        iss = 1.5 if eng == "pool" else 0.2
        self.nodes.append(dict(eng=eng, fns=[lambda e: e.dma_start(out=out, in_=in_)], deps=deps, dur=iss,
                               lat=2.0 + nbytes * 128 / 280e3, dsem=dsem))
        self._reg(nid, reads, writes)
        return nid

    def fence(self, fn):
        nid = len(self.nodes)
        self.nodes.append(dict(eng="dve", fns=[fn], deps=set(range(nid)), dur=0.1, lat=0.0, dsem=None))
        return nid

    def barrier(self, fn):
        nid = len(self.nodes)
        self.nodes.append(dict(eng="dve", fns=[fn], deps=set(range(nid)), dur=0.1, lat=0.0, dsem=None))
        self.bar = nid
        return nid

    def schedule(self):
        N = len(self.nodes)
        bar = self.bar
        succ = [[] for _ in range(N)]
        for i, n in enumerate(self.nodes):
            if i == bar:
                continue
            for d in n["deps"]:
                succ[d].append(i)
        prio = [0.0] * N
        for i in range(N - 1, -1, -1):
            n = self.nodes[i]
            best = 0.0
            for s in succ[i]:
                if prio[s] > best:
                    best = prio[s]
            if bar is not None and i < bar and prio[bar] > best:
                best = prio[bar]
            prio[i] = n["dur"] + n["lat"] + best
        ndep = [len(n["deps"]) for n in self.nodes]
        fin = [0.0] * N
        depfin = [0.0] * N
        tfree = {e: 0.0 for e in self.ENGS}
        ready = {e: [] for e in self.ENGS}
        for i in range(N):
            if ndep[i] == 0:
                ready[self.nodes[i]["eng"]].append(i)
        order = {e: [] for e in self.ENGS}
        done = 0
        while done < N:
            best = None
            for e in self.ENGS:
                r = ready[e]
                if not r:
                    continue
                tf = tfree[e]
                bi = None
                bk = None
                for i in r:
                    st = depfin[i] if depfin[i] > tf else tf
                    k = (st, -prio[i])
                    if bk is None or k < bk:
                        bk = k
                        bi = i
                if best is None or bk < best[0]:
                    best = (bk, e, bi)
            (st, _), e, i = best
            n = self.nodes[i]
            if not hasattr(self, "why"):
                self.why = {}
                self.stt = {}
                self.lastn = {}
            self.stt[i] = st
            if depfin[i] >= tfree[e] - 1e-9:
                dd = [d for d in n["deps"] if abs(fin[d] - depfin[i]) < 1e-9]
                self.why[i] = ("dep", dd[0] if dd else None)
            else:
                self.why[i] = ("eng", self.lastn.get(e))
            self.lastn[e] = i
            ready[e].remove(i)
            tfree[e] = st + n["dur"]
            fin[i] = st + n["dur"] + n["lat"]
            order[e].append(i)
            done += 1
            if bar is not None and i < bar:
                ndep[bar] -= 1
                if depfin[bar] < fin[i]:
                    depfin[bar] = fin[i]
                if ndep[bar] == 0:
                    ready["dve"].append(bar)
            for s in succ[i]:
                ndep[s] -= 1
                if depfin[s] < fin[i]:
                    depfin[s] = fin[i]
                if ndep[s] == 0:
                    ready[self.nodes[s]["eng"]].append(s)
        import os
        if os.environ.get("MK_SERIAL") == "1":
            order = {e: [i for i in range(N) if self.nodes[i]["eng"] == e] for e in self.ENGS}
        self.order = order
        self.est_total = max(fin) if fin else 0.0
        self.fin = fin

    def emit(self, final_dsems):
        nc = self.nc
        self.schedule()
        nodes = self.nodes
        idx = {}
        rank = {}
        dcount = {}
        for e in self.ENGS:
            k = 0
            for i in self.order[e]:
                ds_ = nodes[i]["dsem"]
                if ds_ is None:
                    k += 1
                    idx[i] = k
                else:
                    dcount[id(ds_)] = dcount.get(id(ds_), 0) + 1
                    rank[i] = dcount[id(ds_)]
        engmap = {"pe": "tensor", "act": "scalar", "dve": "vector", "pool": "gpsimd", "sp": "sync"}
        progs = {}
        for e in self.ENGS:
            seen = {}
            items = []
            for i in self.order[e]:
                n = nodes[i]
                best = {}
                for d in n["deps"]:
                    dn = nodes[d]
                    if dn["dsem"] is None:
                        if e == "pe" and dn["eng"] == "pe":
                            continue
                        s, v = self.sem[dn["eng"]], idx[d]
                    else:
                        s, v = dn["dsem"], 16 * rank[d]
                    k = id(s)
                    if k not in best or best[k][1] < v:
                        best[k] = (s, v)
                ws = []
                for k, (s, v) in best.items():
                    if seen.get(k, 0) >= v:
                        continue
                    seen[k] = v
                    ws.append((s, v))
                if n["dsem"] is None:
                    items.append((ws, n["fns"], self.sem[e], 1))
                else:
                    items.append((ws, n["fns"], n["dsem"], 16))
            progs[e] = items
        fw = [(ds_, 16 * dcount[id(ds_)]) for ds_ in final_dsems if id(ds_) in dcount]
        with nc.Block() as block:
            for ename in self.ENGS:
                items = progs[ename]
                f_ = fw if ename == "sp" else ()

                def body(e, items=items, f_=f_):
                    for ws, fns, s, inc in items:
                        for (ws_s, ws_v) in ws:
                            e.wait_ge(ws_s, ws_v)
                        nf = len(fns)
                        for j, fn in enumerate(fns):
                            ins = fn(e)
                            if j == nf - 1:
                                ins.then_inc(s, inc)
                    for (s, v) in f_:
                        e.wait_ge(s, v)
                getattr(block, engmap[ename])(body)


class Arena:
    def __init__(self, tensor, nbytes):
        self.t = tensor
        self.n = nbytes
        self.off = 0

    def alloc(self, shape, dt):
        esz = 2 if dt == BF16 else 4
        nel = int(np.prod(shape))
        nb = (nel * esz + 31) // 32 * 32
        at = self.off
        self.off += nb
        assert self.off <= self.n, ("arena overflow", self.off, self.n)
        a = self.t[:, at // 4:(at + nb) // 4]
        if dt == BF16:
            a = a.bitcast(BF16)
        a = a[:, 0:nel]
        if len(shape) == 2:
            a = a.rearrange("p (a b) -> p a b", a=shape[0])
        elif len(shape) == 3:
            a = a.rearrange("p (a b c) -> p a b c", a=shape[0], b=shape[1])
        return a


def t_act(F):
    return 0.22 + F / 1300.0


def t_dve(F, slow=1.0):
    return 0.2 + slow * F / 950.0


def t_mm(N, n=1):
    return n * (0.035 + N / 2300.0)


def build_nc():
    nc = bass.Bass("TRN2", target_bir_lowering=False)
    di = lambda n, s: nc.dram_tensor(n, s, F32, kind="ExternalInput").ap()
    do = lambda n, s: nc.dram_tensor(n, s, F32, kind="ExternalOutput").ap()
    xs = di("xs", [NT, 128, D])
    ck = di("ck", [2, 128, 128])
    cv = di("cv", [2, 128, 128])
    sr = di("sr", [2, 4, 128, 128])
    norm1 = di("norm1", [D])
    w_in = di("w_in", [D, INW])
    sinks = di("sinks", [8])
    w_out = di("w_out", [D, D])
    norm2 = di("norm2", [D])
    w_up = di("w_up", [D, DFF])
    w_down = di("w_down", [DFF, D])
    norm_f = di("norm_f", [D])
    cosa_d = di("cosa", [128, NT, 32])
    sina_d = di("sina", [128, NT, 32])
    cosr_d = di("cosr", [128, NT, 64])
    sinr_d = di("sinr", [128, NT, 64])
    dt_d = di("dtab", [128, 2, 4, 128])
    qdec_d = di("qdec", [128, 2, 4, 128])
    kdec_d = di("kdec", [128, 2, 4])
    ident_d = di("ident", [128, 128])
    y_o = do("y", [NT, 128, D])
    nk_o = do("nk", [3, 128, 128])
    nv_o = do("nv", [3, 128, 128])
    ns_o = do("ns", [3, 4, 128, 128])

    with ExitStack() as st:
        P = Prog(nc, st)
        sbt = lambda name, shape, dt: st.enter_context(nc.sbuf_tensor(name, shape, dt))
        pairs = [st.enter_context(nc.psum_tensor(f"pp{i}", [128, 1024], F32)) for i in range(4)]
        _pb = [Buf(f"pair{i}", excl=True) for i in range(4)]
        bankbuf = [_pb[i // 2] for i in range(8)]

        def bank(b):
            return pairs[b // 2][:, (b % 2) * 512:(b % 2) * 512 + 512]

        def bank_bf(b):
            return bank(b).bitcast(BF16)

        PERS = 44 * 1024
        A1 = 151296
        pers_t = sbt("pers", [128, PERS // 4], F32)
        a1_t = sbt("a1", [128, A1 // 4], F32)
        wout_t = sbt("wout", [128, 8, D], BF16)
        pers = Arena(pers_t, PERS)
        mixflat = pers.alloc([NT * 1024], BF16)
        xbuf = [pers.alloc([D], F32) for _ in range(2)]
        ident = pers.alloc([128], BF16)
        ones = pers.alloc([64], BF16)
        stat = pers.alloc([64], F32)
        es = pers.alloc([8], F32)
        es_sel = pers.alloc([4], F32)
        sk = pers.alloc([8], F32)

        def mix_k(m):
            g, t = m // 4, m % 4
            ks = 512 if g < 4 else 128
            v = mixflat[:, g * 4096:g * 4096 + 8 * ks].rearrange("p (k x) -> p k x", k=8)
            return v[:, :, t * 128:(t + 1) * 128]

        def mix_grp(g, k):
            ks = 512 if g < 4 else 128
            return mixflat[:, g * 4096 + k * ks:g * 4096 + (k + 1) * ks]

        a1 = Arena(a1_t, A1)
        w_in_sb = a1.alloc([8, INW], BF16)
        g1b = a1.alloc([D], F32)
        cosa = a1.alloc([NT, 32], F32)
        sina = a1.alloc([NT, 32], F32)
        cosr = a1.alloc([NT, 64], F32)
        sinr = a1.alloc([NT, 64], F32)
        dtab = a1.alloc([2, 4, 128], F32)
        qdec = a1.alloc([2, 4, 128], F32)
        kdec = a1.alloc([2, 4], F32)
        dbl = lambda shape, dt: [a1.alloc(shape, dt) for _ in range(2)]
        a_bf = dbl([D], BF16)
        aT = dbl([8, 128], BF16)
        qrot = dbl([4, 2, 64], BF16)
        krot = dbl([128], BF16)
        QT = dbl([2, 4, 64], BF16)
        qkr = dbl([8, 128], BF16)
        QKrT = dbl([8, 128], BF16)
        QdT = dbl([4, 128], BF16)
        Kd = dbl([4, 128], BF16)
        Vr = dbl([4, 128], BF16)
        sg = dbl([4, 128], F32)
        scm = dbl([4, 128], BF16)
        rmix = dbl([4, 128], BF16)
        PTpc = dbl([1024], BF16)
        PTp = [a[:, 0:512] for a in PTpc]
        PTc = [a[:, 512:1024] for a in PTpc]
        rec = a1.alloc([2, 4, 64], F32)
        sgs = a1.alloc([512], F32)
        tcA = a1.alloc([640], F32)
        tsA = a1.alloc([640], F32)
        tcR = a1.alloc([D], F32)
        tsR = a1.alloc([D], F32)
        junk = a1.alloc([D], BF16)
        krot_f = [a1.alloc([128], F32) for _ in range(2)]
        vout_f = [a1.alloc([128], F32) for _ in range(2)]
        KT = [a1.alloc([128], BF16) for _ in range(3)]
        Vt = [a1.alloc([128], BF16) for _ in range(3)]
        KTc = [a1.alloc([128], BF16) for _ in range(2)]
        Vc = [a1.alloc([128], BF16) for _ in range(2)]
        cstage = [a1.alloc([256], F32) for _ in range(2)]
        cstage_bf = [a1.alloc([128], BF16) for _ in range(2)]
        S = [a1.alloc([4, 128], F32) for _ in range(3)]
        Sbf = [a1.alloc([4, 128], BF16) for _ in range(3)]
        a2 = Arena(a1_t, A1)
        NE = 11
        yacc_a = a2.alloc([NE, D], F32)
        g2b = a2.alloc([D], F32)
        gfb = a2.alloc([D], F32)
        a2_bf = [a2.alloc([D], BF16) for _ in range(2)]
        junk2 = a2.alloc([D], BF16)
        assert a2.off <= 62208, a2.off
        a2.off = 62208
        yacc_b = a2.alloc([NT - NE, D], F32)
        uT = [a2.alloc([4, 512], BF16) for _ in range(2)]
        urel = [a2.alloc([512], F32) for _ in range(2)]
        ystage = [a2.alloc([D], F32) for _ in range(2)]
        ring_up = [a2.alloc([8, 512], BF16) for _ in range(2)]
        ring_dn = [a2.alloc([4, D], BF16) for _ in range(2)]

        def yacc_t(m):
            return yacc_a[:, m, :] if m < NE else yacc_b[:, m - NE, :]

        B = {}

        def bf(name):
            if name not in B:
                B[name] = Buf(name)
            return B[name]

        def bf2(name):
            return [bf(name + "0"), bf(name + "1")]

        def cload(name, dst, src, eng="sp", nbytes=4096):
            b_ = bf(name)
            P.dma(eng, dst, src, P.dsem("d_" + name), writes=[b_], nbytes=nbytes)
            return b_

        fl2 = lambda a: a.rearrange("p a b -> p (a b)")
        fl3 = lambda a: a.rearrange("p a b c -> p (a b c)")
        g1bb = cload("g1b", g1b, norm1.partition_broadcast(128))
        idb = cload("ident", ident, ident_d, eng="pool", nbytes=512)
        skb = cload("sk", sk, sinks.partition_broadcast(128), nbytes=65536)
        xsem = [P.dsem("dx0"), P.dsem("dx1")]
        xb = bf2("x")
        x0n = P.dma("sp", xbuf[0], xs[0], xsem[0], writes=[xb[0]], nbytes=4096)
        wbuf = {}
        wprev = None
        wfirst = [x0n, g1bb.w, idb.w]
        for gi_, (c0_, c1_) in enumerate(((0, 768), (768, 1792), (1792, INW))):
            b_ = bf(f"win{gi_}")
            wprev = P.dma("pool", w_in_sb[:, :, c0_:c1_], w_in[:, c0_:c1_].rearrange("(k p) n -> p k n", p=128), P.dsem(f"dwin{gi_}"), writes=[b_],
                          extra=[wprev] + (wfirst if gi_ == 0 else []), nbytes=(c1_ - c0_) * 32)
            wbuf[c0_] = b_
        cosab = cload("cosa", fl2(cosa), fl2(cosa_d), nbytes=2176)
        sinab = cload("sina", fl2(sina), fl2(sina_d), nbytes=2176)
        cosrb = cload("cosr", fl2(cosr), fl2(cosr_d), nbytes=4352)
        sinrb = cload("sinr", fl2(sinr), fl2(sinr_d), nbytes=4352)
        dtabb = cload("dtab", fl3(dtab), fl3(dt_d))
        qdecb = cload("qdec", fl3(qdec), fl3(qdec_d))
        kdecb = cload("kdec", fl2(kdec), fl2(kdec_d), nbytes=32)
        wob = bf("wout")
        P.dma("pool", wout_t[:], w_out.rearrange("(k p) n -> p k n", p=128), P.dsem("dwo"), writes=[wob], extra=[wprev], nbytes=16384)
        onb = bf("ones")
        P.op("dve", lambda e: e.memset(ones, 1.0), writes=[onb], dur=0.1)
        esb = bf("es")
        P.op("act", lambda e: e.activation(out=es, in_=sk, func=AF.Exp), reads=[skb], writes=[esb], dur=0.3)
        essb = bf("es_sel")
        P.op("dve", lambda e: e.tensor_copy(out=es_sel[0:64, :], in_=es[0:64, 0:4]), reads=[esb], writes=[essb], dur=0.1)
        P.op("dve", lambda e: e.tensor_copy(out=es_sel[64:128, :], in_=es[64:128, 4:8]), reads=[esb], writes=[essb], dur=0.1)

        stat_n = [0]

        def stat_col(n):
            c0 = stat_n[0]
            stat_n[0] += n
            assert stat_n[0] <= 60
            return stat[:, c0:c0 + n], bf(f"stat{c0}")

        st_n1 = [stat_col(1) for _ in range(2)]
        st_g = [stat_col(4) for _ in range(2)]
        st_n2 = [stat_col(1) for _ in range(2)]
        st_nf = [stat_col(1) for _ in range(2)]

        _jb = {}

        def jb_of(ap):
            k = id(ap)
            if k not in _jb:
                _jb[k] = Buf("junk")
            return _jb[k]

        def rms(src_ap, src_bufs, gain_ap, gain_bufs, out_ap, out_bufs, stc, junk_ap, F=1024):
            col, cb = stc
            fns = [lambda e: e.activation(out=junk_ap, in_=src_ap, func=AF.Square, accum_out=col),
                   lambda e: e.activation(out=col, in_=col, func=AF.Ln, scale=1.0 / D, bias=EPS),
                   lambda e: e.activation(out=col, in_=col, func=AF.Exp, scale=-0.5)]
            P.op("act", fns[0], reads=src_bufs, writes=[cb, jb_of(junk_ap)], dur=t_act(F))
            P.op("act", fns[1], writes=[cb], dur=0.25)
            P.op("act", fns[2], writes=[cb], dur=0.25)
            P.op("dve", lambda e: e.scalar_tensor_tensor(out=out_ap, in0=src_ap, scalar=col, in1=gain_ap, op0=ALU.mult, op1=ALU.mult),
                 reads=list(src_bufs) + [cb] + list(gain_bufs), writes=out_bufs, dur=t_dve(F))

        TRB = 2

        def transposes(srcs, src_bufs, dst_ap, dst_bufs, trb=TRB):
            n = len(srcs)
            tb = bank_bf(trb)
            fns = [(lambda e, i=i, s=s: e.transpose(out=tb[:, i * 128:(i + 1) * 128], in_=s, identity=ident)) for i, s in enumerate(srcs)]
            P.op("pe", fns, reads=list(src_bufs) + [idb], writes=[bankbuf[trb]], dur=t_mm(128, n))
            src = tb[:, 0:n * 128].rearrange("p (a b) -> p a b", a=n)
            P.op("act", lambda e: e.copy(out=dst_ap, in_=src), reads=[bankbuf[trb]], writes=dst_bufs, dur=t_act(n * 128))

        csb = bf2("cst")
        Sb = [bf("S0"), bf("S1"), bf("S2")]
        Sbfb = [bf("Sbf0"), bf("Sbf1"), bf("Sbf2")]
        d_misc = P.dsem("dmisc")
        KTcb, Vcb, csbf_b = bf2("KTc"), bf2("Vc"), bf2("csbf")
        csvb = bf2("cstv")
        def prep_sample():
            for b in range(2):
                P.dma("sp", cstage[b][:, 0:128], ck[b], P.dsem(f"dck{b}"), writes=[csb[b]], nbytes=512)
                P.dma("sp", cstage[b][:, 128:256], cv[b], P.dsem(f"dcv{b}"), writes=[csvb[b]], nbytes=512)
                P.dma("sp", S[1 + b], sr[b].rearrange("h d e -> d h e"), P.dsem(f"dsr{b}"), writes=[Sb[1 + b]], nbytes=2048)
                P.op("dve", lambda e, b=b: e.tensor_copy(out=cstage_bf[b], in_=cstage[b][:, 0:128]), reads=[csb[b]], writes=[csbf_b[b]], dur=0.3)
                P.op("dve", lambda e, b=b: e.tensor_copy(out=Vc[b], in_=cstage[b][:, 128:256]), reads=[csvb[b]], writes=[Vcb[b]], dur=0.3)
                transposes([cstage_bf[b]], [csbf_b[b]], KTc[b].unsqueeze(1), [KTcb[b]])
                P.op("act", lambda e, b=b: e.copy(out=Sbf[1 + b], in_=S[1 + b]), reads=[Sb[1 + b]], writes=[Sbfb[1 + b]], dur=t_act(512))
                P.dma("sp", nk_o[1 + b, 0:64, :], ck[b, 64:128, :], d_misc, nbytes=512)
                P.dma("sp", nv_o[1 + b, 0:64, :], cv[b, 64:128, :], d_misc, nbytes=512)

        abf_b, aT_b = bf2("a_bf"), bf2("aT")
        tcAb, tsAb, tcRb, tsRb = bf("tcA"), bf("tsA"), bf("tcR"), bf("tsR")
        qrb, krb, QTb = bf2("qrot"), bf2("krot"), bf2("QT")
        KTb, Vtb = [bf(f"KT{i}") for i in range(3)], [bf(f"V{i}") for i in range(3)]
        PTcb, PTpb = bf2("PTc"), bf2("PTp")
        recb, sgsb = bf("rec"), bf("sgs")
        qkrb, QKrTb, QdTb, Kdb, Vrb, sgb, scmb, rmixb = (bf2(n) for n in ("qkr", "QKrT", "QdT", "Kd", "Vr", "sg", "scm", "rmix"))
        mixb = [bf(f"mix{m}") for m in range(NT)]
        krfb, vofb = bf2("krf"), bf2("vof")

        def front(m):
            sample = (m == NT - 1)
            ty = 1 if sample else 0
            xi = m % 2
            pi = m % 2
            if m > 0:
                P.dma("sp", xbuf[xi], xs[m], xsem[xi], writes=[xb[xi]], nbytes=4096)
            rms(xbuf[xi], [xb[xi]], g1b, [g1bb], a_bf[pi], [abf_b[pi]], st_n1[pi], junk)
            transposes([a_bf[pi][:, k * 128:(k + 1) * 128] for k in range(8)], [abf_b[pi]], aT[pi], [aT_b[pi]], trb=CFG['trb'])

            def proj(col0, ncols, pr, pi=pi):
                fns = []
                nb = (ncols + 511) // 512
                tot = 0.0
                for k in range(8):
                    for n in range(nb):
                        c0 = col0 + n * 512
                        w_ = min(512, col0 + ncols - c0)
                        tot += t_mm(w_)
                        fns.append(lambda e, k=k, n=n, c0=c0, w_=w_: e.matmul(
                            bank(2 * pr + n)[:, 0:w_], lhsT=aT[pi][:, k, :], rhs=w_in_sb[:, k, c0:c0 + w_],
                            start=(k == 0), stop=(k == 7)))
                P.op("pe", fns, reads=[aT_b[pi], wbuf[col0]], writes=[bankbuf[2 * pr + n] for n in range(nb)], dur=tot)

            PG0, PG1, PG2 = CFG['pG0'], CFG['pG1'], CFG['pG2']
            proj(0, 768, PG0)
            z0 = pairs[PG0]
            zall = z0[:, 0:640].rearrange("p (h two r) -> p h two r", h=10, two=2)
            tca = tcA.rearrange("p (h two r) -> p h two r", h=10, two=2)
            tsa = tsA.rearrange("p (h two r) -> p h two r", h=10, two=2)
            csb_ = cosa[:, m, :].unsqueeze(1).unsqueeze(1).broadcast_to([128, 10, 2, 32])
            snb_ = sina[:, m, :].unsqueeze(1).unsqueeze(1).broadcast_to([128, 10, 2, 32])
            P.op("dve", lambda e, csb_=csb_: e.tensor_tensor(out=tca, in0=zall, in1=csb_, op=ALU.mult),
                 reads=[bankbuf[2 * PG0], cosab], writes=[tcAb], dur=t_dve(640))
            P.op("dve", lambda e, snb_=snb_: e.tensor_tensor(out=tsa, in0=zall, in1=snb_, op=ALU.mult),
                 reads=[bankbuf[2 * PG0], sinab], writes=[tsAb], dur=t_dve(640))
            cur = m % 3
            prv = (m + 2) % 3
            P.op("act", lambda e, cur=cur: e.copy(out=Vt[cur], in_=z0[:, 640:768]), reads=[bankbuf[2 * PG0]], writes=[Vtb[cur]], dur=t_act(128))
            if m >= NT - 2:
                j = m - (NT - 2)
                P.op("act", lambda e, j=j: e.copy(out=vout_f[j], in_=z0[:, 640:768]), reads=[bankbuf[2 * PG0]], writes=[vofb[j]], dur=t_act(128))
            tcq = tcA[:, 0:512].rearrange("p (g t two r) -> p g t two r", g=2, t=4, two=2)
            tsq = tsA[:, 0:512].rearrange("p (g t two r) -> p g t two r", g=2, t=4, two=2)
            qro = qrot[pi].rearrange("p t g (two r) -> p g t two r", two=2)
            tck = tcA[:, 512:640].rearrange("p (h two r) -> p h two r", h=2, two=2)
            tsk = tsA[:, 512:640].rearrange("p (h two r) -> p h two r", h=2, two=2)
            kro = krot[pi].rearrange("p (h two r) -> p h two r", h=2, two=2)
            P.op("dve", lambda e, qro=qro: e.tensor_tensor(out=qro[:, :, :, 0, :], in0=tcq[:, :, :, 0, :], in1=tsq[:, :, :, 1, :], op=ALU.subtract),
                 reads=[tcAb, tsAb], writes=[qrb[pi]], dur=t_dve(256))
            P.op("dve", lambda e, qro=qro: e.tensor_tensor(out=qro[:, :, :, 1, :], in0=tcq[:, :, :, 1, :], in1=tsq[:, :, :, 0, :], op=ALU.add),
                 reads=[tcAb, tsAb], writes=[qrb[pi]], dur=t_dve(256))
            P.op("dve", lambda e, kro=kro: e.tensor_tensor(out=kro[:, :, 0, :], in0=tck[:, :, 0, :], in1=tsk[:, :, 1, :], op=ALU.subtract),
                 reads=[tcAb, tsAb], writes=[krb[pi]], dur=t_dve(64))
            P.op("dve", lambda e, kro=kro: e.tensor_tensor(out=kro[:, :, 1, :], in0=tck[:, :, 1, :], in1=tsk[:, :, 0, :], op=ALU.add),
                 reads=[tcAb, tsAb], writes=[krb[pi]], dur=t_dve(64))
            if m >= NT - 2:
                j = m - (NT - 2)
                krf = krot_f[j].rearrange("p (h two r) -> p h two r", h=2, two=2)
                P.op("dve", lambda e, krf=krf: e.tensor_tensor(out=krf[:, :, 0, :], in0=tck[:, :, 0, :], in1=tsk[:, :, 1, :], op=ALU.subtract),
                     reads=[tcAb, tsAb], writes=[krfb[j]], dur=t_dve(64))
                P.op("dve", lambda e, krf=krf: e.tensor_tensor(out=krf[:, :, 1, :], in0=tck[:, :, 1, :], in1=tsk[:, :, 0, :], op=ALU.add),
                     reads=[tcAb, tsAb], writes=[krfb[j]], dur=t_dve(64))
                if not sample:
                    P.dma("sp", nk_o[0], krot_f[j], d_misc, reads=[krfb[j]], nbytes=512)
                    P.dma("sp", nv_o[0], vout_f[j], d_misc, reads=[vofb[j]], nbytes=512)
                else:
                    for b in range(2):
                        P.dma("sp", nk_o[1 + b, 64:128, :], krot_f[j][64 * b:64 * b + 64, :], d_misc, reads=[krfb[j]], nbytes=512)
                        P.dma("sp", nv_o[1 + b, 64:128, :], vout_f[j][64 * b:64 * b + 64, :], d_misc, reads=[vofb[j]], nbytes=512)
            TRQ = CFG['trbQ']
            tb = bank_bf(TRQ)
            fns = [(lambda e, t=t, pi=pi: e.transpose(out=tb[:, t * 128:(t + 1) * 128], in_=qrot[pi][:, t, :, :].rearrange("p g d -> p (g d)"), identity=ident)) for t in range(4)]
            fns.append(lambda e, pi=pi: e.transpose(out=tb[:, 512:640], in_=krot[pi], identity=ident))
            P.op("pe", fns, reads=[qrb[pi], krb[pi], idb], writes=[bankbuf[TRQ]], dur=t_mm(128, 5))
            fns = [lambda e, pi=pi: e.copy(out=QT[pi], in_=tb[:, 0:512].rearrange("p (t c q) -> p c t q", t=4, c=2)),
                   lambda e, cur=cur: e.copy(out=KT[cur], in_=tb[:, 512:640])]
            P.op("act", fns, reads=[bankbuf[TRQ]], writes=[QTb[pi], KTb[cur]], dur=t_act(512) + t_act(128))

            proj(768, 1024, PG1)
            z1 = pairs[PG1]
            zr = z1[:, :].rearrange("p (h two r) -> p h two r", h=8, two=2)
            tcr = tcR.rearrange("p (h two r) -> p h two r", h=8, two=2)
            tsr = tsR.rearrange("p (h two r) -> p h two r", h=8, two=2)
            qko = qkr[pi].rearrange("p h (two r) -> p h two r", two=2)
            csr_ = cosr[:, m, :].unsqueeze(1).unsqueeze(1).broadcast_to([128, 8, 2, 64])
            snr_ = sinr[:, m, :].unsqueeze(1).unsqueeze(1).broadcast_to([128, 8, 2, 64])
            P.op("dve", lambda e, csr_=csr_: e.tensor_tensor(out=tcr, in0=zr, in1=csr_, op=ALU.mult), reads=[bankbuf[2 * PG1], cosrb], writes=[tcRb], dur=t_dve(1024))
            P.op("dve", lambda e, snr_=snr_: e.tensor_tensor(out=tsr, in0=zr, in1=snr_, op=ALU.mult), reads=[bankbuf[2 * PG1], sinrb], writes=[tsRb], dur=t_dve(1024))
            P.op("dve", lambda e, qko=qko: e.tensor_tensor(out=qko[:, :, 0, :], in0=tcr[:, :, 0, :], in1=tsr[:, :, 1, :], op=ALU.subtract), reads=[tcRb, tsRb], writes=[qkrb[pi]], dur=t_dve(512))
            P.op("dve", lambda e, qko=qko: e.tensor_tensor(out=qko[:, :, 1, :], in0=tcr[:, :, 1, :], in1=tsr[:, :, 0, :], op=ALU.add), reads=[tcRb, tsRb], writes=[qkrb[pi]], dur=t_dve(512))
            transposes([qkr[pi][:, h, :] for h in range(8)], [qkrb[pi]], QKrT[pi], [QKrTb[pi]], trb=CFG['trbR'])
            P.op("dve", lambda e, ty=ty, pi=pi: e.tensor_tensor(out=QdT[pi], in0=QKrT[pi][:, 0:4, :], in1=qdec[:, ty, :, :], op=ALU.mult),
                 reads=[QKrTb[pi], qdecb], writes=[QdTb[pi]], dur=t_dve(512))
            P.op("dve", lambda e, ty=ty, pi=pi: e.tensor_tensor(out=Kd[pi], in0=qkr[pi][:, 4:8, :], in1=kdec[:, ty, :].unsqueeze(2).broadcast_to([128, 4, 128]), op=ALU.mult),
                 reads=[qkrb[pi], kdecb], writes=[Kdb[pi]], dur=t_dve(512))
            proj(1792, 1024, PG2)
            P.op("act", lambda e, pi=pi: e.copy(out=Vr[pi].rearrange("p a b -> p (a b)"), in_=pairs[PG2][:, 0:512]), reads=[bankbuf[2 * PG2]], writes=[Vrb[pi]], dur=t_act(512))
            gps = pairs[PG2][:, 512:1024]
            P.op("act", lambda e: e.activation(out=sgs, in_=gps, func=AF.Exp, scale=-1.0), reads=[bankbuf[2 * PG2]], writes=[sgsb], dur=t_act(512))
            P.op("act", lambda e: e.activation(out=sgs, in_=sgs, func=AF.Ln, bias=1.0), writes=[sgsb], dur=t_act(512))
            P.op("act", lambda e: e.activation(out=sgs, in_=sgs, func=AF.Exp, scale=-1.0), writes=[sgsb], dur=t_act(512))
            P.op("dve", lambda e, pi=pi: e.tensor_tensor(out=sg[pi].rearrange("p a b -> p (a b)"), in0=gps, in1=sgs, op=ALU.mult),
                 reads=[bankbuf[2 * PG2], sgsb], writes=[sgb[pi]], dur=t_dve(512))

        def back(m):
            sample = (m == NT - 1)
            ty = 1 if sample else 0
            pi = m % 2
            cur = m % 3
            prv = (m + 2) % 3
            pS, pO = CFG['backpair']
            SBc, SBp, OB_, DB_ = 2 * pS + 1, 2 * pS, 2 * pO, 2 * pO + 1
            has_prev = sample or m > 0
            QTf = QT[pi].rearrange("p c t q -> p (c t q)")
            for g in range(2):
                r0 = 64 * g
                fns = [lambda e, r0=r0, cur=cur, QTf=QTf: e.matmul(bank(SBc), lhsT=KT[cur][r0:r0 + 64, :], rhs=QTf[r0:r0 + 64, :], start=True, stop=True)]
                rd = [KTb[cur], QTb[pi]]
                if has_prev:
                    if not sample:
                        fns.append(lambda e, r0=r0, prv=prv, QTf=QTf: e.matmul(bank(SBp), lhsT=KT[prv][r0:r0 + 64, :], rhs=QTf[r0:r0 + 64, :], start=True, stop=True))
                        rd.append(KTb[prv])
                    else:
                        fns += [(lambda e, b=b, r0=r0, QTf=QTf: e.matmul(bank(SBp)[:, 256 * b:256 * b + 256], lhsT=KTc[b][r0:r0 + 64, :], rhs=QTf[r0:r0 + 64, 256 * b:256 * b + 256],
                                                                       start=True, stop=True)) for b in range(2)]
                        rd += KTcb
                P.op("pe", fns, reads=rd, writes=[bankbuf[SBc], bankbuf[SBp]], dur=t_mm(512, len(fns)))
                if has_prev:
                    P.op("act", lambda e, g=g: e.activation(out=PTpc[g], in_=pairs[pS][:, :], func=AF.Exp, scale=0.125),
                         reads=[bankbuf[SBc], bankbuf[SBp]], writes=[PTcb[g], PTpb[g]], dur=t_act(1024))
                else:
                    P.op("act", lambda e, g=g: e.activation(out=PTc[g], in_=bank(SBc), func=AF.Exp, scale=0.125),
                         reads=[bankbuf[SBc]], writes=[PTcb[g]], dur=t_act(512))
                fns = []
                for c in range(2):
                    contrib = []
                    if has_prev:
                        if sample:
                            contrib.append((Vc[c], PTp[g], 0, 128))
                        else:
                            contrib.append((Vt[prv], PTp[g], 0, 128) if c == 0 else (Vt[prv], PTp[g], 64, 128))
                    if sample:
                        contrib.append((Vt[cur], PTc[g], 64 * c, 64 * c + 64))
                    else:
                        contrib.append((Vt[cur], PTc[g], 0, 64) if c == 0 else (Vt[cur], PTc[g], 0, 128))
                    for (dbk_, lsel) in ((OB_, 0), (DB_, 1)):
                        for i, (vv, pt, k0, k1) in enumerate(contrib):
                            lhs = vv[k0:k1, r0:r0 + 64] if lsel == 0 else ones[k0:k1, :]
                            fns.append(lambda e, dbk_=dbk_, lhs=lhs, pt=pt, k0=k0, k1=k1, c=c, i=i, nctr=len(contrib), r0=r0: e.matmul(
                                bank(dbk_)[r0:r0 + 64, 256 * c:256 * c + 256], lhsT=lhs, rhs=pt[k0:k1, 256 * c:256 * c + 256],
                                start=(i == 0), stop=(i == nctr - 1)))
                rd = [Vtb[cur], PTcb[g], onb] + ([PTpb[g]] + (Vcb if sample else [Vtb[prv]]) if has_prev else [])
                P.op("pe", fns, reads=rd, writes=[bankbuf[OB_], bankbuf[DB_]], dur=t_mm(256, len(fns)))
            dbv = bank(DB_).rearrange("p (c t q) -> p c t q", c=2, t=4)
            obv = bank(OB_).rearrange("p (c t q) -> p c t q", c=2, t=4)
            fns = [(lambda e, t=t: e.activation(out=rec[:, :, t, :], in_=dbv[:, :, t, :], func=AF.Ln, bias=es_sel[:, t:t + 1])) for t in range(4)]
            P.op("act", fns, reads=[bankbuf[DB_], essb], writes=[recb], dur=4 * t_act(128))
            P.op("act", lambda e: e.activation(out=rec.rearrange("p c t q -> p (c t q)"), in_=rec.rearrange("p c t q -> p (c t q)"), func=AF.Exp, scale=-1.0),
                 writes=[recb], dur=t_act(512))
            mo = mix_k(m)[:, 0:4, :].rearrange("p t (c q) -> p c t q", c=2)
            P.op("dve", lambda e, mo=mo: e.tensor_tensor(out=mo, in0=obv, in1=rec, op=ALU.mult),
                 reads=[bankbuf[OB_], recb], writes=[mixb[m]], dur=t_dve(512))

            SC_, OR_, UB_, RT_ = CFG.get('ret', (5, 6, 7, 4))
            scv = bank(SC_).rearrange("p (a b) -> p a b", a=4)
            fns = [(lambda e, h=h, pi=pi: e.matmul(scv[:, h, :], lhsT=QKrT[pi][:, 4 + h, :], rhs=QKrT[pi][:, h, :], start=True, stop=True)) for h in range(4)]
            P.op("pe", fns, reads=[QKrTb[pi]], writes=[bankbuf[SC_]], dur=t_mm(128, 4))
            P.op("dve", lambda e, ty=ty, pi=pi: e.tensor_tensor(out=scm[pi], in0=scv, in1=dtab[:, ty, :, :], op=ALU.mult),
                 reads=[bankbuf[SC_], dtabb], writes=[scmb[pi]], dur=t_dve(512))
            orv = bank(OR_).rearrange("p (a b) -> p a b", a=4)
            fns = []
            rd = [scmb[pi], Vrb[pi]]
            if not sample:
                cross = (m > 0)
                for h in range(4):
                    fns.append(lambda e, h=h, cross=cross, pi=pi: e.matmul(orv[:, h, :], lhsT=scm[pi][:, h, :], rhs=Vr[pi][:, h, :], start=True, stop=not cross))
                    if cross:
                        fns.append(lambda e, h=h, pi=pi: e.matmul(orv[:, h, :], lhsT=QdT[pi][:, h, :], rhs=Sbf[0][:, h, :], start=False, stop=True))
                if cross:
                    rd += [QdTb[pi], Sbfb[0]]
            else:
                for h in range(4):
                    fns.append(lambda e, h=h, pi=pi: e.matmul(orv[:, h, :], lhsT=scm[pi][:, h, :], rhs=Vr[pi][:, h, :], start=True, stop=True))
                    for b in range(2):
                        fns.append(lambda e, h=h, b=b, pi=pi: e.matmul(orv[64 * b:64 * b + 64, h, :], lhsT=QdT[pi][:, h, 64 * b:64 * b + 64], rhs=Sbf[1 + b][:, h, :],
                                                                      start=False, stop=False, skip_group_check=True))
                rd += [QdTb[pi], Sbfb[1], Sbfb[2]]
            P.op("pe", fns, reads=rd, writes=[bankbuf[OR_]], dur=t_mm(128, len(fns)))
            gcol, gcb = st_g[pi]
            fns = [(lambda e, h=h, gcol=gcol: e.activation(out=junk[:, h * 128:(h + 1) * 128], in_=orv[:, h, :], func=AF.Square, accum_out=gcol[:, h:h + 1])) for h in range(4)]
            P.op("act", fns, reads=[bankbuf[OR_]], writes=[gcb, jb_of(junk)], dur=4 * t_act(128))
            P.op("act", lambda e, gcol=gcol: e.activation(out=gcol, in_=gcol, func=AF.Ln, scale=1.0 / 128, bias=EPS), writes=[gcb], dur=0.25)
            P.op("act", lambda e, gcol=gcol: e.activation(out=gcol, in_=gcol, func=AF.Exp, scale=-0.5), writes=[gcb], dur=0.25)
            fns = [(lambda e, h=h, gcol=gcol, pi=pi: e.scalar_tensor_tensor(out=rmix[pi][:, h, :], in0=orv[:, h, :], scalar=gcol[:, h:h + 1], in1=sg[pi][:, h, :],
                                                                           op0=ALU.mult, op1=ALU.mult)) for h in range(4)]
            P.op("dve", fns, reads=[bankbuf[OR_], gcb, sgb[pi]], writes=[rmixb[pi]], dur=4 * t_dve(128))
            transposes([rmix[pi][:, h, :] for h in range(4)], [rmixb[pi]], mix_k(m)[:, 4:8, :], [mixb[m]], trb=RT_)
            if not sample:
                ubv = bank(UB_).rearrange("p (a b) -> p a b", a=4)
                fns = [(lambda e, h=h, pi=pi: e.matmul(ubv[:, h, :], lhsT=Kd[pi][:, h, :], rhs=Vr[pi][:, h, :], start=True, stop=True)) for h in range(4)]
                P.op("pe", fns, reads=[Kdb[pi], Vrb[pi]], writes=[bankbuf[UB_]], dur=t_mm(128, 4))
                if m == 0:
                    P.op("dve", lambda e: e.tensor_copy(out=S[0], in_=ubv), reads=[bankbuf[UB_]], writes=[Sb[0]], dur=t_dve(512))
                else:
                    fns = [(lambda e, h=h: e.scalar_tensor_tensor(out=S[0][:, h, :], in0=S[0][:, h, :], scalar=float(GAM[h] ** 128), in1=ubv[:, h, :],
                                                                 op0=ALU.mult, op1=ALU.add)) for h in range(4)]
                    P.op("dve", fns, reads=[bankbuf[UB_]], writes=[Sb[0]], dur=4 * t_dve(128))
                if m < NT - 2:
                    P.op("act", lambda e: e.copy(out=Sbf[0], in_=S[0]), reads=[Sb[0]], writes=[Sbfb[0]], dur=t_act(512))
                else:
                    P.dma("sp", ns_o[0].rearrange("h d e -> d h e"), S[0], d_misc, reads=[Sb[0]], nbytes=2048)
            else:
                for b in range(2):
                    ubk = UB_ if b == 0 else SC_
                    ubv = bank(ubk).rearrange("p (a b) -> p a b", a=4)
                    fns = [(lambda e, h=h, b=b, ubv=ubv, pi=pi: e.matmul(ubv[:, h, :], lhsT=Kd[pi][64 * b:64 * b + 64, h, :], rhs=Vr[pi][64 * b:64 * b + 64, h, :], start=True, stop=True)) for h in range(4)]
                    P.op("pe", fns, reads=[Kdb[pi], Vrb[pi]], writes=[bankbuf[ubk]], dur=t_mm(128, 4))
                    fns = [(lambda e, h=h, b=b, ubv=ubv: e.scalar_tensor_tensor(out=S[1 + b][:, h, :], in0=S[1 + b][:, h, :], scalar=float(GAM[h] ** 64), in1=ubv[:, h, :],
                                                                               op0=ALU.mult, op1=ALU.add)) for h in range(4)]
                    P.op("dve", fns, reads=[bankbuf[ubk], Sbfb[1 + b]], writes=[Sb[1 + b]], dur=4 * t_dve(128))
                    P.dma("sp", ns_o[1 + b].rearrange("h d e -> d h e"), S[1 + b], d_misc, reads=[Sb[1 + b]], nbytes=2048)

        yb = [bf(f"yacc{m}") for m in range(NT)]
        a2b = bf2("a2_bf")
        g2h = {}

        def phase2a(m, early):
            xi = m % 2
            pi = m % 2
            P.dma("sp", xbuf[xi], xs[m], xsem[xi], writes=[xb[xi]], nbytes=4096, nobar=True)
            pr = 0 if early else m % 2
            trb = 2 if early else 4 + 2 * (m % 2)
            mk = mix_k(m)
            fns = []
            for k in range(8):
                for n in range(2):
                    fns.append(lambda e, k=k, n=n, mk=mk, pr=pr: e.matmul(bank(2 * pr + n), lhsT=mk[:, k, :], rhs=wout_t[:, k, n * 512:(n + 1) * 512],
                                                                         start=(k == 0), stop=(k == 7)))
            P.op("pe", fns, reads=[mixb[m], wob], writes=[bankbuf[2 * pr], bankbuf[2 * pr + 1]], dur=t_mm(512, 16), nobar=True)
            P.op("dve", lambda e, m=m, pr=pr, xi=xi: e.tensor_tensor(out=yacc_t(m), in0=pairs[pr][:, :], in1=xbuf[xi], op=ALU.add),
                 reads=[bankbuf[2 * pr], bankbuf[2 * pr + 1], xb[xi]], writes=[yb[m]], dur=t_dve(1024))
            rms(yacc_t(m), [yb[m]], g2b, [g2h["g2b"]], a2_bf[pi], [a2b[pi]], st_n2[pi], junk2)
            transposes([a2_bf[pi][:, k * 128:(k + 1) * 128] for k in range(8)], [a2b[pi]], mk, [mixb[m]], trb=trb)

        front(0)
        for m in range(NT):
            if m + 1 == NT - 2:
                prep_sample()
            if m + 1 < NT:
                front(m + 1)
            if m + 1 == NT - 1:
                fb = P.fence(lambda e: e.memset(stat[:, 61:62], 0.0))
                P.gate = fb
                g2h["g2b"] = cload("g2b", g2b, norm2.partition_broadcast(128))
                g2h["gfb"] = cload("gfb", gfb, norm_f.partition_broadcast(128))
                for mm in range(NE):
                    phase2a(mm, True)
                P.gate = None
            back(m)

        P.barrier(lambda e: e.memset(stat[:, 60:61], 0.0))
        gfbb = g2h["gfb"]
        ringub, ringdb = bf2("ringu"), bf2("ringd")
        rsu = [P.dsem("dru0"), P.dsem("dru1")]
        rsd = [P.dsem("drd0"), P.dsem("drd1")]

        def load_chunk(c):
            s = c % 2
            P.dma("pool", ring_up[s], w_up[:, c * 512:(c + 1) * 512].rearrange("(k p) f -> p k f", p=128), rsu[s], writes=[ringub[s]], nbytes=16384)
            P.dma("pool", ring_dn[s], w_down[c * 512:(c + 1) * 512, :].rearrange("(j p) n -> p j n", p=128), rsd[s], writes=[ringdb[s]], nbytes=16384)

        load_chunk(0)
        load_chunk(1)
        for m in range(NE, NT):
            phase2a(m, False)

        groups = [(0, 4), (4, 4), (8, 4), (12, 4), (16, 1)]
        uTb, urb = bf2("uT"), bf2("ur")
        ub_rot = up_rot = dn_rot = 0
        for c in range(NCH):
            s = c % 2
            for gi, (m0, nt_) in enumerate(groups):
                ntok = nt_ * 128
                ui = ub_rot % 2
                ub_rot += 1
                for j in range(4):
                    bk = 4 + (up_rot % 4)
                    up_rot += 1
                    fns = [(lambda e, k=k, j=j, bk=bk, s=s, ntok=ntok, gi=gi: e.matmul(bank(bk)[:, 0:ntok], lhsT=ring_up[s][:, k, j * 128:(j + 1) * 128],
                                                                                  rhs=mix_grp(gi, k), start=(k == 0), stop=(k == 7))) for k in range(8)]
                    P.op("pe", fns, reads=[mixb[m] for m in range(m0, m0 + nt_)] + [ringub[s]], writes=[bankbuf[bk]], dur=t_mm(ntok, 8))
                    ri = up_rot % 2
                    P.op("act", lambda e, bk=bk, ri=ri, ntok=ntok: e.activation(out=urel[ri][:, 0:ntok], in_=bank(bk)[:, 0:ntok], func=AF.Relu),
                         reads=[bankbuf[bk]], writes=[urb[ri]], dur=t_act(ntok))
                    P.op("act", lambda e, ri=ri, ui=ui, j=j, ntok=ntok: e.activation(out=uT[ui][:, j, 0:ntok], in_=urel[ri][:, 0:ntok], func=AF.Square),
                         reads=[urb[ri]], writes=[uTb[ui]], dur=t_act(ntok))
                for t in range(nt_):
                    m = m0 + t
                    pr = dn_rot % 2
                    dn_rot += 1
                    fns = []
                    for j in range(4):
                        for n in range(2):
                            fns.append(lambda e, j=j, n=n, t=t, pr=pr, ui=ui, s=s: e.matmul(bank(2 * pr + n), lhsT=uT[ui][:, j, t * 128:(t + 1) * 128],
                                                                                           rhs=ring_dn[s][:, j, n * 512:(n + 1) * 512], start=(j == 0), stop=(j == 3)))
                    P.op("pe", fns, reads=[uTb[ui], ringdb[s]], writes=[bankbuf[2 * pr], bankbuf[2 * pr + 1]], dur=t_mm(512, 8))
                    P.op("dve", lambda e, m=m, pr=pr: e.tensor_tensor(out=yacc_t(m), in0=pairs[pr][:, :], in1=yacc_t(m), op=ALU.add),
                         reads=[bankbuf[2 * pr], bankbuf[2 * pr + 1]], writes=[yb[m]], dur=t_dve(1024))
            if c + 2 < NCH:
                load_chunk(c + 2)

        ysem = [P.dsem("dy0"), P.dsem("dy1")]
        ysb = bf2("ys")
        for m in range(NT):
            yi = m % 2
            rms(yacc_t(m), [yb[m]], gfb, [gfbb], ystage[yi], [ysb[yi]], st_nf[yi], junk2)
            P.dma("sp", y_o[m], ystage[yi], ysem[yi], reads=[ysb[yi]], nbytes=4096)
        P.emit([ysem[0], ysem[1], d_misc])
    return nc


def _host_consts():
    pos = np.zeros((128, NT), np.float64)
    for m in range(16):
        pos[:, m] = 128 * m + np.arange(128)
    pos[:, 16] = 2048 + (np.arange(128) % 64)
    pos32 = pos.astype(np.float32)
    try:
        import jax
        import jax.numpy as jnp
        with jax.default_device(jax.devices("cpu")[0]):
            inv_a = np.asarray(1.0 / (10000.0 ** (jnp.arange(0, 64, 2, dtype=jnp.float32) / 64)), dtype=np.float32)
            inv_r = np.asarray(1.0 / (10000.0 ** jnp.linspace(0.0, 1.0, 64, dtype=jnp.float32)), dtype=np.float32)
    except Exception:
        inv_a = (1.0 / (np.float32(10000.0) ** (np.arange(0, 64, 2, dtype=np.float32) / np.float32(64)))).astype(np.float32)
        inv_r = (1.0 / (np.float32(10000.0) ** np.linspace(0.0, 1.0, 64, dtype=np.float32))).astype(np.float32)
    ang_a = (pos32[:, :, None] * inv_a[None, None, :]).astype(np.float32).astype(np.float64)
    ang_r = (pos32[:, :, None] * inv_r[None, None, :]).astype(np.float32).astype(np.float64)
    cosa, sina = np.cos(ang_a).astype(np.float32), np.sin(ang_a).astype(np.float32)
    cosr, sinr = np.cos(ang_r).astype(np.float32), np.sin(ang_r).astype(np.float32)
    logg = np.log1p(-np.exp2(-5.0 - np.arange(4, dtype=np.float64)))
    sc = 128.0 ** -0.5
    j = np.arange(128)[:, None]
    i = np.arange(128)[None, :]
    dtab = np.zeros((128, 2, 4, 128), np.float64)
    qdec = np.zeros((128, 2, 4, 128), np.float64)
    kdec = np.zeros((128, 2, 4), np.float64)
    for h in range(4):
        d0 = np.where(i >= j, np.exp(logg[h] * np.maximum(i - j, 0)), 0.0)
        dtab[:, 0, h, :] = sc * d0
        same = (i // 64) == (j // 64)
        dtab[:, 1, h, :] = sc * np.where(same, d0, 0.0)
        qdec[:, 0, h, :] = np.exp(logg[h] * (np.arange(128) + 1.0))[None, :]
        qdec[:, 1, h, :] = np.exp(logg[h] * ((np.arange(128) % 64) + 1.0))[None, :]
        kdec[:, 0, h] = sc * np.exp(logg[h] * (127.0 - np.arange(128)))
        kdec[:, 1, h] = sc * np.exp(logg[h] * (63.0 - (np.arange(128) % 64)))
    return dict(cosa=cosa, sina=sina, cosr=cosr, sinr=sinr, dtab=dtab.astype(np.float32), qdec=qdec.astype(np.float32),
                kdec=kdec.astype(np.float32), ident=np.eye(128, dtype=np.float32))


_NC_CACHE = {}


def kernel(x_prompt, x_sample, cache_k, cache_v, state_ret, norm1, w_in, sinks, w_out, norm2, w_up, w_down, norm_f):
    f = lambda a: np.ascontiguousarray(np.asarray(a, dtype=np.float32))
    x_prompt, x_sample, cache_k, cache_v, state_ret = map(f, (x_prompt, x_sample, cache_k, cache_v, state_ret))
    norm1, w_in, sinks, w_out, norm2, w_up, w_down, norm_f = map(f, (norm1, w_in, sinks, w_out, norm2, w_up, w_down, norm_f))
    consts = _host_consts()
    perm = []
    for t in range(4):
        perm += list(range(64 * t, 64 * t + 64)) + list(range(256 + 64 * t, 256 + 64 * t + 64))
    perm += list(range(512, 1024))
    w_out_p = np.ascontiguousarray(w_out[0][perm, :])
    if "nc" not in _NC_CACHE:
        _NC_CACHE["nc"] = build_nc()
    nc = _NC_CACHE["nc"]
    in_maps = []
    for c in range(8):
        xs = np.concatenate([x_prompt[c].reshape(16, 128, D), x_sample[2 * c:2 * c + 2].reshape(1, 128, D)], axis=0)
        m = dict(xs=np.ascontiguousarray(xs),
                 ck=np.ascontiguousarray(cache_k[0, 2 * c:2 * c + 2].reshape(2, 128, 128)),
                 cv=np.ascontiguousarray(cache_v[0, 2 * c:2 * c + 2].reshape(2, 128, 128)),
                 sr=np.ascontiguousarray(state_ret[0, 2 * c:2 * c + 2]),
                 norm1=norm1[0], w_in=w_in[0], sinks=sinks[0], w_out=w_out_p, norm2=norm2[0], w_up=w_up[0], w_down=w_down[0], norm_f=norm_f)
        m.update(consts)
        in_maps.append(m)
    res = run_bass_kernel_spmd(nc, in_maps, core_ids=list(range(8)))
    R = res.results
    y_prompt = np.stack([R[c]["y"][:16].reshape(2048, D) for c in range(8)], 0)
    y_sample = np.concatenate([R[c]["y"][16].reshape(2, 64, D) for c in range(8)], 0)
    nk_p = np.stack([R[c]["nk"][0].reshape(128, 2, 64) for c in range(8)], 0)[None]
    nv_p = np.stack([R[c]["nv"][0].reshape(128, 2, 64) for c in range(8)], 0)[None]
    ns_p = np.stack([R[c]["ns"][0] for c in range(8)], 0)[None]
    nk_s = np.concatenate([R[c]["nk"][1:3].reshape(2, 128, 2, 64) for c in range(8)], 0)[None]
    nv_s = np.concatenate([R[c]["nv"][1:3].reshape(2, 128, 2, 64) for c in range(8)], 0)[None]
    ns_s = np.concatenate([R[c]["ns"][1:3] for c in range(8)], 0)[None]
    return (y_prompt.astype(np.float32), y_sample.astype(np.float32), nk_p.astype(np.float32), nv_p.astype(np.float32),
            ns_p.astype(np.float32), nk_s.astype(np.float32), nv_s.astype(np.float32), ns_s.astype(np.float32))
```
